# Optimizing a Trainium2 kernel written in Bass

```python
import math
import jax
import jax.numpy as jnp
from jax import lax
import numpy as np

D_MODEL = 1024
BATCH = 8
SEQ = 4096
DEPTH = 2

GRID_W = 64
CTX_LEN = 256
N_EVEN = (DEPTH + 1) // 2
N_ODD = DEPTH // 2
EPS = 1e-6

NA_HEADS = 8
NA_DH = 64
NA_W = NA_HEADS * NA_DH
NA_WIN_R = 8
NA_WIN_C = 16

DN_HEADS = 4
DN_DK = 128
DN_DV = 128
DN_QK_W = DN_HEADS * DN_DK
DN_W = DN_HEADS * DN_DV
DN_CONV_W = 2 * DN_QK_W + DN_W
DN_CONV = 5
DN_CHUNK = 64

MIX0_W = NA_W + DN_W
OFF_DN = 3 * NA_W
OFF_AB = OFF_DN + DN_CONV_W
OFF_Z = OFF_AB + 4 * DN_HEADS
IN0_W = OFF_Z + MIX0_W

HY_W = D_MODEL
HY_ORDER = 2
HY_DIRS = 2
HY_CONV = 3
HY_EMB = 33
HY_BANDS = (HY_EMB - 1) // 2
HY_FFN = 64
HY_TARGET = 1e-2
HY_DECAY_MIN = math.log(HY_TARGET) / 1.5
HY_DECAY_MAX = math.log(HY_TARGET) / 0.3
IN1_W = (HY_ORDER + 1) * HY_W + HY_W

kernel_name = 'hybrid_na_gdn_hyena_prefix_dit'


def rmsnorm(s, g):
    s32 = s.astype(jnp.float32)
    return (s32 * lax.rsqrt(jnp.mean(s32 * s32, axis=-1, keepdims=True) + EPS)).astype(s.dtype) * g


def l2norm(t):
    return t * lax.rsqrt(jnp.sum(t * t, axis=-1, keepdims=True) + EPS)


def adaln_in(s, cvec, norm_g, mod_w, mod_b):
    m = jax.nn.silu(cvec) @ mod_w + mod_b
    shift, scale, gate = jnp.split(m, 3, axis=-1)
    return rmsnorm(s, norm_g) * (1.0 + scale) + shift, gate


def dwconv(u, w):
    k = w.shape[0]
    return lax.conv_general_dilated(u, w[:, None, :].astype(u.dtype), window_strides=(1,), padding=[(k // 2, k // 2)], dimension_numbers=('NWC', 'WIO', 'NWC'), feature_group_count=u.shape[-1])


def neighbourhood_attention(qx, kx, vx, kc, vc, rpb):
    b, l, h, dh = qx.shape
    rows = l // GRID_W
    wr = min(NA_WIN_R, rows)
    nwin = wr * NA_WIN_C
    qg = (qx * dh ** -0.5).reshape(b, rows, GRID_W, h, dh)
    kg = kx.reshape(b, rows, GRID_W, h, dh)
    vg = vx.reshape(b, rows, GRID_W, h, dh)
    cols = np.arange(GRID_W)
    c0 = np.clip(cols - NA_WIN_C // 2, 0, GRID_W - NA_WIN_C)
    col_idx = c0[:, None] + np.arange(NA_WIN_C)[None, :]
    dc_idx = col_idx - cols[:, None] + NA_WIN_C - 1

    def one_row(r):
        r0 = jnp.clip(r - wr // 2, 0, rows - wr)
        qr = lax.dynamic_index_in_dim(qg, r, axis=1, keepdims=False)
        kr = lax.dynamic_slice_in_dim(kg, r0, wr, axis=1)
        vr = lax.dynamic_slice_in_dim(vg, r0, wr, axis=1)
        kw = kr[:, :, col_idx]
        vw = vr[:, :, col_idx]
        s_win = jnp.einsum('bqhd,brqkhd->bhqrk', qr, kw)
        dr_idx = r0 + jnp.arange(wr) - r + NA_WIN_R - 1
        bias = rpb[:, dr_idx[:, None, None], dc_idx[None, :, :]]
        s_win = s_win + jnp.transpose(bias, (0, 2, 1, 3))[None]
        s_ctx = jnp.einsum('bqhd,bkhd->bhqk', qr, kc)
        s = jnp.concatenate([s_win.reshape(b, h, GRID_W, nwin), s_ctx], axis=-1).astype(jnp.float32)
        p = jax.nn.softmax(s, axis=-1).astype(vx.dtype)
        p_win = p[..., :nwin].reshape(b, h, GRID_W, wr, NA_WIN_C)
        return jnp.einsum('bhqrk,brqkhd->bqhd', p_win, vw) + jnp.einsum('bhqk,bkhd->bqhd', p[..., nwin:], vc)

    o = lax.map(one_row, jnp.arange(rows))
    return jnp.transpose(o, (1, 0, 2, 3, 4)).reshape(b, l, h * dh)


def context_attention(q, k, v):
    b, l, h, dh = q.shape
    s = jnp.einsum('bqhd,bkhd->bhqk', q * dh ** -0.5, k).astype(jnp.float32)
    p = jax.nn.softmax(s, axis=-1).astype(v.dtype)
    return jnp.einsum('bhqk,bkhd->bqhd', p, v).reshape(b, l, h * dh)


def gated_delta_chunked(q, k, v, g, beta, s0):
    b, l, h, dk = q.shape
    dv = v.shape[-1]
    n = l // DN_CHUNK

    def chunk(t):
        return jnp.moveaxis(t.reshape(b, n, DN_CHUNK, h, *t.shape[3:]), 3, 1)

    q, k, v, g, beta = chunk(q), chunk(k), chunk(v), chunk(g), chunk(beta)
    gc = jnp.cumsum(g, axis=-1)
    idx = jnp.arange(DN_CHUNK)
    incl = idx[:, None] >= idx[None, :]
    strict = idx[:, None] > idx[None, :]
    diff = gc[..., :, None] - gc[..., None, :]
    decay = jnp.where(incl, jnp.exp(jnp.where(incl, diff, 0.0)), 0.0)
    kb = k * beta[..., None]
    vb = v * beta[..., None]
    lmat = jnp.where(strict, jnp.einsum('bhncd,bhnsd->bhncs', kb, k) * decay, 0.0)
    rhs = jnp.concatenate([vb, kb * jnp.exp(gc)[..., None]], axis=-1)
    sol = lax.linalg.triangular_solve(lmat, rhs, left_side=True, lower=True, unit_diagonal=True)
    u, w = sol[..., :dv], sol[..., dv:]
    aqk = jnp.einsum('bhncd,bhnsd->bhncs', q, k) * decay

    def step(s, inp):
        qi, ki, ui, wi, gi, ai = inp
        v_new = ui - jnp.einsum('bhck,bhkv->bhcv', wi, s)
        o = jnp.einsum('bhck,bhkv->bhcv', qi * jnp.exp(gi)[..., None], s) + jnp.einsum('bhcs,bhsv->bhcv', ai, v_new)
        glast = gi[..., -1]
        s = s * jnp.exp(glast)[..., None, None] + jnp.einsum('bhck,bhcv->bhkv', ki * jnp.exp(glast[..., None] - gi)[..., None], v_new)
        return s, o

    xs = tuple(jnp.moveaxis(t, 2, 0) for t in (q, k, u, w, gc, aqk))
    s_fin, o = lax.scan(step, s0, xs)
    return jnp.transpose(o, (1, 0, 3, 2, 4)).reshape(b, l, h, dv), s_fin


def dn_prepare(qkv, ab, conv_w, a_log, dt_bias):
    b, l, _ = qkv.shape
    t = jax.nn.silu(dwconv(qkv, conv_w)).astype(jnp.float32)
    q = l2norm(t[..., :DN_QK_W].reshape(b, l, DN_HEADS, DN_DK)) * DN_DK ** -0.5
    k = l2norm(t[..., DN_QK_W:2 * DN_QK_W].reshape(b, l, DN_HEADS, DN_DK))
    v = t[..., 2 * DN_QK_W:].reshape(b, l, DN_HEADS, DN_DV)
    ab = ab.astype(jnp.float32).reshape(b, l, 2, 2, DN_HEADS)
    g = -jnp.exp(a_log.astype(jnp.float32)) * jax.nn.softplus(ab[:, :, 0] + dt_bias.astype(jnp.float32))
    beta = jax.nn.sigmoid(ab[:, :, 1])
    return q, k, v, g, beta


def orient(t, rev):
    return t[:, ::-1] if rev else t


def bidir_gated_deltanet(qkv_x, ab_x, qkv_c, ab_c, conv_w, a_log, dt_bias, norm_g):
    qx, kx, vx, gx, bx = dn_prepare(qkv_x, ab_x, conv_w, a_log, dt_bias)
    qc, kc, vc, gcx, bcx = dn_prepare(qkv_c, ab_c, conv_w, a_log, dt_bias)
    b = qkv_x.shape[0]
    s0 = jnp.zeros((b, DN_HEADS, DN_DK, DN_DV), jnp.float32)
    o_x = 0.0
    o_c = 0.0
    for d in range(2):
        rev = d == 1
        oc, s_ctx = gated_delta_chunked(orient(qc, rev), orient(kc, rev), orient(vc, rev), orient(gcx[:, :, d], rev), orient(bcx[:, :, d], rev), s0)
        ox, _ = gated_delta_chunked(orient(qx, rev), orient(kx, rev), orient(vx, rev), orient(gx[:, :, d], rev), orient(bx[:, :, d], rev), s_ctx)
        o_c = o_c + orient(oc, rev)
        o_x = o_x + orient(ox, rev)
    o_x = rmsnorm(o_x, norm_g).reshape(b, -1, DN_W).astype(qkv_x.dtype)
    o_c = rmsnorm(o_c, norm_g).reshape(b, -1, DN_W).astype(qkv_c.dtype)
    return o_x, o_c


def even_layer(x, ctx, c_lat, c_cx, norm_g, mod_w, mod_b, w_in, rpb, dn_conv, dn_a_log, dn_dt_bias, dn_norm_g, w_out, ctx_needed):
    hx, gate_x = adaln_in(x, c_lat, norm_g, mod_w, mod_b)
    hc, gate_c = adaln_in(ctx, c_cx, norm_g, mod_w, mod_b)
    px = hx @ w_in
    pc = hc @ w_in
    b, l, _ = px.shape
    lc = pc.shape[1]
    qx, kx, vx = (t.reshape(b, l, NA_HEADS, NA_DH) for t in jnp.split(px[..., :OFF_DN], 3, axis=-1))
    qc, kc, vc = (t.reshape(b, lc, NA_HEADS, NA_DH) for t in jnp.split(pc[..., :OFF_DN], 3, axis=-1))
    na_x = neighbourhood_attention(qx, kx, vx, kc, vc, rpb)
    dn_x, dn_c = bidir_gated_deltanet(px[..., OFF_DN:OFF_AB], px[..., OFF_AB:OFF_Z], pc[..., OFF_DN:OFF_AB], pc[..., OFF_AB:OFF_Z], dn_conv, dn_a_log, dn_dt_bias, dn_norm_g)
    x_new = x + gate_x * ((jnp.concatenate([na_x, dn_x], axis=-1) * jax.nn.silu(px[..., OFF_Z:])) @ w_out)
    if ctx_needed:
        na_c = context_attention(qc, kc, vc)
        ctx = ctx + gate_c * ((jnp.concatenate([na_c, dn_c], axis=-1) * jax.nn.silu(pc[..., OFF_Z:])) @ w_out)
    return x_new, ctx


def hyena_filters(length, w1, b1, f1, w2, b2, f2, w3):
    t = jnp.linspace(0.0, 1.0, length, dtype=jnp.float32)[:, None]
    w = 2.0 * math.pi * jnp.arange(length, dtype=jnp.float32)[:, None] / length
    f = jnp.linspace(1e-4, HY_BANDS - 1, HY_BANDS, dtype=jnp.float32)[None, :]
    feats = jnp.concatenate([t, jnp.cos(f * w), -jnp.sin(f * w)], axis=-1)
    hid = jnp.sin(f1 * (feats @ w1 + b1))
    hid = jnp.sin(f2 * (hid @ w2 + b2))
    h = (hid @ w3).astype(jnp.float32).reshape(length, HY_ORDER, HY_DIRS, HY_W)
    decay = jnp.abs(jnp.linspace(HY_DECAY_MIN, HY_DECAY_MAX, HY_W, dtype=jnp.float32))
    h = h * jnp.exp(-t * decay)[:, None, None, :]
    return h * lax.rsqrt(jnp.sum(h * h, axis=0, keepdims=True) + EPS)


def long_conv(u, h_fwd, h_bwd, skip):
    length = u.shape[1]
    n = 2 * length
    u32 = u.astype(jnp.float32)
    hf = jnp.fft.rfft(h_fwd, n=n, axis=0)
    hb = jnp.fft.rfft(h_bwd, n=n, axis=0)
    y_f = jnp.fft.irfft(jnp.fft.rfft(u32, n=n, axis=1) * hf, n=n, axis=1)[:, :length]
    y_b = jnp.fft.irfft(jnp.fft.rfft(u32[:, ::-1], n=n, axis=1) * hb, n=n, axis=1)[:, :length][:, ::-1]
    return (y_f + y_b + u32 * skip.astype(jnp.float32)).astype(u.dtype)


def hyena_layer(s, cvec, norm_g, mod_w, mod_b, w_in, conv_w, fw1, fb1, ff1, fw2, fb2, ff2, fw3, skip, w_out):
    h, gate = adaln_in(s, cvec, norm_g, mod_w, mod_b)
    p = h @ w_in
    u = dwconv(p[..., :3 * HY_W], conv_w)
    v, x1, x2 = jnp.split(u, 3, axis=-1)
    filt = hyena_filters(s.shape[1], fw1, fb1, ff1, fw2, fb2, ff2, fw3)
    z = x1 * long_conv(v, filt[:, 0, 0], filt[:, 0, 1], skip[0])
    y = x2 * long_conv(z, filt[:, 1, 0], filt[:, 1, 1], skip[1])
    return s + gate * ((y * jax.nn.silu(p[..., 3 * HY_W:])) @ w_out)


def setup_inputs(seed: int = 0) -> dict:
    key = jax.random.key(seed)
    ks = jax.random.split(key, 32)
    f32 = jnp.float32
    D = D_MODEL

    def nrm(k, shape, s):
        return jax.random.normal(k, shape, f32) * s

    dt = jnp.exp(jax.random.uniform(ks[11], (N_EVEN, 2, DN_HEADS), f32, math.log(1e-3), math.log(1e-1)))
    return {
        'x': nrm(ks[0], (BATCH, SEQ, D), 1.0),
        'c': nrm(ks[1], (BATCH, D), 1.0),
        'ctx': nrm(ks[2], (BATCH, CTX_LEN, D), 1.0),
        'c_ctx': nrm(ks[3], (D,), 1.0),
        'e_norm_g': 1.0 + nrm(ks[4], (N_EVEN, D), 0.05),
        'e_mod_w': nrm(ks[5], (N_EVEN, D, 3 * D), D ** -0.5),
        'e_mod_b': nrm(ks[6], (N_EVEN, 3 * D), 0.02),
        'e_w_in': nrm(ks[7], (N_EVEN, D, IN0_W), D ** -0.5),
        'e_na_rpb': nrm(ks[8], (N_EVEN, NA_HEADS, 2 * NA_WIN_R - 1, 2 * NA_WIN_C - 1), 0.1),
        'e_dn_conv': nrm(ks[9], (N_EVEN, DN_CONV, DN_CONV_W), DN_CONV ** -0.5),
        'e_dn_a_log': jnp.log(jax.random.uniform(ks[10], (N_EVEN, 2, DN_HEADS), f32, 1.0, 16.0)),
        'e_dn_dt_bias': dt + jnp.log(-jnp.expm1(-dt)),
        'e_dn_norm_g': 1.0 + nrm(ks[12], (N_EVEN, DN_DV), 0.05),
        'e_w_out': nrm(ks[13], (N_EVEN, MIX0_W, D), MIX0_W ** -0.5),
        'o_norm_g': 1.0 + nrm(ks[14], (N_ODD, D), 0.05),
        'o_mod_w': nrm(ks[15], (N_ODD, D, 3 * D), D ** -0.5),
        'o_mod_b': nrm(ks[16], (N_ODD, 3 * D), 0.02),
        'o_w_in': nrm(ks[17], (N_ODD, D, IN1_W), D ** -0.5),
        'o_hy_conv': nrm(ks[18], (N_ODD, HY_CONV, 3 * HY_W), HY_CONV ** -0.5),
        'o_ffn_w1': nrm(ks[19], (N_ODD, HY_EMB, HY_FFN), HY_EMB ** -0.5),
        'o_ffn_b1': nrm(ks[20], (N_ODD, HY_FFN), 0.02),
        'o_ffn_f1': 1.0 + nrm(ks[21], (N_ODD, HY_FFN), 0.05),
        'o_ffn_w2': nrm(ks[22], (N_ODD, HY_FFN, HY_FFN), HY_FFN ** -0.5),
        'o_ffn_b2': nrm(ks[23], (N_ODD, HY_FFN), 0.02),
        'o_ffn_f2': 1.0 + nrm(ks[24], (N_ODD, HY_FFN), 0.05),
        'o_ffn_w3': nrm(ks[25], (N_ODD, HY_FFN, HY_ORDER * HY_DIRS * HY_W), HY_FFN ** -0.5),
        'o_hy_skip': nrm(ks[26], (N_ODD, HY_ORDER, HY_W), 0.5),
        'o_w_out': nrm(ks[27], (N_ODD, HY_W, D), HY_W ** -0.5),
        'final_norm_g': 1.0 + nrm(ks[28], (D,), 0.05),
    }


def reference(x, c, ctx, c_ctx, e_norm_g, e_mod_w, e_mod_b, e_w_in, e_na_rpb, e_dn_conv, e_dn_a_log, e_dn_dt_bias, e_dn_norm_g, e_w_out, o_norm_g, o_mod_w, o_mod_b, o_w_in, o_hy_conv, o_ffn_w1, o_ffn_b1, o_ffn_f1, o_ffn_w2, o_ffn_b2, o_ffn_f2, o_ffn_w3, o_hy_skip, o_w_out, final_norm_g):
    c_lat = c[:, None, :]
    c_cx = c_ctx[None, None, :]
    for i in range(DEPTH):
        ctx_needed = any(j % 2 == 0 for j in range(i + 1, DEPTH))
        j = i // 2
        if i % 2 == 0:
            x, ctx = even_layer(x, ctx, c_lat, c_cx, e_norm_g[j], e_mod_w[j], e_mod_b[j], e_w_in[j], e_na_rpb[j], e_dn_conv[j], e_dn_a_log[j], e_dn_dt_bias[j], e_dn_norm_g[j], e_w_out[j], ctx_needed)
        else:
            x_new = hyena_layer(x, c_lat, o_norm_g[j], o_mod_w[j], o_mod_b[j], o_w_in[j], o_hy_conv[j], o_ffn_w1[j], o_ffn_b1[j], o_ffn_f1[j], o_ffn_w2[j], o_ffn_b2[j], o_ffn_f2[j], o_ffn_w3[j], o_hy_skip[j], o_w_out[j])
            if ctx_needed:
                ctx = hyena_layer(ctx, c_cx, o_norm_g[j], o_mod_w[j], o_mod_b[j], o_w_in[j], o_hy_conv[j], o_ffn_w1[j], o_ffn_b1[j], o_ffn_f1[j], o_ffn_w2[j], o_ffn_b2[j], o_ffn_f2[j], o_ffn_w3[j], o_hy_skip[j], o_w_out[j])
            x = x_new
    return rmsnorm(x, final_norm_g)
```

```python
import math
import numpy as np
import ml_dtypes
from contextlib import ExitStack
import concourse.bass as bass
import concourse.mybir as mybir
from concourse.bass_utils import run_bass_kernel_spmd

F32 = mybir.dt.float32
BF16 = mybir.dt.bfloat16
I32 = mybir.dt.int32
AF = mybir.ActivationFunctionType
ALU = mybir.AluOpType
AX = mybir.AxisListType

ENGS = ('pe', 'act', 'dve', 'pool', 'sp')
NDMA = {'sp': 8, 'act': 8}


class Prog:
    def __init__(self, nc, same_eng_sync=('pool', 'act', 'dve')):
        self.nc = nc
        self.ops = {e: [] for e in ENGS}
        self.ccount = {e: 0 for e in ENGS}
        self.dcount = {e: 0 for e in ENGS}
        self.last_w = {}
        self.readers = {}
        self.waited = {e: {} for e in ENGS}
        self.pending = {e: set() for e in ENGS}
        self.same = same_eng_sync
        self.stack = ExitStack()

    def _nm(self, name):
        self._nmc = getattr(self, '_nmc', 0) + 1
        return '%s_%d' % (name, self._nmc)

    def sb(self, name, shape, dt, stack=None):
        return (stack or self.stack).enter_context(self.nc.sbuf_tensor(self._nm('s_' + name), list(shape), dt))

    def ps(self, name, shape, dt=F32, stack=None):
        return (stack or self.stack).enter_context(self.nc.psum_tensor(self._nm('p_' + name), list(shape), dt))

    def _deps(self, reads, writes):
        deps = set()
        for k in reads:
            t = self.last_w.get(k)
            if t is not None:
                deps.add(t)
        for k in writes:
            t = self.last_w.get(k)
            if t is not None:
                deps.add(t)
            for r in self.readers.get(k, ()):
                deps.add(r)
        return deps

    def _record(self, eng, fn, deps, tok, inc, reads, writes, extra_waits=()):
        deps = set(deps) | self.pending[eng]
        self.pending[eng] = set()
        waits = list(extra_waits)
        w = self.waited[eng]
        for (sk, val) in sorted(deps, key=lambda t: (str(t[0]), t[1])):
            if sk == eng and (eng == 'pe' or eng not in self.same):
                continue
            if w.get(sk, 0) >= val:
                continue
            w[sk] = val
            waits.append((sk, val))
        self.ops[eng].append((waits, fn, tok[0], inc))
        for k in reads:
            self.readers.setdefault(k, []).append(tok)
        for k in writes:
            self.last_w[k] = tok
            self.readers[k] = []

    def op(self, eng, fn, reads=(), writes=()):
        deps = self._deps(reads, writes)
        self.ccount[eng] += 1
        tok = (eng, self.ccount[eng])
        self._record(eng, fn, deps, tok, 1, reads, writes)

    def dma(self, eng, out, in_, reads=(), writes=(), **kw):
        if eng == 'pool':
            eng = 'act'
        deps = self._deps(reads, writes)
        j = self.dcount[eng]
        self.dcount[eng] += 1
        nd = NDMA[eng]
        slot, rnd = j % nd, j // nd
        sk = ('d', eng, slot)
        tok = (sk, 16 * (rnd + 1))
        extra = []
        if rnd > 0:
            w = self.waited[eng]
            if w.get(sk, 0) < 16 * rnd:
                w[sk] = 16 * rnd
                extra.append((sk, 16 * rnd))
        self._record(eng, lambda e: e.dma_start(out=out, in_=in_, **kw), deps, tok, 16, reads, writes, extra)

    def barrier(self):
        toks = set()
        for e in ENGS:
            if self.ccount[e] > 0:
                toks.add((e, self.ccount[e]))
            nd = NDMA.get(e)
            if nd:
                j = self.dcount[e]
                for s in range(min(nd, j)):
                    last_j = ((j - 1 - s) // nd) * nd + s if False else None
                for jj in range(max(0, j - nd), j):
                    toks.add((('d', e, jj % nd), 16 * (jj // nd + 1)))
        for e in ENGS:
            self.pending[e] |= toks
        self.last_w = {}
        self.readers = {}

    def finalize(self):
        nc = self.nc
        self.barrier()
        with ExitStack() as st:
            sems = {}
            for e in ('pe', 'act', 'dve', 'pool'):
                sems[e] = st.enter_context(nc.semaphore('c_' + e))
            for e, nd in NDMA.items():
                for s in range(nd):
                    sems[('d', e, s)] = st.enter_context(nc.semaphore('d_%s_%d' % (e, s)))
            block = st.enter_context(nc.Block())

            def run(eng_name):
                def body(e):
                    for (waits, fn, sk, inc) in self.ops[eng_name]:
                        for (wk, val) in waits:
                            e.wait_ge(sems[wk], val)
                        ins = fn(e)
                        ins.then_inc(sems[sk], inc)
                    w = self.waited[eng_name]
                    for (wk, val) in sorted(self.pending[eng_name], key=lambda t: (str(t[0]), t[1])):
                        if wk == eng_name:
                            continue
                        if w.get(wk, 0) >= val:
                            continue
                        e.wait_ge(sems[wk], val)
                return body

            block.tensor(run('pe'))
            block.scalar(run('act'))
            block.vector(run('dve'))
            block.gpsimd(run('pool'))
            block.sync(run('sp'))

    def stats(self):
        return {e: len(self.ops[e]) for e in ENGS}


D = 1024; L = 4096; LC = 256; IN0 = 4112
OFF_DN = 1536; OFF_AB = 3072; OFF_Z = 3088


def phase_mod(P, T, pre, pers):
    nc = P.nc
    modw = T[pre + '_mod_w']
    with ExitStack() as st:
        cc = P.sb('cc', [128, 16], F32, st)
        sc = P.sb('sc', [128, 16], F32, st)
        mb = P.sb('mb', [128, 24], F32, st)
        ng = P.sb('ng', [128, 8], F32, st)
        wb = [P.sb('mw%d' % k, [128, 3072], F32, st) for k in range(8)]
        ps = P.ps('modps', [128, 512], F32, st)
        modv = pers['modv']
        P.dma('sp', cc[:], T['cc'], writes=['cc'])
        P.dma('sp', mb[:], T[pre + '_mod_b'], writes=['mb'])
        P.dma('sp', ng[:], T[pre + '_norm_g'], writes=['ng'])
        for k in range(8):
            P.dma('sp' if k % 2 == 0 else 'pool', wb[k][:], modw[k * 128:(k + 1) * 128, :], writes=[('mw', k)])
        P.op('act', lambda e: e.activation(sc[:], cc[:], AF.Silu), reads=['cc'], writes=['sc'])
        for m in range(24):
            for k in range(8):
                P.op('pe', lambda e, m=m, k=k: e.matmul(ps[:, 2 * m:2 * m + 2], wb[k][:, m * 128:(m + 1) * 128], sc[:, 2 * k:2 * k + 2], start=(k == 0), stop=(k == 7)),
                     reads=[('mw', k), 'sc'], writes=['modps'])
        for n in range(2):
            P.op('dve', lambda e, n=n: e.tensor_tensor(modv[:, :, n], ps[:, n:48:2], mb[:], ALU.add), reads=['mb'], writes=['modps', ('modv', pre)])
        gs = pers['gs']
        for n in range(2):
            P.op('dve', lambda e, n=n: e.scalar_tensor_tensor(gs[:, :, n], modv[:, 8:16, n], 1.0, ng[:], ALU.add, ALU.mult), reads=['ng', ('modv', pre)], writes=[('gs', pre)])
    P.barrier()


def rr(lst):
    i = [0]
    def nxt():
        v = lst[i[0] % len(lst)]
        i[0] += 1
        return v
    return nxt


def adaln_tile(P, xsrc, ntok, n, pers, pre, bufs, key, dst=None):
    xt, xk = bufs['xt']()
    if dst is not None:
        hT, hk = dst
    else:
        hT, hk = bufs['hT']()
    sq = bufs['sq']; ones = bufs['ones']; rstd = bufs['rstd']; bank = bufs['ssbank']
    modv, gs = pers['modv'], pers['gs']
    P.dma('sp', xt[:, :, :ntok], xsrc, writes=[xk])
    P.op('act', lambda e: e.activation(sq[:, :, :ntok], xt[:, :, :ntok], AF.Square), reads=[xk], writes=['sq'] + [('sqj', j) for j in range(8)])
    for j in range(8):
        P.op('pe', lambda e, j=j: e.matmul(bank[:, :ntok], ones[:], sq[:, j, :ntok], start=(j == 0), stop=(j == 7)),
             reads=['sq', 'ones'], writes=['ssbank'])
    P.op('dve', lambda e: e.tensor_scalar(rstd[:, :ntok], bank[:, :ntok], 1.0 / 1024, 1e-6, ALU.mult, ALU.add), writes=['ssbank', 'rstd'])
    P.op('act', lambda e: e.activation(rstd[:, :ntok], rstd[:, :ntok], AF.Sqrt), writes=['rstd'])
    P.op('dve', lambda e: e.vector.reciprocal(rstd[:, :ntok], rstd[:, :ntok]) if False else e.reciprocal(rstd[:, :ntok], rstd[:, :ntok]), writes=['rstd'])
    for j in range(8):
        P.op('dve', lambda e, j=j: e.scalar_tensor_tensor(sq[:, j, :ntok], xt[:, j, :ntok], gs[:, j, n:n + 1], rstd[:, :ntok], ALU.mult, ALU.mult),
             reads=[xk, 'rstd', ('gs', pre)], writes=[('sqj', j)] + (['sq'] if j == 0 else []))
        P.op('act', lambda e, j=j: e.activation(hT[:, j, :ntok], sq[:, j, :ntok], AF.Identity, bias=modv[:, j, n:n + 1], scale=1.0),
             reads=[('sqj', j), ('modv', pre)], writes=[(hk, j)] + ([hk] if j == 0 else []))
    return hT, [(hk, j) for j in range(8)], xt, xk


def load_weights_bf16(P, wdram, W, wst, ncols, wkey):
    for k in range(8):
        ws, wsk = wst()
        P.dma('sp' if k % 2 == 0 else 'pool', ws[:, :ncols], wdram[k * 128:(k + 1) * 128, :], writes=[wsk])
        eng = 'pool' if k % 2 == 0 else 'act'
        if eng == 'pool':
            P.op('pool', lambda e, k=k, ws=ws: e.tensor_copy(W[:, k, :ncols], ws[:, :ncols]), reads=[wsk], writes=[(wkey, k)])
        else:
            P.op('act', lambda e, k=k, ws=ws: e.copy(W[:, k, :ncols], ws[:, :ncols]), reads=[wsk], writes=[(wkey, k)])


def phase_l0_proj(P, T, pers, S):
    nc = P.nc
    with ExitStack() as st:
        W = P.sb('W', [128, 8, IN0], BF16, st)
        wstl = [(P.sb('wst%d' % i, [128, IN0], F32, st), 'wst%d' % i) for i in range(1)]
        ones = P.sb('ones', [128, 128], F32, st)
        xts = [(P.sb('xt%d' % i, [128, 8, 512], F32, st), 'xt%d' % i) for i in range(2)]
        hTs = [(P.sb('hT%d' % i, [128, 8, 512], BF16, st), 'hT%d' % i) for i in range(2)]
        sq = P.sb('sq', [128, 8, 512], F32, st)
        rstd = P.sb('rstd', [128, 512], F32, st)
        ofs = [(P.sb('of%d' % i, [128, 512], F32, st), 'of%d' % i) for i in range(3)]
        obs = [(P.sb('ob%d' % i, [128, 512], BF16, st), 'ob%d' % i) for i in range(4)]
        banks = [(P.ps('pb%d' % i, [128, 512], F32, st), 'pb%d' % i) for i in range(8)]
        P.op('pool', lambda e: e.memset(ones[:], 1.0), writes=['ones'])
        load_weights_bf16(P, T['e_w_in'], W, rr(wstl), IN0, 'W')
        wkeys = [('W', k) for k in range(8)]
        bufs = dict(xt=rr(xts), hT=rr(hTs), sq=sq, ones=ones, rstd=rstd, ssbank=banks[0][0])
        pbank = rr(banks[1:])
        of = rr(ofs); ob = rr(obs)
        evac_i = [0]

        def evac(dst, src_bank, bkey, okey, func=None, scale=None):
            i = evac_i[0]; evac_i[0] += 1
            if func is not None or scale is not None or i % 2 == 0:
                f = func if func is not None else AF.Copy
                if scale is not None:
                    P.op('act', lambda e: e.activation(dst, src_bank, f, scale=scale), writes=[bkey, okey])
                else:
                    P.op('act', lambda e: e.activation(dst, src_bank, f), writes=[bkey, okey])
            else:
                P.op('dve', lambda e: e.tensor_copy(dst, src_bank), writes=[bkey, okey])

        xT3 = T['xT'].rearrange("(j p) t -> p j t", p=128)
        cT3 = T['ctxT'].rearrange("(j p) t -> p j t", p=128)
        tiles = [(xT3[:, :, tt * 512:(tt + 1) * 512], 512, 0, tt * 512) for tt in range(8)] + [(cT3, 256, 1, 4096)]
        dq = rr(['sp', 'pool'])
        def do_tile(src, ntok, n, t0):
            isx = (n == 0)
            hT, hkeys, _, _ = adaln_tile(P, src, ntok, n, pers, 'e', bufs, None)
            fm = []
            if isx:
                fm += [('q', c) for c in range(4)]
            fm += [('k', c) for c in range(4)] + [('dn', c) for c in range(12)]
            for (kind, c) in fm:
                col0 = {'q': 0, 'k': 512, 'dn': OFF_DN}[kind] + c * 128
                bank, bkey = pbank()
                for k in range(8):
                    P.op('pe', lambda e, k=k, bank=bank, col0=col0: e.matmul(bank[:, :ntok], W[:, k, col0:col0 + 128], hT[:, k, :ntok], start=(k == 0), stop=(k == 7)),
                         reads=wkeys + hkeys, writes=[bkey])
                if kind == 'dn':
                    o, okey = of()
                    evac(o[:, :ntok], bank[:, :ntok], bkey, okey)
                    dst = (S['dnT'][c * 128:(c + 1) * 128, t0:t0 + ntok] if isx else S['dncT'][c * 128:(c + 1) * 128, :])
                else:
                    o, okey = ob()
                    evac(o[:, :ntok], bank[:, :ntok], bkey, okey, scale=(0.125 if kind == 'q' else None))
                    dst = (S['qT'][c * 128:(c + 1) * 128, t0:t0 + ntok] if kind == 'q' else S['kT'][c * 128:(c + 1) * 128, t0:t0 + ntok])
                P.dma(dq(), dst, o[:, :ntok], reads=[okey], writes=[('dram', kind, c, t0)])
            for s in range(ntok // 128):
                tk0 = t0 + s * 128
                groups = [('v', 1024, 512), ('ab', OFF_AB, 16)]
                if isx:
                    groups += [('z0', OFF_Z, 512), ('z1', OFF_Z + 512, 512)]
                for (kind, col0, ncol) in groups:
                    bank, bkey = pbank()
                    for k in range(8):
                        P.op('pe', lambda e, k=k, bank=bank, col0=col0, ncol=ncol, s=s: e.matmul(bank[:, :ncol], hT[:, k, s * 128:(s + 1) * 128], W[:, k, col0:col0 + ncol], start=(k == 0), stop=(k == 7)),
                             reads=wkeys + hkeys, writes=[bkey])
                    if kind == 'ab':
                        o, okey = of()
                        evac(o[:, :16], bank[:, :16], bkey, okey)
                        P.dma(dq(), S['abtok'][tk0:tk0 + 128, :], o[:, :16], reads=[okey], writes=[('dram', 'ab', tk0)])
                    elif kind == 'v':
                        o, okey = ob()
                        evac(o[:, :], bank[:, :], bkey, okey)
                        P.dma(dq(), S['vtok'][tk0:tk0 + 128, :], o[:, :], reads=[okey], writes=[('dram', 'v', tk0)])
                    else:
                        zc = 0 if kind == 'z0' else 512
                        o, okey = ob()
                        evac(o[:, :], bank[:, :], bkey, okey, func=AF.Silu)
                        P.dma(dq(), S['ztok'][tk0:tk0 + 128, zc:zc + 512], o[:, :], reads=[okey], writes=[('dram', 'z', tk0, zc)])
        for tl in tiles:
            do_tile(*tl)
    P.barrier()


NEG = -30000.0

def na_bias_table(rpb):
    out = np.full((5, 128, 8, 5, 128), NEG, np.float32)
    blocks = [0, 1, 2, 30, 31]
    for vi, i in enumerate(blocks):
        cs = min(max(i - 2, 0), 27)
        for ql in range(128):
            r = 2 * i + ql // 64; qc = ql % 64
            r0 = min(max(r - 4, 0), 56); c0 = min(max(qc - 8, 0), 48)
            for ch in range(5):
                for kr_l in range(2):
                    kr = 2 * (cs + ch) + kr_l
                    if not (r0 <= kr < r0 + 8):
                        continue
                    kcs = np.arange(c0, c0 + 16)
                    out[vi, kr_l * 64 + kcs, :, ch, ql] = rpb[:, kr - r + 7, kcs - qc + 15].T
    return out

def variant_of(i):
    return {0: 0, 1: 1, 30: 3, 31: 4}.get(i, 2)


def phase_na(P, T, S):
    with ExitStack() as st:
        kT = P.sb('kT', [128, 4, 4352], BF16, st)
        qT = P.sb('qT', [128, 4, 4096], BF16, st)
        va = P.sb('va', [128, 34, 8, 65], BF16, st)
        bt = P.sb('bt', [128, 5, 8 * 5 * 128], BF16, st)
        btf = P.sb('btf', [128, 8 * 5 * 128], F32, st)
        ident = P.sb('ident', [128, 128], BF16, st)
        identf = P.sb('identf', [128, 128], F32, st)
        pts = [(P.sb('pt%d' % i, [128, 896], BF16, st), 'pt%d' % i) for i in range(2)]
        zts = [(P.sb('zt%d' % i, [128, 512], BF16, st), 'zt%d' % i) for i in range(2)]
        nas = [(P.sb('na%d' % i, [128, 8, 64], F32, st), 'na%d' % i) for i in range(2)]
        mxs = [(P.sb('mx%d' % i, [128, 512], BF16, st), 'mx%d' % i) for i in range(2)]
        rcs = [(P.sb('rc%d' % i, [128, 8], F32, st), 'rc%d' % i) for i in range(2)]
        sA = [(P.ps('sA%d' % i, [128, 512], F32, st), 'sA%d' % i) for i in range(2)]
        sB = [(P.ps('sB%d' % i, [128, 512], F32, st), 'sB%d' % i) for i in range(2)]
        oC = [(P.ps('oC%d' % i, [128, 512], F32, st), 'oC%d' % i) for i in range(4)]
        for hp in range(4):
            P.dma('sp', kT[:, hp, :], S['kT'][hp * 128:(hp + 1) * 128, :], writes=['kT'])
            P.dma('pool', qT[:, hp, :], S['qT'][hp * 128:(hp + 1) * 128, :], writes=['qT'])
        P.op('pool', lambda e: e.memset(va[:], 1.0), writes=['va'])
        vsrc = S['vtok'].rearrange("(c p) f -> p c f", p=128)
        for h in range(8):
            for (c0, c1) in ((0, 17), (17, 34)):
                P.dma('sp' if h % 2 == 0 else 'act', va[:, c0:c1, h, 0:64], vsrc[:, c0:c1, h * 64:(h + 1) * 64], writes=['va'])
        P.dma('sp', identf[:], T['ident'], writes=['identf'])
        P.op('pool', lambda e: e.tensor_copy(ident[:], identf[:]), reads=['identf'], writes=['ident'])
        for v in range(5):
            P.dma('sp', btf[:], T['na_bias'][v], writes=['btf'])
            P.op('act', lambda e, v=v: e.copy(bt[:, v, :], btf[:]), reads=['btf'], writes=['bt'])
        pt_n = rr(pts); zt_n = rr(zts); na_n = rr(nas); mx_n = rr(mxs); rc_n = rr(rcs)
        sA_n = rr(sA); sB_n = rr(sB); oC_n = rr(oC)

        blk = {}

        def stage1(i, h):
            cs = min(max(i - 2, 0), 27)
            v = variant_of(i)
            chunks = [cs + c for c in range(5)] + [32, 33]
            q0 = i * 128
            if h == 0:
                zt, ztk = zt_n()
                P.dma('act', zt[:], S['ztok'][q0:q0 + 128, 0:512], writes=[ztk])
                blk[i] = dict(zt=(zt, ztk), ocs=[oC_n(), oC_n()])
            hp, hb = h // 2, (h % 2) * 64
            a, ak = sA_n(); b, bk = sB_n()
            for ci, ch in enumerate(chunks):
                bank, bkk = (a, ak) if ci < 4 else (b, bk)
                col = (ci % 4) * 128
                has_bias = ci < 5
                P.op('pe', lambda e, bank=bank, col=col, ch=ch, has_bias=has_bias: e.matmul(
                    bank[:, col:col + 128], kT[hb:hb + 64, hp, ch * 128:(ch + 1) * 128], qT[hb:hb + 64, hp, q0:q0 + 128], start=True, stop=not has_bias),
                    reads=['kT', 'qT'], writes=[bkk])
                if has_bias:
                    off = (h * 5 + ci) * 128
                    P.op('pe', lambda e, bank=bank, col=col, off=off: e.matmul(bank[:, col:col + 128], ident[:], bt[:, v, off:off + 128], start=False, stop=True),
                         reads=['ident', 'bt'], writes=[bkk])
            pt, ptk = pt_n()
            P.op('act', lambda e: e.activation(pt[:, 0:512], a[:, :], AF.Exp), writes=[ak, (ptk, 0)])
            P.op('act', lambda e: e.activation(pt[:, 512:896], b[:, 0:384], AF.Exp), writes=[bk, (ptk, 1)])
            return dict(i=i, h=h, chunks=chunks, pt=pt, ptk=ptk, q0=q0)

        def stage2(c):
            i, h, chunks, pt, ptk, q0 = c['i'], c['h'], c['chunks'], c['pt'], c['ptk'], c['q0']
            ocs = blk[i]['ocs']
            oc, ock = ocs[h // 4]
            hh = h % 4
            for ci, ch in enumerate(chunks):
                P.op('pe', lambda e, ci=ci, ch=ch: e.matmul(oc[:, hh * 65:hh * 65 + 65], pt[:, ci * 128:(ci + 1) * 128], va[:, ch, h, :], start=(ci == 0), stop=(ci == 6)),
                     reads=[(ptk, 0), (ptk, 1), 'va'], writes=[ock])
            if h < 7:
                return
            zt, ztk = blk[i]['zt']
            na, nak = na_n(); rc, rck = rc_n(); mx, mxk = mx_n()
            for g in range(2):
                ocg, ocgk = ocs[g]
                ocv = ocg[:, 0:260].rearrange("p (h d) -> p h d", d=65)
                P.op('dve', lambda e, ocv=ocv, g=g: e.reciprocal(rc[:, g * 4:(g + 1) * 4], ocv[:, :, 64]), writes=[ocgk, (rck, g)])
                for h2 in range(4):
                    P.op('dve', lambda e, ocv=ocv, g=g, h2=h2: e.tensor_scalar(na[:, g * 4 + h2, :], ocv[:, h2, 0:64], rc[:, g * 4 + h2:g * 4 + h2 + 1], None, ALU.mult),
                         reads=[(rck, g)], writes=[ocgk, (nak, g)])
            P.op('pool', lambda e: e.tensor_tensor(mx[:], na[:].rearrange("p h d -> p (h d)"), zt[:], ALU.mult), reads=[(nak, 0), (nak, 1), ztk], writes=[mxk])
            P.dma('sp', S['mixtok'][q0:q0 + 128, 0:512], mx[:], reads=[mxk], writes=[('dram', 'mix', i)])
            if 'na_dbg' in S:
                P.dma('sp', S['na_dbg'][q0:q0 + 128, :], na[:].rearrange("p h d -> p (h d)"), reads=[(nak, 0), (nak, 1)], writes=[('dram', 'nadbg', i)])

        items = [(i, h) for i in range(32) for h in range(8)]
        prev = None
        for (i, h) in items:
            cur = stage1(i, h)
            if prev is not None:
                stage2(prev)
            prev = cur
        stage2(prev)
    P.barrier()


def phase_dn_gb(P, T, S, pers):
    with ExitStack() as st:
        ab = P.sb('ab', [128, 34, 16], F32, st)
        xa = P.sb('xa', [128, 34, 8], F32, st)
        t1 = P.sb('t1', [128, 34, 8], F32, st)
        t2 = P.sb('t2', [128, 34, 8], F32, st)
        al = P.sb('al', [128, 8], F32, st)
        dtb = P.sb('dtb', [128, 8], F32, st)
        g, beta = pers['g'], pers['beta']
        P.dma('sp', ab[:], S['abtok'].rearrange("(c p) f -> p c f", p=128), writes=['ab'])
        P.dma('sp', al[:], T['e_dn_a_log'].partition_broadcast(128), writes=['al'])
        P.dma('sp', dtb[:], T['e_dn_dt_bias'].partition_broadcast(128), writes=['dtb'])
        P.op('act', lambda e: e.activation(al[:], al[:], AF.Exp), writes=['al'])
        P.op('dve', lambda e: e.tensor_tensor(xa[:], ab[:, :, 0:8], dtb[:].unsqueeze(1).to_broadcast([128, 34, 8]), ALU.add), reads=['ab', 'dtb'], writes=['xa'])
        P.op('act', lambda e: e.activation(t1[:], xa[:], AF.Abs), reads=['xa'], writes=['t1'])
        P.op('act', lambda e: e.activation(t1[:], t1[:], AF.Exp, scale=-1.0), writes=['t1'])
        P.op('dve', lambda e: e.tensor_scalar_add(t1[:], t1[:], 1.0), writes=['t1'])
        P.op('act', lambda e: e.activation(t1[:], t1[:], AF.Ln), writes=['t1'])
        P.op('dve', lambda e: e.tensor_scalar_max(t2[:], xa[:], 0.0), reads=['xa'], writes=['t2'])
        P.op('dve', lambda e: e.tensor_tensor(t2[:], t2[:], t1[:], ALU.add), reads=['t1'], writes=['t2'])
        P.op('dve', lambda e: e.scalar_tensor_tensor(g[:], t2[:], -1.0, al[:].unsqueeze(1).to_broadcast([128, 34, 8]), ALU.mult, ALU.mult), reads=['t2', 'al'], writes=['g'])
        P.op('act', lambda e: e.activation(beta[:], ab[:, :, 8:16], AF.Sigmoid), reads=['ab'], writes=['beta'])
    P.barrier()


def phase_dn_prep(P, T, S):
    with ExitStack() as st:
        cw = P.sb('cw', [128, 60], F32, st)
        ones = P.sb('ones', [128, 128], F32, st)
        ident = P.sb('ident', [128, 128], BF16, st)
        identf = P.sb('identf', [128, 128], F32, st)
        raws = [(P.sb('raw%d' % i, [128, 4100], F32, st), 'raw%d' % i) for i in range(2)]
        accs = [(P.sb('acc%d' % i, [128, 4096], F32, st), 'acc%d' % i) for i in range(2)]
        sq = P.sb('sq', [128, 4096], F32, st)
        rns = [(P.sb('rn%d' % i, [128, 4096], F32, st), 'rn%d' % i) for i in range(2)]
        rn_n = rr(rns)
        obs = [(P.sb('obf%d' % i, [128, 4096], BF16, st), 'obf%d' % i) for i in range(2)]
        tks = [(P.sb('tk%d' % i, [128, 8, 128], BF16, st), 'tk%d' % i) for i in range(2)]
        banks = [(P.ps('nb%d' % i, [128, 512], F32, st), 'nb%d' % i) for i in range(2)]
        tps = [(P.ps('tpd%d' % i, [128, 1024], BF16, st), 'tpd%d' % i) for i in range(2)]
        P.dma('sp', cw[:], T['e_dn_conv'], writes=['cw'])
        P.op('pool', lambda e: e.memset(ones[:], 1.0), writes=['ones'])
        P.dma('sp', identf[:], T['ident'], writes=['identf'])
        P.op('pool', lambda e: e.tensor_copy(ident[:], identf[:]), reads=['identf'], writes=['ident'])
        for (r, rk) in raws:
            P.op('pool', lambda e, r=r: e.memset(r[:], 0.0), writes=[rk])
        raw_n = rr(raws); acc_n = rr(accs); ob_n = rr(obs); tk_n = rr(tks); bk_n = rr(banks); tp_n = rr(tps)
        dq = rr(['sp', 'pool'])

        def do_chunk(c, src, Lt, tok0):
            kind = c // 4; h = c % 4
            raw, rk = raw_n(); acc, ak = acc_n(); ob, obk = ob_n()
            rn, rnk = rn_n()
            if Lt < 4096:
                P.op('pool', lambda e: e.memset(raw[:, 2 + Lt:4 + Lt], 0.0), writes=[rk])
            P.dma(dq(), raw[:, 2:2 + Lt], src[c * 128:(c + 1) * 128, :], writes=[rk])
            P.op('dve', lambda e: e.tensor_scalar(acc[:, :Lt], raw[:, 0:Lt], cw[:, c * 5:c * 5 + 1], None, ALU.mult), reads=[rk, 'cw'], writes=[ak])
            for j in range(1, 5):
                P.op('dve', lambda e, j=j: e.scalar_tensor_tensor(acc[:, :Lt], raw[:, j:j + Lt], cw[:, c * 5 + j:c * 5 + j + 1], acc[:, :Lt], ALU.mult, ALU.add), reads=[rk, 'cw'], writes=[ak])
            P.op('act', lambda e: e.activation(acc[:, :Lt], acc[:, :Lt], AF.Silu), writes=[ak])
            if kind < 2:
                P.op('act', lambda e: e.activation(sq[:, :Lt], acc[:, :Lt], AF.Square), reads=[ak], writes=['sq'])
                for t0 in range(0, Lt, 512):
                    n = min(512, Lt - t0)
                    bank, bk = bk_n()
                    P.op('pe', lambda e, t0=t0, n=n, bank=bank: e.matmul(bank[:, :n], ones[:], sq[:, t0:t0 + n], start=True, stop=True), reads=['sq', 'ones'], writes=[bk])
                    P.op('dve', lambda e, t0=t0, n=n, bank=bank: e.tensor_scalar_add(rn[:, t0:t0 + n], bank[:, :n], 1e-6), writes=[bk, (rnk, t0), rnk])
                rkeys = [(rnk, t0) for t0 in range(0, Lt, 512)]
                yield
                P.op('act', lambda e: e.activation(rn[:, :Lt], rn[:, :Lt], AF.Sqrt), writes=rkeys + [rnk])
                P.op('dve', lambda e: e.reciprocal(rn[:, :Lt], rn[:, :Lt]), writes=[rnk])
                sc = (128 ** -0.5) if kind == 0 else 1.0
                P.op('dve', lambda e: e.scalar_tensor_tensor(ob[:, :Lt], acc[:, :Lt], sc, rn[:, :Lt], ALU.mult, ALU.mult), reads=[ak, rnk], writes=[obk])
                dst = S['dQT'] if kind == 0 else S['dKT']
                P.dma(dq(), dst[h, :, tok0:tok0 + Lt], ob[:, :Lt], reads=[obk], writes=[('dram', 'qk', c, tok0)])
            else:
                yield
                P.op('act', lambda e: e.copy(ob[:, :Lt], acc[:, :Lt]), reads=[ak], writes=[obk])
            if kind >= 1:
                dst = S['dKtok'] if kind == 1 else S['dVtok']
                for g0 in range(0, Lt // 128, 8):
                    ng = min(8, Lt // 128 - g0)
                    tp, tpk = tp_n(); tk, tkk = tk_n()
                    for s in range(ng):
                        P.op('pe', lambda e, s=s, g0=g0, tp=tp: e.transpose(tp[:, s * 128:(s + 1) * 128], ob[:, (g0 + s) * 128:(g0 + s + 1) * 128], ident[:]), reads=[obk, 'ident'], writes=[tpk])
                    P.op('act', lambda e, tp=tp, tk=tk, ng=ng: e.copy(tk[:, :ng, :], tp[:, :ng * 128].rearrange("p (s d) -> p s d", d=128)), writes=[tpk, tkk])
                    ta = tok0 + g0 * 128
                    P.dma(dq(), dst[ta:ta + ng * 128, h, :].rearrange("(s p) d -> p s d", p=128), tk[:, :ng, :], reads=[tkk], writes=[('dram', 'tok', c, ta)])

        items = [(c, S['dnT'], 4096, 0) for c in range(12)] + [(c, S['dncT'], 256, 4096) for c in range(12)]
        pend = None
        for it in items:
            g_ = do_chunk(*it)
            next(g_)
            if pend is not None:
                for _ in pend:
                    pass
            pend = g_
        for _ in pend:
            pass
    P.barrier()


def dn_masks():
    j = np.arange(128)[:, None]; t = np.arange(128)[None, :]
    same = (j // 64) == (t // 64)
    m = np.zeros((8, 128, 128), np.float32)
    m[0] = same & (j <= t)
    m[1] = same & (j >= t)
    m[2] = same & (t < j)
    m[3] = same & (t > j)
    m[4] = (j // 64 == 0) * np.ones((1, 128))
    m[5] = (j // 64 == 1) * np.ones((1, 128))
    m[6] = 1.0
    m[7] = np.eye(128)
    return m


def phase_dn_main(P, T, S, pers):
    g_all, b_all = pers['g'], pers['beta']
    with ExitStack() as st:
        msk = P.sb('msk', [128, 8, 128], F32, st)
        identb = P.sb('identb', [128, 128], BF16, st)
        P.dma('sp', msk[:], T['dn_masks'].rearrange("m p f -> p m f"), writes=['msk'])
        P.op('pool', lambda e: e.tensor_copy(identb[:], msk[:, 7, :]), reads=['msk'], writes=['identb'])
        TRI = [msk[:, 0, :], msk[:, 1, :]]; BM = [msk[:, 2, :], msk[:, 3, :]]; CH = [msk[:, 4, :], msk[:, 5, :]]
        ONES = msk[:, 6, :]; IDF = msk[:, 7, :]

        def bc_h(ap2):
            return ap2.unsqueeze(1).to_broadcast([128, 4, 128])

        def bc_l(ap2):
            return ap2.unsqueeze(2).to_broadcast([128, 4, 128])

        NS = 2
        def mk(name, dt, n=NS, shape=(128, 4, 128)):
            return [[(P.sb('%s%d_%d' % (name, d, i), list(shape), dt, st), '%s%d_%d' % (name, d, i)) for i in range(n)] for d in range(2)]
        QTt = mk('QTt', BF16); KTt = mk('KTt', BF16); Ktk = mk('Ktk', BF16); Vtk = mk('Vtk', BF16)
        Ab = mk('A', F32); Dm = mk('Dm', F32); DTm = mk('DTm', F32); EG = mk('EG', F32)
        Lb = mk('L', F32, 3); Nb = mk('N', F32, 3); XTf = mk('XT', F32, 3)
        XTb = mk('XTb', BF16); vb = mk('vb', BF16); kbg = mk('kbg', BF16); Kd = mk('Kd', BF16)
        Ub = mk('U', F32); WTb = mk('WT', BF16); Aqk = mk('Aqk', BF16); Qg = mk('Qg', BF16)
        stt_ = mk('st', F32, NS, (128, 16)); bgb = mk('bg', F32, NS, (128, 4)); tmpb = mk('tmp', F32)
        vnb = mk('vn', BF16, 1); ob = mk('o', F32, 2)
        Sf = [(P.sb('S%d' % d, [128, 4, 128], F32, st), 'S%d' % d) for d in range(2)]
        Sb = [(P.sb('Sb%d' % d, [128, 4, 128], BF16, st), 'Sb%d' % d) for d in range(2)]
        banks = [(P.ps('db%d' % i, [128, 512], F32, st), 'db%d' % i) for i in range(8)]
        bk_n = rr(banks)
        ctr = {}
        def nxt(pool, d):
            k = (id(pool), d)
            i = ctr.get(k, 0); ctr[k] = i + 1
            lst = pool[d]
            return lst[i % len(lst)]
        for d in range(2):
            P.op('pool', lambda e, d=d: e.memset(Sf[d][0][:], 0.0), writes=[Sf[d][1]])
            P.op('pool', lambda e, d=d: e.memset(Sb[d][0][:], 0.0), writes=[Sb[d][1]])
            P.op('pool', lambda e, d=d: e.memset(vnb[d][0][0][:], 0.0), writes=[vnb[d][0][1]])
        dq = rr(['sp', 'pool'])

        def prepass(tile, d, res):
            t0 = tile * 128
            R = {}
            dbgi = [0]
            def dbg(ap3, key):
                if 'dbg' in S and tile == S['dbg_tile'] and d == S['dbg_d']:
                    P.dma('sp', S['dbg'][dbgi[0]], ap3.rearrange("p h t -> p (h t)"), reads=[key], writes=[('dram', 'dbg', dbgi[0])])
                dbgi[0] += 1
            qt, qtk = nxt(QTt, d); kt, ktk = nxt(KTt, d); ktok, ktokk = nxt(Ktk, d); vtok, vtokk = nxt(Vtk, d)
            P.dma(dq(), qt[:], S['dQT'][:, :, t0:t0 + 128].rearrange("h p t -> p h t"), writes=[qtk])
            P.dma(dq(), kt[:], S['dKT'][:, :, t0:t0 + 128].rearrange("h p t -> p h t"), writes=[ktk])
            P.dma(dq(), ktok[:], S['dKtok'][t0:t0 + 128, :, :], writes=[ktokk])
            P.dma(dq(), vtok[:], S['dVtok'][t0:t0 + 128, :, :], writes=[vtokk])
            gt = g_all[:, tile, d * 4:(d + 1) * 4]; bt = b_all[:, tile, d * 4:(d + 1) * 4]
            z, zk = bk_n()
            P.op('pe', lambda e: e.matmul(z[:, 0:4], TRI[d], gt, start=True, stop=True), reads=['msk', 'g'], writes=[zk])
            P.op('pe', lambda e: e.matmul(z[:, 4:8], BM[d], gt, start=True, stop=True), reads=['msk', 'g'], writes=[zk])
            P.op('pe', lambda e: e.matmul(z[:, 8:12], CH[0], gt, start=True, stop=True), reads=['msk', 'g'], writes=[zk])
            P.op('pe', lambda e: e.matmul(z[:, 12:16], CH[1], gt, start=True, stop=True), reads=['msk', 'g'], writes=[zk])
            stv, stk = nxt(stt_, d)
            P.op('act', lambda e: e.activation(stv[:], z[:, 0:16], AF.Exp), writes=[zk, stk])
            yield
            bg, bgk = nxt(bgb, d)
            P.op('dve', lambda e: e.tensor_tensor(bg[:], bt, stv[:, 0:4], ALU.mult), reads=['beta', stk], writes=[bgk])
            A, Ak = nxt(Ab, d)
            for h in range(4):
                P.op('dve', lambda e, h=h: e.tensor_scalar(A[:, h, :], TRI[d], gt[:, h:h + 1], None, ALU.mult), reads=['msk', 'g'], writes=[Ak])
            d1, d1k = bk_n(); d2, d2k = bk_n(); d3, d3k = bk_n()
            for h in range(4):
                P.op('pe', lambda e, h=h: e.matmul(d1[:, h * 128:(h + 1) * 128], A[:, h, :], BM[d], start=True, stop=True), reads=[Ak, 'msk'], writes=[d1k])
            for h in range(4):
                P.op('pe', lambda e, h=h: e.matmul(d2[:, h * 128:(h + 1) * 128], BM[d], A[:, h, :], start=True, stop=True), reads=[Ak, 'msk'], writes=[d2k])
            for h in range(4):
                P.op('pe', lambda e, h=h: e.matmul(d3[:, h * 128:(h + 1) * 128], ONES, A[:, h, :], start=True, stop=True), reads=[Ak, 'msk'], writes=[d3k])
            dm, dmk = nxt(Dm, d); dtm, dtmk = nxt(DTm, d); eg, egk = nxt(EG, d)
            f3 = "p (h t) -> p h t"
            P.op('act', lambda e: e.activation(dm[:], d1[:].rearrange(f3, h=4), AF.Exp), writes=[d1k, dmk])
            P.op('act', lambda e: e.activation(dtm[:], d2[:].rearrange(f3, h=4), AF.Exp), writes=[d2k, dtmk])
            P.op('act', lambda e: e.activation(eg[:], d3[:].rearrange(f3, h=4), AF.Exp), writes=[d3k, egk])
            yield
            P.op('pool', lambda e: e.tensor_tensor(dm[:], dm[:], bc_h(BM[d]), ALU.mult), reads=['msk'], writes=[dmk])
            P.op('pool', lambda e: e.tensor_tensor(dtm[:], dtm[:], bc_h(TRI[d]), ALU.mult), reads=['msk'], writes=[dtmk])
            e1, e1k = bk_n(); e2, e2k = bk_n()
            for h in range(4):
                P.op('pe', lambda e, h=h: e.matmul(e1[:, h * 128:(h + 1) * 128], kt[:, h, :], kt[:, h, :], start=True, stop=True), reads=[ktk], writes=[e1k])
            for h in range(4):
                P.op('pe', lambda e, h=h: e.matmul(e2[:, h * 128:(h + 1) * 128], kt[:, h, :], qt[:, h, :], start=True, stop=True), reads=[ktk, qtk], writes=[e2k])
            tmp, tmpk = nxt(tmpb, d)
            L0, L0k = nxt(Lb, d)
            P.op('dve', lambda e: e.tensor_tensor(tmp[:], e1[:].rearrange(f3, h=4), dm[:], ALU.mult), reads=[dmk], writes=[e1k, tmpk])
            for h in range(4):
                P.op('dve', lambda e, h=h: e.tensor_scalar(L0[:, h, :], tmp[:, h, :], bt[:, h:h + 1], None, ALU.mult), reads=[tmpk, 'beta'], writes=[L0k])
            dbg(L0[:], L0k)
            aq, aqk = nxt(Aqk, d)
            P.op('dve', lambda e: e.tensor_tensor(aq[:], e2[:].rearrange(f3, h=4), dtm[:], ALU.mult), reads=[dtmk], writes=[e2k, aqk])
            yield
            qg, qgk = nxt(Qg, d)
            P.op('pool', lambda e: e.tensor_tensor(qg[:], qt[:], eg[:], ALU.mult), reads=[qtk, egk], writes=[qgk])
            tb, tbk = bk_n()
            for h in range(4):
                P.op('pe', lambda e, h=h: e.transpose(tb[:, h * 128:(h + 1) * 128], L0[:, h, :], IDF), reads=[L0k, 'msk'], writes=[tbk])
            N0, N0k = nxt(Nb, d)
            P.op('act', lambda e: e.copy(N0[:], tb[:].rearrange(f3, h=4)), writes=[tbk, N0k])
            yield
            XT, XTk = nxt(XTf, d)
            for h in range(4):
                P.op('dve', lambda e, h=h, XT=XT: e.scalar_tensor_tensor(XT[:, h, :], N0[:, h, :], -1.0, IDF, ALU.mult, ALU.add), reads=['msk', N0k], writes=[XTk])
            dbg(N0[:], N0k)
            dbg(XT[:], XTk)
            Lp, Lpk, Np, Npk = L0, L0k, N0, N0k
            for lev in range(1, 6):
                f1, f1k = bk_n()
                for h in range(4):
                    P.op('pe', lambda e, h=h, f1=f1, Lp=Lp, Np=Np: e.matmul(f1[:, h * 128:(h + 1) * 128], Np[:, h, :], Lp[:, h, :], start=True, stop=True), reads=[Lpk, Npk], writes=[f1k])
                Ln, Lnk = nxt(Lb, d)
                P.op('act', lambda e, f1=f1, Ln=Ln: e.copy(Ln[:], f1[:].rearrange(f3, h=4)), writes=[f1k, Lnk])
                if lev < 5:
                    f2, f2k = bk_n()
                    for h in range(4):
                        P.op('pe', lambda e, h=h, f2=f2, Lp=Lp, Np=Np: e.matmul(f2[:, h * 128:(h + 1) * 128], Lp[:, h, :], Np[:, h, :], start=True, stop=True), reads=[Lpk, Npk], writes=[f2k])
                    Nn, Nnk = nxt(Nb, d)
                    P.op('dve', lambda e, f2=f2, Nn=Nn: e.tensor_copy(Nn[:], f2[:].rearrange(f3, h=4)), writes=[f2k, Nnk])
                yield
                f3b, f3k = bk_n()
                for h in range(4):
                    P.op('pe', lambda e, h=h, f3b=f3b, Ln=Ln, XT=XT: e.matmul(f3b[:, h * 128:(h + 1) * 128], Ln[:, h, :], XT[:, h, :], start=True, stop=True), reads=[Lnk, XTk], writes=[f3k])
                XTn, XTnk = nxt(XTf, d)
                P.op('dve', lambda e, f3b=f3b, XT=XT, XTn=XTn: e.tensor_tensor(XTn[:], f3b[:].rearrange(f3, h=4), XT[:], ALU.add), reads=[XTk], writes=[f3k, XTnk])
                XT, XTk = XTn, XTnk
                yield
                if lev == 1:
                    dbg(Ln[:], Lnk)
                    dbg(XT[:], XTk)
                Lp, Lpk = Ln, Lnk
                if lev < 5:
                    Np, Npk = Nn, Nnk
            xtb, xtbk = nxt(XTb, d)
            P.op('act', lambda e: e.copy(xtb[:], XT[:]), reads=[XTk], writes=[xtbk])
            vbt, vbk = nxt(vb, d); kb, kbk = nxt(kbg, d); kd, kdk = nxt(Kd, d)
            for h in range(4):
                P.op('act', lambda e, h=h: e.activation(vbt[:, h, :], vtok[:, h, :], AF.Copy, scale=bt[:, h:h + 1]), reads=[vtokk, 'beta'], writes=[vbk])
                P.op('dve', lambda e, h=h: e.tensor_scalar(kb[:, h, :], ktok[:, h, :], bg[:, h:h + 1], None, ALU.mult), reads=[ktokk, bgk], writes=[kbk])
                P.op('act', lambda e, h=h: e.activation(kd[:, h, :], ktok[:, h, :], AF.Copy, scale=stv[:, 4 + h:5 + h]), reads=[ktokk, stk], writes=[kdk])
            g1, g1k = bk_n(); g2, g2k = bk_n()
            for h in range(4):
                P.op('pe', lambda e, h=h: e.matmul(g1[:, h * 128:(h + 1) * 128], xtb[:, h, :], vbt[:, h, :], start=True, stop=True), reads=[xtbk, vbk], writes=[g1k])
            for h in range(4):
                P.op('pe', lambda e, h=h: e.matmul(g2[:, h * 128:(h + 1) * 128], kb[:, h, :], xtb[:, h, :], start=True, stop=True), reads=[xtbk, kbk], writes=[g2k])
            U, Uk = nxt(Ub, d); WT, WTk = nxt(WTb, d)
            P.op('act', lambda e: e.copy(U[:], g1[:].rearrange(f3, h=4)), writes=[g1k, Uk])
            P.op('dve', lambda e: e.tensor_copy(WT[:], g2[:].rearrange(f3, h=4)), writes=[g2k, WTk])
            dbg(XT[:], XTk)
            dbg(U[:], Uk)
            res.update(U=(U, Uk), WT=(WT, WTk), aq=(aq, aqk), qg=(qg, qgk), kd=(kd, kdk), st=(stv, stk), tile=tile)

        def scan(pp, d, want_out):
            U, Uk = pp['U']; WT, WTk = pp['WT']; aq, aqk = pp['aq']; qg, qgk = pp['qg']; kd, kdk = pp['kd']; stv, stk = pp['st']
            tile = pp['tile']
            Sfd, Sfk = Sf[d]; Sbd, Sbk = Sb[d]
            vn, vnk = vnb[d][0]
            f3 = "p (h t) -> p h t"
            if want_out:
                o, ok = nxt(ob, d)
            def chunk_step(c):
                r0 = 64 * c
                h1, h1k = bk_n()
                for h in range(4):
                    P.op('pe', lambda e, h=h: e.matmul(h1[:, h * 128:(h + 1) * 128], WT[:, h, :], Sbd[:, h, :], start=True, stop=True), reads=[WTk, Sbk], writes=[h1k])
                P.op('dve', lambda e, r0=r0: e.tensor_tensor(vn[r0:r0 + 64, :, :], U[r0:r0 + 64, :, :], h1[r0:r0 + 64, :].rearrange(f3, h=4), ALU.subtract), reads=[Uk], writes=[h1k, vnk])
                yield
                if want_out:
                    h2, h2k = bk_n()
                    for h in range(4):
                        P.op('pe', lambda e, h=h: e.matmul(h2[:, h * 128:(h + 1) * 128], qg[:, h, :], Sbd[:, h, :], start=True, stop=False), reads=[qgk, Sbk], writes=[h2k])
                        P.op('pe', lambda e, h=h, r0=r0: e.matmul(h2[:, h * 128:(h + 1) * 128], aq[r0:r0 + 64, h, :], vn[r0:r0 + 64, h, :], start=False, stop=True), reads=[aqk, vnk], writes=[h2k])
                    P.op('act', lambda e, r0=r0: e.copy(o[r0:r0 + 64, :, :], h2[r0:r0 + 64, :].rearrange(f3, h=4)), writes=[h2k, ok])
                h3, h3k = bk_n()
                for h in range(4):
                    P.op('pe', lambda e, h=h, r0=r0: e.matmul(h3[:, h * 128:(h + 1) * 128], kd[r0:r0 + 64, h, :], vn[r0:r0 + 64, h, :], start=True, stop=True), reads=[kdk, vnk], writes=[h3k])
                for h in range(4):
                    P.op('dve', lambda e, c=c, h=h: e.scalar_tensor_tensor(Sfd[:, h, :], Sfd[:, h, :], stv[:, 8 + 4 * c + h:9 + 4 * c + h], h3[:, h * 128:(h + 1) * 128], ALU.mult, ALU.add), reads=[stk], writes=[h3k, Sfk])
                P.op('act', lambda e: e.copy(Sbd[:], Sfd[:]), reads=[Sfk], writes=[Sbk])
                yield
            for c in ([0, 1] if d == 0 else [1, 0]):
                yield from chunk_step(c)
            if want_out:
                t0 = tile * 128
                P.dma(dq(), S['dno'][d, t0:t0 + 128, :], o[:].rearrange("p h v -> p (h v)"), reads=[ok], writes=[('dram', 'dno', d, tile)])

        order = [[32, 33] + list(range(32)), [33, 32] + list(range(31, -1, -1))]
        nsteps = int(pers.get('dn_steps', 34))
        pp = {}
        def drive(gens):
            gens = list(gens)
            while gens:
                for g_ in list(gens):
                    try:
                        next(g_)
                    except StopIteration:
                        gens.remove(g_)

        for k in range(nsteps + 1):
            gens = []
            if k < nsteps:
                for d in range(2):
                    pp[(k, d)] = {}
                    gens.append(prepass(order[d][k], d, pp[(k, d)]))
            if k >= 1:
                for d in range(2):
                    gens.append(scan(pp[(k - 1, d)], d, order[d][k - 1] < 32))
            drive(gens)
        if 'dS' in S:
            for d in range(2):
                P.dma('sp', S['dS'][d], Sf[d][0][:].rearrange("p h v -> p (h v)"), reads=[Sf[d][1]], writes=[('dram', 'dS', d)])
    P.barrier()


def phase_dn_out(P, T, S):
    with ExitStack() as st:
        ng = P.sb('dng', [128, 128], F32, st)
        P.dma('sp', ng[:], T['e_dn_norm_g'].partition_broadcast(128), writes=['dng'])
        o0s = [(P.sb('oa%d' % i, [128, 4, 128], F32, st), 'oa%d' % i) for i in range(2)]
        o1s = [(P.sb('oc%d' % i, [128, 4, 128], F32, st), 'oc%d' % i) for i in range(2)]
        sqs = [(P.sb('osq%d' % i, [128, 4, 128], F32, st), 'osq%d' % i) for i in range(2)]
        zts = [(P.sb('oz%d' % i, [128, 4, 128], BF16, st), 'oz%d' % i) for i in range(2)]
        mxs = [(P.sb('om%d' % i, [128, 4, 128], BF16, st), 'om%d' % i) for i in range(2)]
        sss = [(P.sb('oss%d' % i, [128, 4], F32, st), 'oss%d' % i) for i in range(2)]
        o0n = rr(o0s); o1n = rr(o1s); sqn = rr(sqs); ztn = rr(zts); mxn = rr(mxs); ssn = rr(sss)

        def do_tile(i):
            t0 = i * 128
            a, ak = o0n(); b, bk = o1n(); sq, sk = sqn(); zt, zk = ztn(); mx, mk_ = mxn(); ss, ssk = ssn()
            P.dma('sp', a[:], S['dno'][0, t0:t0 + 128, :].rearrange("p (h v) -> p h v", h=4), writes=[ak])
            P.dma('pool', b[:], S['dno'][1, t0:t0 + 128, :].rearrange("p (h v) -> p h v", h=4), writes=[bk])
            P.dma('sp', zt[:], S['ztok'][t0:t0 + 128, 512:1024].rearrange("p (h v) -> p h v", h=4), writes=[zk])
            P.op('dve', lambda e: e.tensor_tensor(a[:], a[:], b[:], ALU.add), reads=[bk], writes=[ak])
            P.op('act', lambda e: e.activation(sq[:], a[:], AF.Square), reads=[ak], writes=[sk])
            P.op('dve', lambda e: e.tensor_reduce(ss[:], sq[:], AX.X, ALU.add), reads=[sk], writes=[ssk])
            P.op('dve', lambda e: e.tensor_scalar(ss[:], ss[:], 1.0 / 128, 1e-6, ALU.mult, ALU.add), writes=[ssk])
            P.op('act', lambda e: e.activation(ss[:], ss[:], AF.Sqrt), writes=[ssk])
            P.op('dve', lambda e: e.reciprocal(ss[:], ss[:]), writes=[ssk])
            for h in range(4):
                P.op('dve', lambda e, h=h: e.tensor_scalar(sq[:, h, :], a[:, h, :], ss[:, h:h + 1], None, ALU.mult), reads=[ak, ssk], writes=[sk])
            P.op('pool', lambda e: e.tensor_tensor(sq[:], sq[:], ng[:].unsqueeze(1).to_broadcast([128, 4, 128]), ALU.mult), reads=['dng'], writes=[sk])
            P.op('pool', lambda e: e.tensor_tensor(mx[:], sq[:], zt[:], ALU.mult), reads=[sk, zk], writes=[mk_])
            P.dma('sp', S['mixtok'][t0:t0 + 128, 512:1024], mx[:].rearrange("p h v -> p (h v)"), reads=[mk_], writes=[('dram', 'mixdn', i)])

        for i in range(32):
            do_tile(i)
    P.barrier()


def phase_out(P, T, S, pers, pre, wname, mix_is_tok, src_xT, dst_xT, final_out=None):
    with ExitStack() as st:
        W = P.sb('Wo', [128, 8, 1024], BF16, st)
        wstl = [(P.sb('wsto%d' % i, [128, 1024], F32, st), 'wsto%d' % i) for i in range(2)]
        ident = P.sb('ident', [128, 128], BF16, st)
        identf = P.sb('identf', [128, 128], F32, st)
        mts = [(P.sb('mt%d' % i, [128, 4, 1024], BF16, st), 'mt%d' % i) for i in range(2)]
        mTs = [(P.sb('mT%d' % i, [128, 8, 512], BF16, st), 'mT%d' % i) for i in range(2)]
        xts = [(P.sb('xo%d' % i, [128, 8, 512], F32, st), 'xo%d' % i) for i in range(2)]
        ots = [(P.sb('oo%d' % i, [128, 512], F32, st), 'oo%d' % i) for i in range(3)]
        tps = [(P.ps('tp%d' % i, [128, 1024], BF16, st), 'tp%d' % i) for i in range(2)]
        banks = [(P.ps('ob%d' % i, [128, 512], F32, st), 'ob%d' % i) for i in range(4)]
        if final_out is not None:
            xgs = [(P.sb('xg%d' % i, [128, 8, 512], F32, st), 'xg%d' % i) for i in range(2)]
            sqf = P.sb('sqfin', [128, 8, 512], F32, st)
            onesf = P.sb('onesf', [128, 128], F32, st)
            fg = P.sb('fgf', [128, 8], F32, st)
            rstdf = P.sb('rstdfin', [128, 512], F32, st)
            fbank = (P.ps('fbk', [128, 512], F32, st), 'fbk')
            P.op('pool', lambda e: e.memset(onesf[:], 1.0), writes=['onesf'])
            P.dma('sp', fg[:], T['final_norm_g'], writes=['fgf'])
            xg_n = rr(xgs)
            fo3 = final_out.rearrange("(j p) t -> p j t", p=128)
        load_weights_bf16(P, T[wname], W, rr(wstl), 1024, 'Wo')
        wkeys = [('Wo', k) for k in range(8)]
        P.dma('sp', identf[:], T['ident'], writes=['identf'])
        P.op('pool', lambda e: e.tensor_copy(ident[:], identf[:]), reads=['identf'], writes=['ident'])
        modv = pers['modv']
        mt_n = rr(mts); mT_n = rr(mTs); xt_n = rr(xts); ot_n = rr(ots); tp_n = rr(tps); bk_n = rr(banks)
        xs3 = src_xT.rearrange("(j p) t -> p j t", p=128)
        dq = rr(['sp', 'pool'])

        def do_group(gi):
            t0 = gi * 512
            mT, mTk = mT_n()
            if mix_is_tok:
                mt, mtk = mt_n()
                P.dma('sp', mt[:], S['mixtok'][t0:t0 + 512, :].rearrange("(s p) f -> p s f", p=128), writes=[mtk])
                for s in range(4):
                    tp, tpk = tp_n()
                    for k in range(8):
                        P.op('pe', lambda e, s=s, k=k, tp=tp: e.transpose(tp[:, k * 128:(k + 1) * 128], mt[:, s, k * 128:(k + 1) * 128], ident[:]),
                             reads=[mtk, 'ident'], writes=[tpk])
                    eng = 'act' if s % 2 == 0 else 'dve'
                    if eng == 'act':
                        P.op('act', lambda e, s=s, tp=tp: e.copy(mT[:, :, s * 128:(s + 1) * 128], tp[:].rearrange("p (k t) -> p k t", t=128)), writes=[tpk, (mTk, s)])
                    else:
                        P.op('dve', lambda e, s=s, tp=tp: e.tensor_copy(mT[:, :, s * 128:(s + 1) * 128], tp[:].rearrange("p (k t) -> p k t", t=128)), writes=[tpk, (mTk, s)])
                mkeys = [(mTk, s) for s in range(4)]
            else:
                P.dma('sp', mT[:], S['mixT'].rearrange("(k p) t -> p k t", p=128)[:, :, t0:t0 + 512], writes=[mTk])
                mkeys = [mTk]
            xt, xtk = xt_n()
            P.dma('pool', xt[:], xs3[:, :, t0:t0 + 512], writes=[xtk])
            if final_out is not None:
                xg, xgk = xg_n()
            for mc in range(8):
                bank, bkk = bk_n()
                for k in range(8):
                    P.op('pe', lambda e, k=k, mc=mc, bank=bank: e.matmul(bank[:], W[:, k, mc * 128:(mc + 1) * 128], mT[:, k, :], start=(k == 0), stop=(k == 7)),
                         reads=wkeys + mkeys, writes=[bkk])
                if final_out is None:
                    o, ok = ot_n()
                    P.op('dve', lambda e, mc=mc, bank=bank, o=o: e.scalar_tensor_tensor(o[:], bank[:], modv[:, 16 + mc, 0:1], xt[:, mc, :], ALU.mult, ALU.add),
                         reads=[xtk, ('modv', pre)], writes=[bkk, ok])
                    P.dma(dq(), dst_xT[mc * 128:(mc + 1) * 128, t0:t0 + 512], o[:], reads=[ok], writes=[('dram', 'xn', mc, gi)])
                else:
                    P.op('dve', lambda e, mc=mc, bank=bank: e.scalar_tensor_tensor(xg[:, mc, :], bank[:], modv[:, 16 + mc, 0:1], xt[:, mc, :], ALU.mult, ALU.add),
                         reads=[xtk, ('modv', pre)], writes=[bkk, (xgk, mc)] + ([xgk] if mc == 0 else []))
            if final_out is not None:
                xkeys = [(xgk, mc) for mc in range(8)]
                fb, fbk = fbank
                P.op('act', lambda e: e.activation(sqf[:], xg[:], AF.Square), reads=xkeys, writes=['sqfin'])
                for j in range(8):
                    P.op('pe', lambda e, j=j: e.matmul(fb[:], onesf[:], sqf[:, j, :], start=(j == 0), stop=(j == 7)), reads=['sqfin', 'onesf'], writes=[fbk])
                P.op('dve', lambda e: e.tensor_scalar(rstdf[:], fb[:], 1.0 / 1024, 1e-6, ALU.mult, ALU.add), writes=[fbk, 'rstdfin'])
                P.op('act', lambda e: e.activation(rstdf[:], rstdf[:], AF.Sqrt), writes=['rstdfin'])
                P.op('dve', lambda e: e.reciprocal(rstdf[:], rstdf[:]), writes=['rstdfin'])
                for j in range(8):
                    P.op('dve', lambda e, j=j: e.scalar_tensor_tensor(sqf[:, j, :], xg[:, j, :], fg[:, j:j + 1], rstdf[:], ALU.mult, ALU.mult),
                         reads=xkeys + ['rstdfin', 'fgf'], writes=['sqfin'])
                P.dma(dq(), fo3[:, :, t0:t0 + 512], sqf[:], reads=['sqfin'], writes=[('dram', 'fin', gi), xgk] + xkeys)

        for gi in range(8):
            do_group(gi)
    P.barrier()


def phase_final(P, T, src_xT, dst):
    with ExitStack() as st:
        ones = P.sb('ones', [128, 128], F32, st)
        fg = P.sb('fg', [128, 8], F32, st)
        xts = [(P.sb('xf%d' % i, [128, 8, 512], F32, st), 'xf%d' % i) for i in range(2)]
        sqs = [(P.sb('sf%d' % i, [128, 8, 512], F32, st), 'sf%d' % i) for i in range(2)]
        rstd = P.sb('rstdf', [128, 512], F32, st)
        bank = P.ps('fb', [128, 512], F32, st)
        P.op('pool', lambda e: e.memset(ones[:], 1.0), writes=['ones'])
        P.dma('sp', fg[:], T['final_norm_g'], writes=['fg'])
        xs3 = src_xT.rearrange("(j p) t -> p j t", p=128)
        ds3 = dst.rearrange("(j p) t -> p j t", p=128)
        xt_n = rr(xts); sq_n = rr(sqs)

        def do_tile(tt):
            t0 = tt * 512
            xt, xk = xt_n(); sq, sk = sq_n()
            P.dma('sp', xt[:], xs3[:, :, t0:t0 + 512], writes=[xk])
            P.op('act', lambda e: e.activation(sq[:], xt[:], AF.Square), reads=[xk], writes=[sk])
            for j in range(8):
                P.op('pe', lambda e, j=j: e.matmul(bank[:], ones[:], sq[:, j, :], start=(j == 0), stop=(j == 7)), reads=[sk, 'ones'], writes=['fb'])
            P.op('dve', lambda e: e.tensor_scalar(rstd[:], bank[:], 1.0 / 1024, 1e-6, ALU.mult, ALU.add), writes=['fb', 'rstdf'])
            P.op('act', lambda e: e.activation(rstd[:], rstd[:], AF.Sqrt), writes=['rstdf'])
            P.op('dve', lambda e: e.reciprocal(rstd[:], rstd[:]), writes=['rstdf'])
            for j in range(8):
                P.op('dve', lambda e, j=j: e.scalar_tensor_tensor(sq[:, j, :], xt[:, j, :], fg[:, j:j + 1], rstd[:], ALU.mult, ALU.mult),
                     reads=[xk, 'rstdf', 'fg'], writes=[sk])
            P.dma('pool', ds3[:, :, t0:t0 + 512], sq[:], reads=[sk], writes=[('dram', 'fin', tt)])

        for tt in range(8):
            do_tile(tt)
    P.barrier()


NT = 2176
NMOD = 4096.0


def phase_dft_tables(P, T, S):
    with ExitStack() as st:
        kv = P.sb('kv', [128, NT], F32, st)
        rv = P.sb('rv', [128, 17], F32, st)
        hp = P.sb('hp', [128, 1], F32, st)
        W = NT
        prods = [(P.sb('pr%d' % i, [128, W], F32, st), 'pr%d' % i) for i in range(2)]
        qis = [(P.sb('qi%d' % i, [128, W], I32, st), 'qi%d' % i) for i in range(2)]
        qfs = [(P.sb('qf%d' % i, [128, W], F32, st), 'qf%d' % i) for i in range(2)]
        abss = [(P.sb('ab%d' % i, [128, W], F32, st), 'ab%d' % i) for i in range(2)]
        cts = [(P.sb('ct%d' % i, [128, W], BF16, st), 'ct%d' % i) for i in range(2)]
        sts = [(P.sb('sn%d' % i, [128, W], BF16, st), 'sn%d' % i) for i in range(2)]
        P.dma('sp', kv[:], T['kvec'].partition_broadcast(128), writes=['kv'])
        P.dma('sp', rv[:], T['rvals'], writes=['rv'])
        P.op('pool', lambda e: e.memset(hp[:], math.pi / 2), writes=['hp'])
        pn = rr(prods); qn = rr(qis); fn = rr(qfs); an = rr(abss); cn = rr(cts); sn = rr(sts)
        w0 = 2 * math.pi / NMOD

        def piece(rc, half):
            c0 = half * W
            pr, prk = pn(); qi, qik = qn(); qf, qfk = fn(); ab, abk = an(); ct, ctk = cn(); sn_, snk = sn()
            P.op('dve', lambda e: e.tensor_scalar(pr[:], kv[:, c0:c0 + W], rv[:, rc:rc + 1], None, ALU.mult), reads=['kv', 'rv'], writes=[prk])
            P.op('dve', lambda e: e.tensor_scalar(qi[:], pr[:], 1.0 / NMOD, None, ALU.mult), reads=[prk], writes=[qik])
            P.op('pool', lambda e: e.tensor_copy(qf[:], qi[:]), reads=[qik], writes=[qfk])
            P.op('dve', lambda e: e.scalar_tensor_tensor(pr[:], qf[:], -NMOD, pr[:], ALU.mult, ALU.add), reads=[qfk], writes=[prk])
            P.op('act', lambda e: e.activation(sn_[:], pr[:], AF.Sin, scale=w0), reads=[prk], writes=[snk])
            P.op('act', lambda e: e.activation(ab[:], pr[:], AF.Abs), reads=[prk], writes=[abk])
            P.op('act', lambda e: e.activation(ct[:], ab[:], AF.Sin, bias=hp[:], scale=-w0), reads=[abk, 'hp'], writes=[ctk])
            P.dma('sp', S['Ctab'][rc * 128:(rc + 1) * 128, c0:c0 + W], ct[:], reads=[ctk], writes=[('dram', 'C', rc, half)])
            P.dma('pool', S['Stab'][rc * 128:(rc + 1) * 128, c0:c0 + W], sn_[:], reads=[snk], writes=[('dram', 'S', rc, half)])

        for rc in range(17):
            piece(rc, 0)
    P.barrier()


def dft_consts():
    kvec = np.arange(NT, dtype=np.float32).reshape(1, NT)
    rvals = (np.arange(17)[None, :] * 128 + np.arange(128)[:, None]).astype(np.float32)
    return kvec, rvals


class _View:
    def __init__(self, base, t0):
        self.base, self.t0 = base, t0
    def __getitem__(self, idx):
        p, j, t = idx
        t = slice((t.start or 0) + self.t0, (t.stop if t.stop is not None else 512) + self.t0)
        return self.base[p, j, t]


def phase_l1_proj(P, T, pers, S, src_xT):
    with ExitStack() as st:
        W = P.sb('W1', [128, 8, 4096], BF16, st)
        hT = P.sb('hT1', [128, 8, 4096], BF16, st)
        cw = P.sb('cw1', [128, 72], F32, st)
        P.dma('sp', cw[:], T['o_hy_conv'], writes=['cw1'])
        with ExitStack() as st1:
            wstl = [(P.sb('wst1', [128, 4096], F32, st1), 'wst1')]
            ones = P.sb('ones', [128, 128], F32, st1)
            xts = [(P.sb('xt1_%d' % i, [128, 8, 512], F32, st1), 'xt1_%d' % i) for i in range(2)]
            sq = P.sb('sq1', [128, 8, 512], F32, st1)
            rstd = P.sb('rstd1', [128, 512], F32, st1)
            bank = P.ps('ssb1', [128, 512], F32, st1)
            P.op('pool', lambda e: e.memset(ones[:], 1.0), writes=['ones'])
            load_weights_bf16(P, T['o_w_in'], W, rr(wstl), 4096, 'W1')
            bufs = dict(xt=rr(xts), hT=None, sq=sq, ones=ones, rstd=rstd, ssbank=bank)
            x3 = src_xT.rearrange("(j p) t -> p j t", p=128)
            for tt in range(8):
                adaln_tile(P, x3[:, :, tt * 512:(tt + 1) * 512], 512, 0, pers, 'o', bufs, None, dst=(_View(hT, tt * 512), ('hT1', tt)))
            P.barrier()
        wkeys = []
        with ExitStack() as st2:
            prows = [(P.sb('prow%d' % i, [128, 4098], F32, st2), 'prow%d' % i) for i in range(2)]
            accs = [(P.sb('acc1_%d' % i, [128, 4096], F32, st2), 'acc1_%d' % i) for i in range(2)]
            grows = [(P.sb('grow%d' % i, [128, 4096], BF16, st2), 'grow%d' % i) for i in range(1)]
            banks = [(P.ps('pj%d' % i, [128, 512], F32, st2), 'pj%d' % i) for i in range(6)]
            for (pr, prk) in prows:
                P.op('pool', lambda e, pr=pr: e.memset(pr[:], 0.0), writes=[prk])
            pr_n = rr(prows); acc_n = rr(accs); gr_n = rr(grows); bk_n = rr(banks)
            dq = rr(['sp', 'pool'])

            def do_chunk(cc):
                isconv = cc < 24
                if isconv:
                    pr, prk = pr_n()
                else:
                    gr, grk = gr_n()
                for tt in range(8):
                    bank, bkk = bk_n()
                    for k in range(8):
                        P.op('pe', lambda e, k=k, bank=bank, tt=tt: e.matmul(bank[:], W[:, k, cc * 128:(cc + 1) * 128], hT[:, k, tt * 512:(tt + 1) * 512], start=(k == 0), stop=(k == 7)), writes=[bkk])
                    if isconv:
                        if tt % 2 == 0:
                            P.op('act', lambda e, bank=bank, tt=tt: e.copy(pr[:, 1 + tt * 512:1 + (tt + 1) * 512], bank[:]), writes=[bkk, (prk, tt)])
                        else:
                            P.op('dve', lambda e, bank=bank, tt=tt: e.tensor_copy(pr[:, 1 + tt * 512:1 + (tt + 1) * 512], bank[:]), writes=[bkk, (prk, tt)])
                    else:
                        P.op('act', lambda e, bank=bank, tt=tt: e.activation(gr[:, tt * 512:(tt + 1) * 512], bank[:], AF.Silu), writes=[bkk, (grk, tt)])
                if isconv:
                    acc, ak = acc_n()
                    pk = [(prk, tt) for tt in range(8)]
                    P.op('dve', lambda e: e.tensor_scalar(acc[:], pr[:, 0:4096], cw[:, cc * 3:cc * 3 + 1], None, ALU.mult), reads=['cw1'], writes=[ak, prk] + pk)
                    for j in (1, 2):
                        P.op('dve', lambda e, j=j: e.scalar_tensor_tensor(acc[:], pr[:, j:j + 4096], cw[:, cc * 3 + j:cc * 3 + j + 1], acc[:], ALU.mult, ALU.add), reads=['cw1'], writes=[ak, prk] + pk)
                    P.dma(dq(), S['uT'][cc * 128:(cc + 1) * 128, :], acc[:], reads=[ak], writes=[('dram', 'u', cc)])
                else:
                    g = cc - 24
                    P.dma(dq(), S['gT'][g * 128:(g + 1) * 128, :], gr[:], reads=[(grk, tt) for tt in range(8)], writes=[('dram', 'g', g)])

            for cc in range(32):
                do_chunk(cc)
    P.barrier()


HY_DMIN = math.log(1e-2) / 1.5
HY_DMAX = math.log(1e-2) / 0.3


def hy_consts():
    L = 4096
    t = np.linspace(0.0, 1.0, L, dtype=np.float32)[:, None]
    w = (2.0 * math.pi * np.arange(L, dtype=np.float32)[:, None] / L).astype(np.float32)
    f = np.linspace(1e-4, 15, 16, dtype=np.float32)[None, :]
    feats = np.concatenate([t, np.cos(f * w), -np.sin(f * w)], axis=-1).astype(np.float32)
    perm = np.concatenate([np.arange(0, L, 2), np.arange(1, L, 2)])
    featsT = np.ascontiguousarray(feats[perm].T)
    ntpos = -np.ascontiguousarray(t[:, 0].reshape(32, 128).T)
    decay = np.abs(np.linspace(HY_DMIN, HY_DMAX, 1024, dtype=np.float32)).reshape(1, 1024).astype(np.float32)
    k = np.arange(33 * 128)
    wk = np.where(k > 4096, 0.0, np.where((k == 0) | (k == 4096), 1.0, 2.0)) / 8192.0
    wk = np.ascontiguousarray(wk.reshape(33, 128).T).astype(np.float32)
    E = np.exp(-(t.astype(np.float32)) * decay).astype(np.float32)[perm]
    Etab = np.ascontiguousarray(E.reshape(32, 128, 8, 128).transpose(2, 1, 0, 3))
    kk = np.arange(17 * 128)
    wP = np.where(kk > 2048, 0.0, np.where(kk == 0, 1.0, 2.0)) / 8192.0
    wM = np.where(kk >= 2048, 0.0, np.where(kk == 0, 1.0, 2.0)) / 8192.0
    sl = lambda v: np.ascontiguousarray(v.reshape(17, 128).T)
    wts = np.concatenate([sl(wP), sl(wM), -sl(wP)], axis=1).astype(np.float32)
    return featsT, ntpos.astype(np.float32), decay, wts, Etab


def sin_reduced(P, dst, src_ps, bvec, fvec, bufs, key_ps, key_dst, np_, n):
    arg, argk = bufs['arg']; qi, qik = bufs['qi']; qf, qfk = bufs['qf']
    P.op('dve', lambda e: e.tensor_scalar(arg[:np_, :n], src_ps, bvec, fvec, ALU.add, ALU.mult), writes=[key_ps, argk])
    P.op('dve', lambda e: e.tensor_scalar(qi[:np_, :n], arg[:np_, :n], 1.0 / (2 * math.pi), None, ALU.mult), reads=[argk], writes=[qik])
    P.op('pool', lambda e: e.tensor_copy(qf[:np_, :n], qi[:np_, :n]), reads=[qik], writes=[qfk])
    P.op('dve', lambda e: e.scalar_tensor_tensor(arg[:np_, :n], qf[:np_, :n], -2 * math.pi, arg[:np_, :n], ALU.mult, ALU.add), reads=[qfk], writes=[argk])
    P.op('act', lambda e: e.activation(dst, arg[:np_, :n], AF.Sin), reads=[argk], writes=[key_dst])


def phase_hy_filters(P, T, S):
    with ExitStack() as st:
        fv = P.sb('fv', [64, 4], F32, st)
        P.dma('sp', fv[:], T['o_ffn_vec'], writes=['fv'])
        with ExitStack() as st1:
            featsT = P.sb('featsT', [33, 4096], F32, st1)
            w1 = P.sb('fw1', [33, 64], F32, st1); w2 = P.sb('fw2', [64, 64], F32, st1)
            h1 = P.sb('hid1T', [64, 4096], F32, st1)
            h2 = P.sb('hid2T', [64, 4096], F32, st1)
            arg = (P.sb('arg', [64, 512], F32, st1), 'arg'); qi = (P.sb('qi', [64, 512], I32, st1), 'qi'); qf = (P.sb('qf', [64, 512], F32, st1), 'qf')
            bufs = dict(arg=arg, qi=qi, qf=qf)
            pbs1 = [(P.ps('fh1_%d' % i, [128, 512], F32, st1), 'fh1_%d' % i) for i in range(2)]
            pb1 = rr(pbs1)
            P.dma('sp', featsT[:], T['featsT'], writes=['featsT'])
            P.dma('sp', w1[:], T['o_ffn_w1'], writes=['fw1']); P.dma('sp', w2[:], T['o_ffn_w2'], writes=['fw2'])
            for tt in range(8):
                ps, psk = pb1()
                P.op('pe', lambda e, ps=ps, tt=tt: e.matmul(ps[0:64, :], w1[:], featsT[:, tt * 512:(tt + 1) * 512], start=True, stop=True), reads=['fw1', 'featsT'], writes=[psk])
                sin_reduced(P, h1[:, tt * 512:(tt + 1) * 512], ps[0:64, :], fv[:, 0:1], fv[:, 1:2], bufs, psk, ('h1', tt), 64, 512)
            for tt in range(8):
                ps, psk = pb1()
                P.op('pe', lambda e, ps=ps, tt=tt: e.matmul(ps[0:64, :], w2[:], h1[:, tt * 512:(tt + 1) * 512], start=True, stop=True), reads=['fw2', ('h1', tt)], writes=[psk])
                sin_reduced(P, h2[:, tt * 512:(tt + 1) * 512], ps[0:64, :], fv[:, 2:3], fv[:, 3:4], bufs, psk, ('h2', tt), 64, 512)
            P.dma('sp', S['h2sc'], h2[:], reads=[('h2', tt) for tt in range(8)], writes=[('dram', 'h2sc')])
            P.barrier()
        wk = P.sb('wk', [128, 51], F32, st)
        tw = P.sb('twf', [128, 51], F32, st)
        ones = P.sb('ones', [128, 128], F32, st)
        av = P.sb('av', [128, 32, 1024], BF16, st); dv = P.sb('dv', [128, 32, 1024], BF16, st)
        P.dma('sp', wk[:], T['wk'], writes=['wk'])
        P.dma('sp', tw[:], T['hy_tw'], writes=['twf'])
        P.op('pool', lambda e: e.memset(ones[:], 1.0), writes=['ones'])
        dq = rr(['sp', 'act'])

        def gen_order(o):
            with ExitStack() as sg:
                hb = [(P.sb('hb%d' % i, [128, 32, 128], F32, sg), 'hb%d' % i) for i in range(2)]
                Et = (P.sb('Etab', [128, 32, 128], F32, sg), 'Etab')
                w3 = P.sb('fw3', [64, 2048], F32, sg)
                h2 = P.sb('hid2T', [64, 4096], F32, sg)
                P.dma('sp', h2[:], S['h2sc'], writes=['h2r'])
                P.dma('act', w3[:], T['o_ffn_w3'][:, o * 2048:(o + 1) * 2048], writes=['fw3'])
                prt = [rr([(P.sb('prt%d_%d' % (d_, i), [128, 128], F32, sg), 'prt%d_%d' % (d_, i)) for i in range(1)]) for d_ in range(2)]
                rns = [(P.sb('rnf%d' % i, [128, 128], F32, sg), 'rnf%d' % i) for i in range(2)]
                skp = P.sb('skp', [1, 128], F32, sg)
                pbs = [(P.ps('fh%d' % i, [128, 1024], F32, sg), 'fh%d' % i) for i in range(3)]
                nbs = [(P.ps('fn%d' % i, [128, 512], F32, sg), 'fn%d' % i) for i in range(2)]
                pb_n = rr(pbs)

                def do_dir(cg, dr):
                    hbt, hbk = hb[dr]
                    E, Ek = Et
                    rn, rnk = rns[dr]
                    col0 = dr * 1024 + cg * 128
                    nbank, nk = nbs[dr]
                    for q in range(4):
                        ps, psk = pb_n(); pr, prk = prt[dr]()
                        for jj in range(8):
                            j = q * 8 + jj
                            P.op('pe', lambda e, ps=ps, j=j, jj=jj: e.matmul(ps[:, jj * 128:(jj + 1) * 128], h2[:, j * 128:(j + 1) * 128], w3[:, col0:col0 + 128], start=True, stop=True), reads=['fw3', 'h2r'], writes=[psk])
                        js = slice(q * 8, (q + 1) * 8)
                        P.op('dve', lambda e, ps=ps, js=js: e.tensor_tensor(hbt[:, js, :], ps[:].rearrange("p (j c) -> p j c", c=128), E[:, js, :], ALU.mult), reads=[Ek], writes=[psk, (hbk, q)] + ([hbk] if q == 0 else []))
                        yield
                        P.op('act', lambda e, ps=ps, js=js: e.activation(ps[:].rearrange("p (j c) -> p j c", c=128), hbt[:, js, :], AF.Square), reads=[(hbk, q)], writes=[psk])
                        P.op('dve', lambda e, ps=ps, pr=pr: e.tensor_reduce(pr[:], ps[:].rearrange("p (j c) -> p c j", c=128), AX.X, ALU.add), writes=[psk, prk])
                        P.op('pe', lambda e, pr=pr, q=q: e.matmul(nbank[:, 0:128], ones[:], pr[:], start=(q == 0), stop=(q == 3)), reads=[prk, 'ones'], writes=[nk])
                        yield
                    P.op('dve', lambda e: e.tensor_scalar_add(rn[:], nbank[:, 0:128], 1e-6), writes=[nk, rnk])
                    P.op('act', lambda e: e.activation(rn[:], rn[:], AF.Sqrt), writes=[rnk])
                    P.op('dve', lambda e: e.reciprocal(rn[:], rn[:]), writes=[rnk])
                    yield
                    hk_all = [(hbk, q) for q in range(4)]
                    P.op('pool', lambda e: e.tensor_tensor(hbt[:], hbt[:], rn[:].unsqueeze(1).to_broadcast([128, 32, 128]), ALU.mult), reads=[rnk], writes=[hbk] + hk_all)

                def do_cg(cg):
                    P.dma('act', Et[0][:], T['Etab'][cg], writes=[Et[1]])
                    P.dma('sp', skp[:], T['o_hy_skip'][0:1, o * 1024 + cg * 128:o * 1024 + (cg + 1) * 128], writes=['skp'])
                    gens = [do_dir(cg, 0), do_dir(cg, 1)]
                    while gens:
                        for g_ in list(gens):
                            try:
                                next(g_)
                            except StopIteration:
                                gens.remove(g_)
                    (h0, h0k), (h1_, h1k) = hb
                    cs = slice(cg * 128, (cg + 1) * 128)
                    t0r = rns[0][0][0:1, :]; t0k = rns[0][1]
                    P.op('dve', lambda e: e.tensor_tensor(av[:, :, cs], h0[:], h1_[:], ALU.add), reads=[h0k, h1k], writes=[('av', cg)])
                    P.op('pool', lambda e: e.tensor_tensor(dv[:, :, cs], h0[:], h1_[:], ALU.subtract), reads=[h0k, h1k], writes=[('dv', cg)])
                    P.op('dve', lambda e: e.tensor_tensor(t0r, h0[0:1, 0, :], h1_[0:1, 0, :], ALU.add), reads=[h0k, h1k], writes=[t0k])
                    P.op('dve', lambda e: e.tensor_tensor(t0r, t0r, skp[0:1, :], ALU.add), reads=['skp'], writes=[t0k])
                    P.op('dve', lambda e: e.tensor_copy(av[0:1, 0, cs], t0r), reads=[t0k], writes=[('av', cg)])
                    P.op('dve', lambda e: e.tensor_copy(dv[0:1, 0, cs], t0r), reads=[t0k], writes=[('dv', cg)])

                for cg in range(8):
                    do_cg(cg)
                P.barrier()

        def xform_order(o):
            with ExitStack() as sx:
                cts = [(P.sb('ctf%d' % i, [128, 16, 128], BF16, sx), 'ctf%d' % i) for i in range(2)]
                sts = [(P.sb('stf%d' % i, [128, 16, 128], BF16, sx), 'stf%d' % i) for i in range(2)]
                hos = [(P.sb('ho%d' % i, [128, 4, 1024], F32, sx), 'ho%d' % i) for i in range(2)]
                tmps = [[(P.sb('xt%d_%d' % (q, i), [128, 512], F32, sx), 'xt%d_%d' % (q, i)) for i in range(5)] for q in range(2)]
                cbs = [(P.ps('fc%d' % i, [128, 512], F32, sx), 'fc%d' % i) for i in range(8)]
                cb_n = rr(cbs); ct_n = rr(cts); st_n = rr(sts); ho_n = rr(hos); tm_n = rr(tmps)

                def do_kc(kc):
                    ct, ctk = ct_n(); stt, stk = st_n(); ho, hok = ho_n()
                    P.dma('sp', ct[:], S['Ctab'][0:2048, kc * 128:(kc + 1) * 128].rearrange("(j p) q -> p j q", p=128), writes=[ctk])
                    P.dma('act', stt[:], S['Stab'][0:2048, kc * 128:(kc + 1) * 128].rearrange("(j p) q -> p j q", p=128), writes=[stk])
                    ck = tw[:, kc:kc + 1]; sk = tw[:, 17 + kc:18 + kc]; nsk = tw[:, 34 + kc:35 + kc]
                    wP = wk[:, kc:kc + 1]; wM = wk[:, 17 + kc:18 + kc]; nwP = wk[:, 34 + kc:35 + kc]

                    def half(ch, src, is_a):
                        cs = slice(ch * 512, (ch + 1) * 512)
                        (b0, kb0), (b1, kb1), (b2, kb2) = cb_n(), cb_n(), cb_n()
                        plan = ((b0, kb0, ct, ctk, 0), (b1, kb1, ct, ctk, 16), (b2, kb2, stt, stk, 16)) if is_a else ((b0, kb0, stt, stk, 0), (b1, kb1, stt, stk, 16), (b2, kb2, ct, ctk, 16))
                        for (bank, bkey, tab, tkey, joff) in plan:
                            for j in range(16):
                                P.op('pe', lambda e, j=j, bank=bank, tab=tab, joff=joff: e.matmul(bank[:], tab[:, j, :], src[:, joff + j, cs], start=(j == 0), stop=(j == 15)), reads=[tkey], writes=[bkey])
                        (u, uk), (tt_, ttk), (ev, evk), (pp_, ppk), (mm_, mmk) = tm_n()
                        P.op('act', lambda e: e.activation(u[:], b1[:], AF.Copy, scale=ck), reads=['twf'], writes=[kb1, uk])
                        P.op('dve', lambda e: e.scalar_tensor_tensor(tt_[:], b2[:], (nsk if is_a else sk), u[:], ALU.mult, ALU.add), reads=['twf', uk], writes=[kb2, ttk])
                        P.op('act', lambda e: e.copy(ev[:], b0[:]), writes=[kb0, evk])
                        P.op('pool', lambda e: e.tensor_tensor(pp_[:], ev[:], tt_[:], ALU.add), reads=[evk, ttk], writes=[ppk])
                        P.op('dve', lambda e: e.tensor_tensor(mm_[:], ev[:], tt_[:], ALU.subtract), reads=[evk, ttk], writes=[mmk])
                        fP, fM = (0, 2) if is_a else (1, 3)
                        P.op('act', lambda e: e.activation(ho[:, fP, cs], pp_[:], AF.Copy, scale=(wP if is_a else nwP)), reads=['wk', ppk], writes=[(hok, fP, ch)])
                        P.op('act', lambda e: e.activation(ho[:, fM, cs], mm_[:], AF.Copy, scale=wM), reads=['wk', mmk], writes=[(hok, fM, ch)])
                    for ch in range(2):
                        half(ch, av, True)
                        half(ch, dv, False)
                    allk = [(hok, f, ch) for f in range(4) for ch in range(2)]
                    P.dma('sp', S['Hsc'][o, :, kc * 128:(kc + 1) * 128, :].rearrange("f p c -> p f c"), ho[:], reads=allk, writes=[('dram', 'H', o, kc)] + allk)

                for kc in range(17):
                    do_kc(kc)
                P.barrier()

        for o in range(int(S.get('hy_orders', 2))):
            gen_order(o)
            if not S.get('skip_xform'):
                xform_order(o)
    P.barrier()


NKC = 17


def hy_twiddles():
    k = np.arange(NKC * 128, dtype=np.float64)
    th = 2 * np.pi * k / 8192.0
    ck = np.cos(th).reshape(NKC, 128).T; sk = np.sin(th).reshape(NKC, 128).T
    return np.ascontiguousarray(np.concatenate([ck, sk, -sk], axis=1)).astype(np.float32)


def phase_hy_conv(P, T, S, o, srcT, mul1T, gateT, dstT, dst_bf16):
    with ExitStack() as st:
        xtok = P.sb('xtok', [128, 32, 1024], BF16, st)
        with ExitStack() as st1:
            identf = P.sb('identf', [128, 128], F32, st1)
            ident = P.sb('identb', [128, 128], BF16, st1)
            srcs = [(P.sb('src%d' % i, [128, 8, 512], F32, st1), 'src%d' % i) for i in range(2)]
            sbfs = [(P.sb('sbf%d' % i, [128, 8, 512], BF16, st1), 'sbf%d' % i) for i in range(2)]
            tps = [(P.ps('tpx%d' % i, [128, 1024], BF16, st1), 'tpx%d' % i) for i in range(4)]
            P.dma('sp', identf[:], T['ident'], writes=['identf'])
            P.op('pool', lambda e: e.tensor_copy(ident[:], identf[:]), reads=['identf'], writes=['identb'])
            src_n = rr(srcs); sbf_n = rr(sbfs); tp_n = rr(tps)
            s3 = srcT.rearrange("(f p) t -> p f t", p=128)

            def load_tile(tt):
                sr, srk = src_n(); sb_, sbk = sbf_n()
                P.dma('sp' if tt % 2 == 0 else 'act', sr[:], s3[:, :, tt * 512:(tt + 1) * 512], writes=[srk])
                P.op('act', lambda e: e.copy(sb_[:, 0:4, :], sr[:, 0:4, :]), reads=[srk], writes=[(sbk, 0)])
                P.op('dve', lambda e: e.tensor_copy(sb_[:, 4:8, :], sr[:, 4:8, :]), reads=[srk], writes=[(sbk, 1)])
                cnt = 0
                for s2 in range(2):
                    for r in range(2):
                        tp, tpk = tp_n()
                        j = tt * 2 + s2 + 16 * r
                        a0 = s2 * 256 + r
                        for f in range(8):
                            P.op('pe', lambda e, f=f, a0=a0, s2=s2, tp=tp: e.transpose(tp[:, f * 128:(f + 1) * 128], sb_[:, f, a0:s2 * 256 + 256:2], ident[:]), reads=[(sbk, 0), (sbk, 1), 'identb'], writes=[tpk])
                        if cnt % 2 == 0:
                            P.op('act', lambda e, tp=tp, j=j: e.copy(xtok[:, j, :], tp[:]), writes=[tpk, ('xtok', j)])
                        else:
                            P.op('dve', lambda e, tp=tp, j=j: e.tensor_copy(xtok[:, j, :], tp[:]), writes=[tpk, ('xtok', j)])
                        cnt += 1
            for tt in range(8):
                load_tile(tt)
            P.barrier()
        tw = P.sb('tw', [128, 51], F32, st)
        P.dma('sp', tw[:], T['hy_tw'], writes=['tw'])
        cts = [(P.sb('ctc%d' % i, [128, 16, 128], BF16, st), 'ctc%d' % i) for i in range(2)]
        sts = [(P.sb('stc%d' % i, [128, 16, 128], BF16, st), 'stc%d' % i) for i in range(2)]
        hts = [(P.sb('ht%d' % i, [128, 4, 1024], F32, st), 'ht%d' % i) for i in range(2)]
        sets = [[(P.sb('sl%d_%d' % (q, i), [128, 512], F32, st), 'sl%d_%d' % (q, i)) for i in range(10)] for q in range(2)]
        outs = [(P.sb('yo%d' % i, [128, 4, 1024], BF16, st), 'yo%d' % i) for i in range(2)]
        banks = [(P.ps('cb%d' % i, [128, 512], F32, st), 'cb%d' % i) for i in range(8)]
        ct_n = rr(cts); st_n = rr(sts); ht_n = rr(hts); set_n = rr(sets); out_n = rr(outs); bk_n = rr(banks)

        def fwd_kc(kc):
            ct, ctk = ct_n(); stt, stk = st_n(); ht, htk = ht_n(); yo, yok = out_n()
            P.dma('sp', ct[:], S['Ctab'][0:2048, kc * 128:(kc + 1) * 128].rearrange("(j p) q -> p j q", p=128), writes=[ctk])
            P.dma('act', stt[:], S['Stab'][0:2048, kc * 128:(kc + 1) * 128].rearrange("(j p) q -> p j q", p=128), writes=[stk])
            P.dma('sp', ht[:], S['Hsc'][o, :, kc * 128:(kc + 1) * 128, :].rearrange("f p c -> p f c"), writes=[htk])
            ck = tw[:, kc:kc + 1]; sk = tw[:, 17 + kc:18 + kc]; nsk = tw[:, 34 + kc:35 + kc]

            def do_ch(ch):
                cs = slice(ch * 512, (ch + 1) * 512)
                (bEc, kEc), (bEs, kEs), (bOc, kOc), (bOs, kOs) = bk_n(), bk_n(), bk_n(), bk_n()
                for (bank, bkey, tab, tkey, joff) in ((bEc, kEc, ct, ctk, 0), (bEs, kEs, stt, stk, 0), (bOc, kOc, ct, ctk, 16), (bOs, kOs, stt, stk, 16)):
                    for j in range(16):
                        P.op('pe', lambda e, j=j, bank=bank, tab=tab, joff=joff: e.matmul(bank[:], tab[:, j, :], xtok[:, joff + j, cs], start=(j == 0), stop=(j == 15)), reads=[tkey], writes=[bkey])
                sl = set_n()
                (s0, k0), (s1, k1), (s2, k2), (s3_, k3), (s4, k4), (s5, k5), (s6, k6), (s7, k7), (s8, k8), (s9, k9) = sl
                A = P.op
                A('act', lambda e: e.copy(s0[:], bEc[:]), writes=[kEc, k0])
                A('act', lambda e: e.copy(s1[:], bEs[:]), writes=[kEs, k1])
                A('act', lambda e: e.activation(s2[:], bOc[:], AF.Copy, scale=ck), reads=['tw'], writes=[kOc, k2])
                A('act', lambda e: e.activation(s3_[:], bOs[:], AF.Copy, scale=ck), reads=['tw'], writes=[kOs, k3])
                A('dve', lambda e: e.scalar_tensor_tensor(s4[:], bOs[:], nsk, s2[:], ALU.mult, ALU.add), reads=['tw', k2], writes=[kOs, k4])
                A('dve', lambda e: e.scalar_tensor_tensor(s5[:], bOc[:], sk, s3_[:], ALU.mult, ALU.add), reads=['tw', k3], writes=[kOc, k5])
                yield
                A('pool', lambda e: e.tensor_tensor(s6[:], s0[:], s4[:], ALU.add), reads=[k0, k4], writes=[k6])
                A('dve', lambda e: e.tensor_tensor(s7[:], s0[:], s4[:], ALU.subtract), reads=[k0, k4], writes=[k7])
                A('pool', lambda e: e.tensor_tensor(s8[:], s1[:], s5[:], ALU.add), reads=[k1, k5], writes=[k8])
                A('dve', lambda e: e.tensor_tensor(s9[:], s5[:], s1[:], ALU.subtract), reads=[k1, k5], writes=[k9])
                hrP = ht[:, 0, cs]; hiP = ht[:, 1, cs]; hrM = ht[:, 2, cs]; hiM = ht[:, 3, cs]
                A('pool', lambda e: e.tensor_tensor(s0[:], s6[:], hrP, ALU.mult), reads=[k6, htk], writes=[k0])
                A('dve', lambda e: e.tensor_tensor(s1[:], s8[:], hiP, ALU.mult), reads=[k8, htk], writes=[k1])
                A('pool', lambda e: e.tensor_tensor(s2[:], s0[:], s1[:], ALU.add), reads=[k0, k1], writes=[k2])
                A('dve', lambda e: e.tensor_tensor(s3_[:], s6[:], hiP, ALU.mult), reads=[k6, htk], writes=[k3])
                A('pool', lambda e: e.tensor_tensor(s4[:], s8[:], hrP, ALU.mult), reads=[k8, htk], writes=[k4])
                A('dve', lambda e: e.tensor_tensor(s5[:], s4[:], s3_[:], ALU.subtract), reads=[k3, k4], writes=[k5])
                A('pool', lambda e: e.tensor_tensor(s0[:], s7[:], hrM, ALU.mult), reads=[k7, htk], writes=[k0])
                A('dve', lambda e: e.tensor_tensor(s1[:], s9[:], hiM, ALU.mult), reads=[k9, htk], writes=[k1])
                A('pool', lambda e: e.tensor_tensor(s6[:], s0[:], s1[:], ALU.add), reads=[k0, k1], writes=[k6])
                A('dve', lambda e: e.tensor_tensor(s3_[:], s7[:], hiM, ALU.mult), reads=[k7, htk], writes=[k3])
                A('pool', lambda e: e.tensor_tensor(s4[:], s9[:], hrM, ALU.mult), reads=[k9, htk], writes=[k4])
                A('dve', lambda e: e.tensor_tensor(s8[:], s4[:], s3_[:], ALU.subtract), reads=[k3, k4], writes=[k8])
                A('pool', lambda e: e.tensor_tensor(yo[:, 0, cs], s2[:], s6[:], ALU.add), reads=[k2, k6], writes=[(yok, ch, 0)])
                A('dve', lambda e: e.tensor_tensor(yo[:, 1, cs], s5[:], s8[:], ALU.subtract), reads=[k5, k8], writes=[(yok, ch, 1)])
                A('pool', lambda e: e.tensor_tensor(s7[:], s2[:], s6[:], ALU.subtract), reads=[k2, k6], writes=[k7])
                A('dve', lambda e: e.tensor_tensor(s9[:], s5[:], s8[:], ALU.add), reads=[k5, k8], writes=[k9])
                A('act', lambda e: e.activation(s0[:], s9[:], AF.Copy, scale=sk), reads=['tw', k9], writes=[k0])
                A('act', lambda e: e.activation(s1[:], s7[:], AF.Copy, scale=nsk), reads=['tw', k7], writes=[k1])
                A('dve', lambda e: e.scalar_tensor_tensor(yo[:, 2, cs], s7[:], ck, s0[:], ALU.mult, ALU.add), reads=['tw', k7, k0], writes=[(yok, ch, 2)])
                A('dve', lambda e: e.scalar_tensor_tensor(yo[:, 3, cs], s9[:], ck, s1[:], ALU.mult, ALU.add), reads=['tw', k9, k1], writes=[(yok, ch, 3)])
            def store():
                allk = [(yok, ch, q) for ch in range(2) for q in range(4)]
                P.dma('act', S['Ysc'][:, kc * 128:(kc + 1) * 128, :].rearrange("f p c -> p f c"), yo[:], reads=allk, writes=[('dram', 'Y', kc)] + allk)
            return [(do_ch(0), None), (do_ch(1), store)]

        pend = None
        for kc in S.get('fwd_list', range(NKC)):
            for (g_, fin) in fwd_kc(kc):
                next(g_)
                if pend is not None:
                    for _ in pend[0]:
                        pass
                    if pend[1] is not None:
                        pend[1]()
                pend = (g_, fin)
        if pend is not None:
            for _ in pend[0]:
                pass
            if pend[1] is not None:
                pend[1]()
    P.barrier()
    with ExitStack() as st:
        Yc = P.sb('Yc', [128, 4, NKC, 512], BF16, st)
        cts = [(P.sb('cti%d' % i, [128, NKC, 256], BF16, st), 'cti%d' % i) for i in range(2)]
        sts = [(P.sb('sti%d' % i, [128, NKC, 256], BF16, st), 'sti%d' % i) for i in range(2)]
        m1s = [(P.sb('m1_%d' % i, [128, 4, 512], F32, st), 'm1_%d' % i) for i in range(2)]
        gts = [(P.sb('gt_%d' % i, [128, 4, 512], BF16, st), 'gt_%d' % i) for i in range(2)]
        ofs = [(P.sb('of_%d' % i, [128, 4, 512], F32, st), 'of_%d' % i) for i in range(2)]
        obs = [(P.sb('ob_%d' % i, [128, 4, 512], BF16, st), 'ob_%d' % i) for i in range(2)]
        banks = [(P.ps('ib%d' % i, [128, 512], F32, st), 'ib%d' % i) for i in range(6)]
        ct_n = rr(cts); st_n = rr(sts); m1_n = rr(m1s); gt_n = rr(gts); of_n = rr(ofs); ob_n = rr(obs); bk_n = rr(banks)
        m13 = mul1T.rearrange("(f p) t -> p f t", p=128)
        d3 = dstT.rearrange("(f p) t -> p f t", p=128)
        g3 = gateT.rearrange("(f p) t -> p f t", p=128) if gateT is not None else None

        def inv_half(chh):
            c0 = chh * 512
            for q in range(4):
                P.dma('sp' if q % 2 == 0 else 'act', Yc[:, q, :, :], S['Ysc'][q, :, c0:c0 + 512].rearrange("(k p) c -> p k c", p=128), writes=[('Yc', q)])
            ykeys = [('Yc', q) for q in range(4)]

            def inv_nt(nt):
                n0 = nt * 512; m0 = nt * 256
                ct, ctk = ct_n(); stt, stk = st_n(); m1, m1k = m1_n(); of, ofk = of_n()
                P.dma('sp', ct[:], S['Ctab'][:, m0:m0 + 256].rearrange("(k p) n -> p k n", p=128), writes=[ctk])
                P.dma('act', stt[:], S['Stab'][:, m0:m0 + 256].rearrange("(k p) n -> p k n", p=128), writes=[stk])
                P.dma('sp', m1[:], m13[:, chh * 4:(chh + 1) * 4, n0:n0 + 512], writes=[m1k])
                if g3 is not None:
                    gt, gtk = gt_n(); ob, obk = ob_n()
                    P.dma('act', gt[:], g3[:, chh * 4:(chh + 1) * 4, n0:n0 + 512], writes=[gtk])
                okeys = []
                for r in range(2):
                    for cc in range(4):
                        bank, bkk = bk_n()
                        for kc in range(NKC):
                            P.op('pe', lambda e, kc=kc, cc=cc, bank=bank, r=r: e.matmul(bank[:, 0:256], Yc[:, 2 * r, kc, cc * 128:(cc + 1) * 128], ct[:, kc, :], start=(kc == 0), stop=False), reads=ykeys + [ctk], writes=[bkk])
                        for kc in range(NKC):
                            P.op('pe', lambda e, kc=kc, cc=cc, bank=bank, r=r: e.matmul(bank[:, 0:256], Yc[:, 2 * r + 1, kc, cc * 128:(cc + 1) * 128], stt[:, kc, :], start=False, stop=(kc == NKC - 1)), reads=ykeys + [stk], writes=[bkk])
                        P.op('dve', lambda e, cc=cc, bank=bank, r=r: e.tensor_tensor(of[:, cc, r:512:2], bank[:, 0:256], m1[:, cc, r:512:2], ALU.mult), reads=[m1k], writes=[bkk, (ofk, cc, r)])
                        okeys.append((ofk, cc, r))
                if g3 is not None:
                    P.op('pool', lambda e: e.tensor_tensor(ob[:], of[:], gt[:], ALU.mult), reads=okeys + [gtk], writes=[obk])
                    P.dma('sp', d3[:, chh * 4:(chh + 1) * 4, n0:n0 + 512], ob[:], reads=[obk], writes=[('dram', 'cv', chh, nt), ofk] + okeys)
                else:
                    P.dma('sp', d3[:, chh * 4:(chh + 1) * 4, n0:n0 + 512], of[:], reads=okeys, writes=[('dram', 'cv', chh, nt), ofk] + okeys)
            for nt in range(8):
                inv_nt(nt)

        for chh in range(int(S.get('n_inv', 2))):
            inv_half(chh)
    P.barrier()


def build_program():
    nc = bass.Bass("TRN2", target_bir_lowering=False)

    def din(name, shape, dt=F32):
        return nc.dram_tensor(name, list(shape), dt, kind="ExternalInput").ap()

    def dint(name, shape, dt=F32):
        return nc.dram_tensor(name, list(shape), dt, kind="Internal").ap()

    T = dict(cc=din('cc', [128, 16]), xT=din('xT', [1024, 4096]), ctxT=din('ctxT', [1024, 256]),
             e_mod_w=din('e_mod_w', [1024, 3072]), e_mod_b=din('e_mod_b', [128, 24]), e_norm_g=din('e_norm_g', [128, 8]),
             e_w_in=din('e_w_in', [1024, 4112]), e_w_out=din('e_w_out', [1024, 1024]),
             na_bias=din('na_bias', [5, 128, 5120]), ident=din('ident', [128, 128]),
             e_dn_conv=din('e_dn_conv', [128, 60]), e_dn_a_log=din('e_dn_a_log', [1, 8]), e_dn_dt_bias=din('e_dn_dt_bias', [1, 8]),
             e_dn_norm_g=din('e_dn_norm_g', [1, 128]), dn_masks=din('dn_masks', [8, 128, 128]),
             o_mod_w=din('o_mod_w', [1024, 3072]), o_mod_b=din('o_mod_b', [128, 24]), o_norm_g=din('o_norm_g', [128, 8]),
             o_w_in=din('o_w_in', [1024, 4096]), o_hy_conv=din('o_hy_conv', [128, 72]), o_w_out=din('o_w_out', [1024, 1024]),
             kvec=din('kvec', [1, 2176]), rvals=din('rvals', [128, 17]), hy_tw=din('hy_tw', [128, 51]), featsT=din('featsT', [33, 4096]), ntpos=din('ntpos', [128, 32]),
             decay=din('decay', [1, 1024]), wk=din('wk', [128, 51]), Etab=din('Etab', [8, 128, 32, 128]),
             o_ffn_w1=din('o_ffn_w1', [33, 64]), o_ffn_w2=din('o_ffn_w2', [64, 64]), o_ffn_vec=din('o_ffn_vec', [64, 4]),
             o_ffn_w3=din('o_ffn_w3', [64, 4096]), o_hy_skip=din('o_hy_skip', [1, 2048]),
             final_norm_g=din('final_norm_g', [128, 8]))
    S = dict(qT=dint('qT', [512, 4096], BF16), kT=dint('kT', [512, 4352], BF16), vtok=dint('vtok', [4352, 512], BF16),
             dnT=dint('dnT', [1536, 4096]), dncT=dint('dncT', [1536, 256]), abtok=dint('abtok', [4352, 16]),
             ztok=dint('ztok', [4096, 1024], BF16), mixtok=dint('mixtok', [4096, 1024], BF16),
             dQT=dint('dQT', [4, 128, 4352], BF16), dKT=dint('dKT', [4, 128, 4352], BF16),
             dKtok=dint('dKtok', [4352, 4, 128], BF16), dVtok=dint('dVtok', [4352, 4, 128], BF16),
             dno=dint('dno', [2, 4096, 512]),
             x1T=dint('x1T', [1024, 4096]),
             Ctab=dint('Ctab', [2176, 2176], BF16), Stab=dint('Stab', [2176, 2176], BF16),
             Hsc=dint('Hsc', [2, 4, 2176, 1024]), h2sc=dint('h2sc', [64, 4096]), Ysc=dint('Ysc', [4, 2176, 1024], BF16),
             uT=dint('uT', [3072, 4096]), gT=dint('gT', [1024, 4096], BF16), zT=dint('zT', [1024, 4096]),
             mixT=dint('mixT', [1024, 4096], BF16), x2T=dint('x2T', [1024, 4096]))
    outT = nc.dram_tensor('outT', [1024, 4096], F32, kind="ExternalOutput").ap()
    P = Prog(nc)
    pers = dict(modv=P.sb('modv', [128, 24, 2], F32), gs=P.sb('gs', [128, 8, 2], F32))
    phase_mod(P, T, 'e', pers)
    phase_l0_proj(P, T, pers, S)
    phase_na(P, T, S)
    with ExitStack() as dn_stack:
        pers['g'] = P.sb('g', [128, 34, 8], F32, dn_stack)
        pers['beta'] = P.sb('beta', [128, 34, 8], F32, dn_stack)
        phase_dn_gb(P, T, S, pers)
        phase_dn_prep(P, T, S)
        phase_dn_main(P, T, S, pers)
    phase_dn_out(P, T, S)
    phase_out(P, T, S, pers, 'e', 'e_w_out', True, T['xT'], S['x1T'])
    phase_dft_tables(P, T, S)
    phase_mod(P, T, 'o', pers)
    phase_l1_proj(P, T, pers, S, S['x1T'])
    phase_hy_filters(P, T, S)
    phase_hy_conv(P, T, S, 0, S['uT'][0:1024, :], S['uT'][1024:2048, :], None, S['zT'], False)
    phase_hy_conv(P, T, S, 1, S['zT'], S['uT'][2048:3072, :], S['gT'], S['mixT'], True)
    phase_out(P, T, S, pers, 'o', 'o_w_out', False, S['x1T'], S['x2T'], final_out=outT)
    P.finalize()
    return nc


def _pl(v, k):
    return np.ascontiguousarray(np.asarray(v, np.float32).reshape(k, 128).T)


def kernel(**inp):
    inp = {k: np.asarray(v) for k, v in inp.items()}
    nc = build_program()
    f32 = np.float32
    tab = na_bias_table(inp['e_na_rpb'][0]).reshape(5, 128, 5120)
    ident = np.eye(128, dtype=f32)
    kvec, rvals = dft_consts()
    featsT, ntpos, decay, wk, Etab = hy_consts()
    dncw = np.ascontiguousarray(inp['e_dn_conv'][0].T.reshape(12, 128, 5).transpose(1, 0, 2).reshape(128, 60)).astype(f32)
    hycw = np.ascontiguousarray(inp['o_hy_conv'][0].T.reshape(24, 128, 3).transpose(1, 0, 2).reshape(128, 72)).astype(f32)
    fvec = np.stack([inp['o_ffn_b1'][0], inp['o_ffn_f1'][0], inp['o_ffn_b2'][0], inp['o_ffn_f2'][0]], axis=1).astype(f32)
    shared = dict(
        e_mod_w=inp['e_mod_w'][0], e_mod_b=_pl(inp['e_mod_b'][0], 24), e_norm_g=_pl(inp['e_norm_g'][0], 8),
        e_w_in=inp['e_w_in'][0], e_w_out=inp['e_w_out'][0], na_bias=tab, ident=ident,
        e_dn_conv=dncw, e_dn_a_log=inp['e_dn_a_log'][0].reshape(1, 8).astype(f32), e_dn_dt_bias=inp['e_dn_dt_bias'][0].reshape(1, 8).astype(f32),
        e_dn_norm_g=inp['e_dn_norm_g'][0].reshape(1, 128).astype(f32), dn_masks=dn_masks(),
        o_mod_w=inp['o_mod_w'][0], o_mod_b=_pl(inp['o_mod_b'][0], 24), o_norm_g=_pl(inp['o_norm_g'][0], 8),
        o_w_in=inp['o_w_in'][0], o_hy_conv=hycw, o_w_out=inp['o_w_out'][0],
        kvec=kvec, rvals=rvals, hy_tw=hy_twiddles(), featsT=featsT, ntpos=ntpos, decay=decay, wk=wk, Etab=Etab,
        o_ffn_w1=inp['o_ffn_w1'][0], o_ffn_w2=inp['o_ffn_w2'][0], o_ffn_vec=fvec, o_ffn_w3=inp['o_ffn_w3'][0],
        o_hy_skip=inp['o_hy_skip'][0].reshape(1, 2048).astype(f32), final_norm_g=_pl(inp['final_norm_g'], 8))
    shared = {k: np.ascontiguousarray(v, dtype=f32) for k, v in shared.items()}
    in_maps = []
    for b in range(8):
        cc = np.zeros((128, 16), f32)
        cc[:, 0::2] = _pl(inp['c'][b], 8)
        cc[:, 1::2] = _pl(inp['c_ctx'], 8)
        m = dict(shared)
        m.update(cc=cc, xT=np.ascontiguousarray(inp['x'][b].T, dtype=f32), ctxT=np.ascontiguousarray(inp['ctx'][b].T, dtype=f32))
        in_maps.append(m)
    res = run_bass_kernel_spmd(nc, in_maps, core_ids=list(range(8)))
    out = np.stack([np.ascontiguousarray(np.asarray(r['outT']).T) for r in res.results], axis=0)
    return out.astype(np.float32)
```

```python
import math
import numpy as np
import ml_dtypes
from contextlib import ExitStack
import concourse.bass as bass
import concourse.mybir as mybir
from concourse.bass_utils import run_bass_kernel_spmd

F32 = mybir.dt.float32
BF16 = mybir.dt.bfloat16
I32 = mybir.dt.int32
AF = mybir.ActivationFunctionType
ALU = mybir.AluOpType
AX = mybir.AxisListType

ENGS = ('pe', 'act', 'dve', 'pool', 'sp')
NDMA = {'sp': 8, 'act': 8}


class Prog:
    def __init__(self, nc, same_eng_sync=('pool', 'act', 'dve')):
        self.nc = nc
        self.ops = {e: [] for e in ENGS}
        self.ccount = {e: 0 for e in ENGS}
        self.dcount = {e: 0 for e in ENGS}
        self.last_w = {}
        self.readers = {}
        self.waited = {e: {} for e in ENGS}
        self.pending = {e: set() for e in ENGS}
        self.same = same_eng_sync
        self.stack = ExitStack()

    def _nm(self, name):
        self._nmc = getattr(self, '_nmc', 0) + 1
        return '%s_%d' % (name, self._nmc)

    def sb(self, name, shape, dt, stack=None):
        return (stack or self.stack).enter_context(self.nc.sbuf_tensor(self._nm('s_' + name), list(shape), dt))

    def ps(self, name, shape, dt=F32, stack=None):
        return (stack or self.stack).enter_context(self.nc.psum_tensor(self._nm('p_' + name), list(shape), dt))

    def _deps(self, reads, writes):
        deps = set()
        for k in reads:
            t = self.last_w.get(k)
            if t is not None:
                deps.add(t)
        for k in writes:
            t = self.last_w.get(k)
            if t is not None:
                deps.add(t)
            for r in self.readers.get(k, ()):
                deps.add(r)
        return deps

    def _record(self, eng, fn, deps, tok, inc, reads, writes, extra_waits=()):
        deps = set(deps) | self.pending[eng]
        self.pending[eng] = set()
        waits = list(extra_waits)
        w = self.waited[eng]
        for (sk, val) in sorted(deps, key=lambda t: (str(t[0]), t[1])):
            if sk == eng and (eng == 'pe' or eng not in self.same):
                continue
            if w.get(sk, 0) >= val:
                continue
            w[sk] = val
            waits.append((sk, val))
        self.ops[eng].append((waits, fn, tok[0], inc))
        for k in reads:
            self.readers.setdefault(k, []).append(tok)
        for k in writes:
            self.last_w[k] = tok
            self.readers[k] = []

    def op(self, eng, fn, reads=(), writes=()):
        deps = self._deps(reads, writes)
        self.ccount[eng] += 1
        tok = (eng, self.ccount[eng])
        self._record(eng, fn, deps, tok, 1, reads, writes)

    def dma(self, eng, out, in_, reads=(), writes=(), **kw):
        if eng == 'pool':
            eng = 'act'
        deps = self._deps(reads, writes)
        j = self.dcount[eng]
        self.dcount[eng] += 1
        nd = NDMA[eng]
        slot, rnd = j % nd, j // nd
        sk = ('d', eng, slot)
        tok = (sk, 16 * (rnd + 1))
        extra = []
        if rnd > 0:
            w = self.waited[eng]
            if w.get(sk, 0) < 16 * rnd:
                w[sk] = 16 * rnd
                extra.append((sk, 16 * rnd))
        self._record(eng, lambda e: e.dma_start(out=out, in_=in_, **kw), deps, tok, 16, reads, writes, extra)

    def barrier(self):
        toks = set()
        for e in ENGS:
            if self.ccount[e] > 0:
                toks.add((e, self.ccount[e]))
            nd = NDMA.get(e)
            if nd:
                j = self.dcount[e]
                for s in range(min(nd, j)):
                    last_j = ((j - 1 - s) // nd) * nd + s if False else None
                for jj in range(max(0, j - nd), j):
                    toks.add((('d', e, jj % nd), 16 * (jj // nd + 1)))
        for e in ENGS:
            self.pending[e] |= toks
        self.last_w = {}
        self.readers = {}

    def finalize(self):
        nc = self.nc
        self.barrier()
        with ExitStack() as st:
            sems = {}
            for e in ('pe', 'act', 'dve', 'pool'):
                sems[e] = st.enter_context(nc.semaphore('c_' + e))
            for e, nd in NDMA.items():
                for s in range(nd):
                    sems[('d', e, s)] = st.enter_context(nc.semaphore('d_%s_%d' % (e, s)))
            block = st.enter_context(nc.Block())

            def run(eng_name):
                def body(e):
                    for (waits, fn, sk, inc) in self.ops[eng_name]:
                        for (wk, val) in waits:
                            e.wait_ge(sems[wk], val)
                        ins = fn(e)
                        ins.then_inc(sems[sk], inc)
                    w = self.waited[eng_name]
                    for (wk, val) in sorted(self.pending[eng_name], key=lambda t: (str(t[0]), t[1])):
                        if wk == eng_name:
                            continue
                        if w.get(wk, 0) >= val:
                            continue
                        e.wait_ge(sems[wk], val)
                return body

            block.tensor(run('pe'))
            block.scalar(run('act'))
            block.vector(run('dve'))
            block.gpsimd(run('pool'))
            block.sync(run('sp'))

    def stats(self):
        return {e: len(self.ops[e]) for e in ENGS}


D = 1024; L = 4096; LC = 256; IN0 = 4112
OFF_DN = 1536; OFF_AB = 3072; OFF_Z = 3088


def phase_mod(P, T, pre, pers):
    nc = P.nc
    modw = T[pre + '_mod_w']
    with ExitStack() as st:
        cc = P.sb('cc', [128, 16], F32, st)
        sc = P.sb('sc', [128, 16], F32, st)
        mb = P.sb('mb', [128, 24], F32, st)
        ng = P.sb('ng', [128, 8], F32, st)
        wb = [P.sb('mw%d' % k, [128, 3072], F32, st) for k in range(8)]
        ps = P.ps('modps', [128, 512], F32, st)
        modv = pers['modv']
        P.dma('sp', cc[:], T['cc'], writes=['cc'])
        P.dma('sp', mb[:], T[pre + '_mod_b'], writes=['mb'])
        P.dma('sp', ng[:], T[pre + '_norm_g'], writes=['ng'])
        for k in range(8):
            P.dma('sp' if k % 2 == 0 else 'pool', wb[k][:], modw[k * 128:(k + 1) * 128, :], writes=[('mw', k)])
        P.op('act', lambda e: e.activation(sc[:], cc[:], AF.Silu), reads=['cc'], writes=['sc'])
        for m in range(24):
            for k in range(8):
                P.op('pe', lambda e, m=m, k=k: e.matmul(ps[:, 2 * m:2 * m + 2], wb[k][:, m * 128:(m + 1) * 128], sc[:, 2 * k:2 * k + 2], start=(k == 0), stop=(k == 7)),
                     reads=[('mw', k), 'sc'], writes=['modps'])
        for n in range(2):
            P.op('dve', lambda e, n=n: e.tensor_tensor(modv[:, :, n], ps[:, n:48:2], mb[:], ALU.add), reads=['mb'], writes=['modps', ('modv', pre)])
        gs = pers['gs']
        for n in range(2):
            P.op('dve', lambda e, n=n: e.scalar_tensor_tensor(gs[:, :, n], modv[:, 8:16, n], 1.0, ng[:], ALU.add, ALU.mult), reads=['ng', ('modv', pre)], writes=[('gs', pre)])
    P.barrier()


def rr(lst):
    i = [0]
    def nxt():
        v = lst[i[0] % len(lst)]
        i[0] += 1
        return v
    return nxt


def adaln_tile(P, xsrc, ntok, n, pers, pre, bufs, key, dst=None):
    xt, xk = bufs['xt']()
    if dst is not None:
        hT, hk = dst
    else:
        hT, hk = bufs['hT']()
    sq = bufs['sq']; ones = bufs['ones']; rstd = bufs['rstd']; bank = bufs['ssbank']
    modv, gs = pers['modv'], pers['gs']
    P.dma('sp', xt[:, :, :ntok], xsrc, writes=[xk])
    P.op('act', lambda e: e.activation(sq[:, :, :ntok], xt[:, :, :ntok], AF.Square), reads=[xk], writes=['sq'] + [('sqj', j) for j in range(8)])
    for j in range(8):
        P.op('pe', lambda e, j=j: e.matmul(bank[:, :ntok], ones[:], sq[:, j, :ntok], start=(j == 0), stop=(j == 7)),
             reads=['sq', 'ones'], writes=['ssbank'])
    P.op('dve', lambda e: e.tensor_scalar(rstd[:, :ntok], bank[:, :ntok], 1.0 / 1024, 1e-6, ALU.mult, ALU.add), writes=['ssbank', 'rstd'])
    P.op('act', lambda e: e.activation(rstd[:, :ntok], rstd[:, :ntok], AF.Sqrt), writes=['rstd'])
    P.op('dve', lambda e: e.vector.reciprocal(rstd[:, :ntok], rstd[:, :ntok]) if False else e.reciprocal(rstd[:, :ntok], rstd[:, :ntok]), writes=['rstd'])
    for j in range(8):
        P.op('dve', lambda e, j=j: e.scalar_tensor_tensor(sq[:, j, :ntok], xt[:, j, :ntok], gs[:, j, n:n + 1], rstd[:, :ntok], ALU.mult, ALU.mult),
             reads=[xk, 'rstd', ('gs', pre)], writes=[('sqj', j)] + (['sq'] if j == 0 else []))
        P.op('act', lambda e, j=j: e.activation(hT[:, j, :ntok], sq[:, j, :ntok], AF.Identity, bias=modv[:, j, n:n + 1], scale=1.0),
             reads=[('sqj', j), ('modv', pre)], writes=[(hk, j)] + ([hk] if j == 0 else []))
    return hT, [(hk, j) for j in range(8)], xt, xk


def load_weights_bf16(P, wdram, W, wst, ncols, wkey):
    for k in range(8):
        ws, wsk = wst()
        P.dma('sp' if k % 2 == 0 else 'pool', ws[:, :ncols], wdram[k * 128:(k + 1) * 128, :], writes=[wsk])
        eng = 'pool' if k % 2 == 0 else 'act'
        if eng == 'pool':
            P.op('pool', lambda e, k=k, ws=ws: e.tensor_copy(W[:, k, :ncols], ws[:, :ncols]), reads=[wsk], writes=[(wkey, k)])
        else:
            P.op('act', lambda e, k=k, ws=ws: e.copy(W[:, k, :ncols], ws[:, :ncols]), reads=[wsk], writes=[(wkey, k)])


def phase_l0_proj(P, T, pers, S):
    nc = P.nc
    with ExitStack() as st:
        W = P.sb('W', [128, 8, IN0], BF16, st)
        wstl = [(P.sb('wst%d' % i, [128, IN0], F32, st), 'wst%d' % i) for i in range(1)]
        ones = P.sb('ones', [128, 128], F32, st)
        xts = [(P.sb('xt%d' % i, [128, 8, 512], F32, st), 'xt%d' % i) for i in range(2)]
        hTs = [(P.sb('hT%d' % i, [128, 8, 512], BF16, st), 'hT%d' % i) for i in range(2)]
        sq = P.sb('sq', [128, 8, 512], F32, st)
        rstd = P.sb('rstd', [128, 512], F32, st)
        ofs = [(P.sb('of%d' % i, [128, 512], F32, st), 'of%d' % i) for i in range(3)]
        obs = [(P.sb('ob%d' % i, [128, 512], BF16, st), 'ob%d' % i) for i in range(4)]
        banks = [(P.ps('pb%d' % i, [128, 512], F32, st), 'pb%d' % i) for i in range(8)]
        P.op('pool', lambda e: e.memset(ones[:], 1.0), writes=['ones'])
        load_weights_bf16(P, T['e_w_in'], W, rr(wstl), IN0, 'W')
        wkeys = [('W', k) for k in range(8)]
        bufs = dict(xt=rr(xts), hT=rr(hTs), sq=sq, ones=ones, rstd=rstd, ssbank=banks[0][0])
        pbank = rr(banks[1:])
        of = rr(ofs); ob = rr(obs)
        evac_i = [0]

        def evac(dst, src_bank, bkey, okey, func=None, scale=None):
            i = evac_i[0]; evac_i[0] += 1
            if func is not None or scale is not None or i % 2 == 0:
                f = func if func is not None else AF.Copy
                if scale is not None:
                    P.op('act', lambda e: e.activation(dst, src_bank, f, scale=scale), writes=[bkey, okey])
                else:
                    P.op('act', lambda e: e.activation(dst, src_bank, f), writes=[bkey, okey])
            else:
                P.op('dve', lambda e: e.tensor_copy(dst, src_bank), writes=[bkey, okey])

        xT3 = T['xT'].rearrange("(j p) t -> p j t", p=128)
        cT3 = T['ctxT'].rearrange("(j p) t -> p j t", p=128)
        tiles = [(xT3[:, :, tt * 512:(tt + 1) * 512], 512, 0, tt * 512) for tt in range(8)] + [(cT3, 256, 1, 4096)]
        dq = rr(['sp', 'pool'])
        def do_tile(src, ntok, n, t0):
            isx = (n == 0)
            hT, hkeys, _, _ = adaln_tile(P, src, ntok, n, pers, 'e', bufs, None)
            fm = []
            if isx:
                fm += [('q', c) for c in range(4)]
            fm += [('k', c) for c in range(4)] + [('dn', c) for c in range(12)]
            for (kind, c) in fm:
                col0 = {'q': 0, 'k': 512, 'dn': OFF_DN}[kind] + c * 128
                bank, bkey = pbank()
                for k in range(8):
                    P.op('pe', lambda e, k=k, bank=bank, col0=col0: e.matmul(bank[:, :ntok], W[:, k, col0:col0 + 128], hT[:, k, :ntok], start=(k == 0), stop=(k == 7)),
                         reads=wkeys + hkeys, writes=[bkey])
                if kind == 'dn':
                    o, okey = of()
                    evac(o[:, :ntok], bank[:, :ntok], bkey, okey)
                    dst = (S['dnT'][c * 128:(c + 1) * 128, t0:t0 + ntok] if isx else S['dncT'][c * 128:(c + 1) * 128, :])
                else:
                    o, okey = ob()
                    evac(o[:, :ntok], bank[:, :ntok], bkey, okey, scale=(0.125 if kind == 'q' else None))
                    dst = (S['qT'][c * 128:(c + 1) * 128, t0:t0 + ntok] if kind == 'q' else S['kT'][c * 128:(c + 1) * 128, t0:t0 + ntok])
                P.dma(dq(), dst, o[:, :ntok], reads=[okey], writes=[('dram', kind, c, t0)])
            for s in range(ntok // 128):
                tk0 = t0 + s * 128
                groups = [('v', 1024, 512), ('ab', OFF_AB, 16)]
                if isx:
                    groups += [('z0', OFF_Z, 512), ('z1', OFF_Z + 512, 512)]
                for (kind, col0, ncol) in groups:
                    bank, bkey = pbank()
                    for k in range(8):
                        P.op('pe', lambda e, k=k, bank=bank, col0=col0, ncol=ncol, s=s: e.matmul(bank[:, :ncol], hT[:, k, s * 128:(s + 1) * 128], W[:, k, col0:col0 + ncol], start=(k == 0), stop=(k == 7)),
                             reads=wkeys + hkeys, writes=[bkey])
                    if kind == 'ab':
                        o, okey = of()
                        evac(o[:, :16], bank[:, :16], bkey, okey)
                        P.dma(dq(), S['abtok'][tk0:tk0 + 128, :], o[:, :16], reads=[okey], writes=[('dram', 'ab', tk0)])
                    elif kind == 'v':
                        o, okey = ob()
                        evac(o[:, :], bank[:, :], bkey, okey)
                        P.dma(dq(), S['vtok'][tk0:tk0 + 128, :], o[:, :], reads=[okey], writes=[('dram', 'v', tk0)])
                    else:
                        zc = 0 if kind == 'z0' else 512
                        o, okey = ob()
                        evac(o[:, :], bank[:, :], bkey, okey, func=AF.Silu)
                        P.dma(dq(), S['ztok'][tk0:tk0 + 128, zc:zc + 512], o[:, :], reads=[okey], writes=[('dram', 'z', tk0, zc)])
        for tl in tiles:
            do_tile(*tl)
    P.barrier()


NEG = -30000.0

def na_bias_table(rpb):
    out = np.full((5, 128, 8, 5, 128), NEG, np.float32)
    blocks = [0, 1, 2, 30, 31]
    for vi, i in enumerate(blocks):
        cs = min(max(i - 2, 0), 27)
        for ql in range(128):
            r = 2 * i + ql // 64; qc = ql % 64
            r0 = min(max(r - 4, 0), 56); c0 = min(max(qc - 8, 0), 48)
            for ch in range(5):
                for kr_l in range(2):
                    kr = 2 * (cs + ch) + kr_l
                    if not (r0 <= kr < r0 + 8):
                        continue
                    kcs = np.arange(c0, c0 + 16)
                    out[vi, kr_l * 64 + kcs, :, ch, ql] = rpb[:, kr - r + 7, kcs - qc + 15].T
    return out

def variant_of(i):
    return {0: 0, 1: 1, 30: 3, 31: 4}.get(i, 2)


def phase_na(P, T, S):
    with ExitStack() as st:
        kT = P.sb('kT', [128, 4, 4352], BF16, st)
        qT = P.sb('qT', [128, 4, 4096], BF16, st)
        va = P.sb('va', [128, 34, 8, 65], BF16, st)
        bt = P.sb('bt', [128, 5, 8 * 5 * 128], BF16, st)
        btf = P.sb('btf', [128, 8 * 5 * 128], F32, st)
        ident = P.sb('ident', [128, 128], BF16, st)
        identf = P.sb('identf', [128, 128], F32, st)
        pts = [(P.sb('pt%d' % i, [128, 896], BF16, st), 'pt%d' % i) for i in range(2)]
        zts = [(P.sb('zt%d' % i, [128, 512], BF16, st), 'zt%d' % i) for i in range(2)]
        nas = [(P.sb('na%d' % i, [128, 8, 64], F32, st), 'na%d' % i) for i in range(2)]
        mxs = [(P.sb('mx%d' % i, [128, 512], BF16, st), 'mx%d' % i) for i in range(2)]
        rcs = [(P.sb('rc%d' % i, [128, 8], F32, st), 'rc%d' % i) for i in range(2)]
        sA = [(P.ps('sA%d' % i, [128, 512], F32, st), 'sA%d' % i) for i in range(2)]
        sB = [(P.ps('sB%d' % i, [128, 512], F32, st), 'sB%d' % i) for i in range(2)]
        oC = [(P.ps('oC%d' % i, [128, 512], F32, st), 'oC%d' % i) for i in range(4)]
        for hp in range(4):
            P.dma('sp', kT[:, hp, :], S['kT'][hp * 128:(hp + 1) * 128, :], writes=['kT'])
            P.dma('pool', qT[:, hp, :], S['qT'][hp * 128:(hp + 1) * 128, :], writes=['qT'])
        P.op('pool', lambda e: e.memset(va[:], 1.0), writes=['va'])
        vsrc = S['vtok'].rearrange("(c p) f -> p c f", p=128)
        for h in range(8):
            for (c0, c1) in ((0, 17), (17, 34)):
                P.dma('sp' if h % 2 == 0 else 'act', va[:, c0:c1, h, 0:64], vsrc[:, c0:c1, h * 64:(h + 1) * 64], writes=['va'])
        P.dma('sp', identf[:], T['ident'], writes=['identf'])
        P.op('pool', lambda e: e.tensor_copy(ident[:], identf[:]), reads=['identf'], writes=['ident'])
        for v in range(5):
            P.dma('sp', btf[:], T['na_bias'][v], writes=['btf'])
            P.op('act', lambda e, v=v: e.copy(bt[:, v, :], btf[:]), reads=['btf'], writes=['bt'])
        pt_n = rr(pts); zt_n = rr(zts); na_n = rr(nas); mx_n = rr(mxs); rc_n = rr(rcs)
        sA_n = rr(sA); sB_n = rr(sB); oC_n = rr(oC)

        blk = {}

        def stage1(i, h):
            cs = min(max(i - 2, 0), 27)
            v = variant_of(i)
            chunks = [cs + c for c in range(5)] + [32, 33]
            q0 = i * 128
            if h == 0:
                zt, ztk = zt_n()
                P.dma('act', zt[:], S['ztok'][q0:q0 + 128, 0:512], writes=[ztk])
                blk[i] = dict(zt=(zt, ztk), ocs=[oC_n(), oC_n()])
            hp, hb = h // 2, (h % 2) * 64
            a, ak = sA_n(); b, bk = sB_n()
            for ci, ch in enumerate(chunks):
                bank, bkk = (a, ak) if ci < 4 else (b, bk)
                col = (ci % 4) * 128
                has_bias = ci < 5
                P.op('pe', lambda e, bank=bank, col=col, ch=ch, has_bias=has_bias: e.matmul(
                    bank[:, col:col + 128], kT[hb:hb + 64, hp, ch * 128:(ch + 1) * 128], qT[hb:hb + 64, hp, q0:q0 + 128], start=True, stop=not has_bias),
                    reads=['kT', 'qT'], writes=[bkk])
                if has_bias:
                    off = (h * 5 + ci) * 128
                    P.op('pe', lambda e, bank=bank, col=col, off=off: e.matmul(bank[:, col:col + 128], ident[:], bt[:, v, off:off + 128], start=False, stop=True),
                         reads=['ident', 'bt'], writes=[bkk])
            pt, ptk = pt_n()
            P.op('act', lambda e: e.activation(pt[:, 0:512], a[:, :], AF.Exp), writes=[ak, (ptk, 0)])
            P.op('act', lambda e: e.activation(pt[:, 512:896], b[:, 0:384], AF.Exp), writes=[bk, (ptk, 1)])
            return dict(i=i, h=h, chunks=chunks, pt=pt, ptk=ptk, q0=q0)

        def stage2(c):
            i, h, chunks, pt, ptk, q0 = c['i'], c['h'], c['chunks'], c['pt'], c['ptk'], c['q0']
            ocs = blk[i]['ocs']
            oc, ock = ocs[h // 4]
            hh = h % 4
            for ci, ch in enumerate(chunks):
                P.op('pe', lambda e, ci=ci, ch=ch: e.matmul(oc[:, hh * 65:hh * 65 + 65], pt[:, ci * 128:(ci + 1) * 128], va[:, ch, h, :], start=(ci == 0), stop=(ci == 6)),
                     reads=[(ptk, 0), (ptk, 1), 'va'], writes=[ock])
            if h < 7:
                return
            zt, ztk = blk[i]['zt']
            na, nak = na_n(); rc, rck = rc_n(); mx, mxk = mx_n()
            for g in range(2):
                ocg, ocgk = ocs[g]
                ocv = ocg[:, 0:260].rearrange("p (h d) -> p h d", d=65)
                P.op('dve', lambda e, ocv=ocv, g=g: e.reciprocal(rc[:, g * 4:(g + 1) * 4], ocv[:, :, 64]), writes=[ocgk, (rck, g)])
                for h2 in range(4):
                    P.op('dve', lambda e, ocv=ocv, g=g, h2=h2: e.tensor_scalar(na[:, g * 4 + h2, :], ocv[:, h2, 0:64], rc[:, g * 4 + h2:g * 4 + h2 + 1], None, ALU.mult),
                         reads=[(rck, g)], writes=[ocgk, (nak, g)])
            P.op('pool', lambda e: e.tensor_tensor(mx[:], na[:].rearrange("p h d -> p (h d)"), zt[:], ALU.mult), reads=[(nak, 0), (nak, 1), ztk], writes=[mxk])
            P.dma('sp', S['mixtok'][q0:q0 + 128, 0:512], mx[:], reads=[mxk], writes=[('dram', 'mix', i)])
            if 'na_dbg' in S:
                P.dma('sp', S['na_dbg'][q0:q0 + 128, :], na[:].rearrange("p h d -> p (h d)"), reads=[(nak, 0), (nak, 1)], writes=[('dram', 'nadbg', i)])

        items = [(i, h) for i in range(32) for h in range(8)]
        prev = None
        for (i, h) in items:
            cur = stage1(i, h)
            if prev is not None:
                stage2(prev)
            prev = cur
        stage2(prev)
    P.barrier()


def phase_dn_gb(P, T, S, pers):
    with ExitStack() as st:
        ab = P.sb('ab', [128, 34, 16], F32, st)
        xa = P.sb('xa', [128, 34, 8], F32, st)
        t1 = P.sb('t1', [128, 34, 8], F32, st)
        t2 = P.sb('t2', [128, 34, 8], F32, st)
        al = P.sb('al', [128, 8], F32, st)
        dtb = P.sb('dtb', [128, 8], F32, st)
        g, beta = pers['g'], pers['beta']
        P.dma('sp', ab[:], S['abtok'].rearrange("(c p) f -> p c f", p=128), writes=['ab'])
        P.dma('sp', al[:], T['e_dn_a_log'].partition_broadcast(128), writes=['al'])
        P.dma('sp', dtb[:], T['e_dn_dt_bias'].partition_broadcast(128), writes=['dtb'])
        P.op('act', lambda e: e.activation(al[:], al[:], AF.Exp), writes=['al'])
        P.op('dve', lambda e: e.tensor_tensor(xa[:], ab[:, :, 0:8], dtb[:].unsqueeze(1).to_broadcast([128, 34, 8]), ALU.add), reads=['ab', 'dtb'], writes=['xa'])
        P.op('act', lambda e: e.activation(t1[:], xa[:], AF.Abs), reads=['xa'], writes=['t1'])
        P.op('act', lambda e: e.activation(t1[:], t1[:], AF.Exp, scale=-1.0), writes=['t1'])
        P.op('dve', lambda e: e.tensor_scalar_add(t1[:], t1[:], 1.0), writes=['t1'])
        P.op('act', lambda e: e.activation(t1[:], t1[:], AF.Ln), writes=['t1'])
        P.op('dve', lambda e: e.tensor_scalar_max(t2[:], xa[:], 0.0), reads=['xa'], writes=['t2'])
        P.op('dve', lambda e: e.tensor_tensor(t2[:], t2[:], t1[:], ALU.add), reads=['t1'], writes=['t2'])
        P.op('dve', lambda e: e.scalar_tensor_tensor(g[:], t2[:], -1.0, al[:].unsqueeze(1).to_broadcast([128, 34, 8]), ALU.mult, ALU.mult), reads=['t2', 'al'], writes=['g'])
        P.op('act', lambda e: e.activation(beta[:], ab[:, :, 8:16], AF.Sigmoid), reads=['ab'], writes=['beta'])
    P.barrier()


def phase_dn_prep(P, T, S):
    with ExitStack() as st:
        cw = P.sb('cw', [128, 60], F32, st)
        ones = P.sb('ones', [128, 128], F32, st)
        ident = P.sb('ident', [128, 128], BF16, st)
        identf = P.sb('identf', [128, 128], F32, st)
        raws = [(P.sb('raw%d' % i, [128, 4100], F32, st), 'raw%d' % i) for i in range(2)]
        accs = [(P.sb('acc%d' % i, [128, 4096], F32, st), 'acc%d' % i) for i in range(2)]
        sq = P.sb('sq', [128, 4096], F32, st)
        rns = [(P.sb('rn%d' % i, [128, 4096], F32, st), 'rn%d' % i) for i in range(2)]
        rn_n = rr(rns)
        obs = [(P.sb('obf%d' % i, [128, 4096], BF16, st), 'obf%d' % i) for i in range(2)]
        tks = [(P.sb('tk%d' % i, [128, 8, 128], BF16, st), 'tk%d' % i) for i in range(2)]
        banks = [(P.ps('nb%d' % i, [128, 512], F32, st), 'nb%d' % i) for i in range(2)]
        tps = [(P.ps('tpd%d' % i, [128, 1024], BF16, st), 'tpd%d' % i) for i in range(2)]
        P.dma('sp', cw[:], T['e_dn_conv'], writes=['cw'])
        P.op('pool', lambda e: e.memset(ones[:], 1.0), writes=['ones'])
        P.dma('sp', identf[:], T['ident'], writes=['identf'])
        P.op('pool', lambda e: e.tensor_copy(ident[:], identf[:]), reads=['identf'], writes=['ident'])
        for (r, rk) in raws:
            P.op('pool', lambda e, r=r: e.memset(r[:], 0.0), writes=[rk])
        raw_n = rr(raws); acc_n = rr(accs); ob_n = rr(obs); tk_n = rr(tks); bk_n = rr(banks); tp_n = rr(tps)
        dq = rr(['sp', 'pool'])

        def do_chunk(c, src, Lt, tok0):
            kind = c // 4; h = c % 4
            raw, rk = raw_n(); acc, ak = acc_n(); ob, obk = ob_n()
            rn, rnk = rn_n()
            if Lt < 4096:
                P.op('pool', lambda e: e.memset(raw[:, 2 + Lt:4 + Lt], 0.0), writes=[rk])
            P.dma(dq(), raw[:, 2:2 + Lt], src[c * 128:(c + 1) * 128, :], writes=[rk])
            P.op('dve', lambda e: e.tensor_scalar(acc[:, :Lt], raw[:, 0:Lt], cw[:, c * 5:c * 5 + 1], None, ALU.mult), reads=[rk, 'cw'], writes=[ak])
            for j in range(1, 5):
                P.op('dve', lambda e, j=j: e.scalar_tensor_tensor(acc[:, :Lt], raw[:, j:j + Lt], cw[:, c * 5 + j:c * 5 + j + 1], acc[:, :Lt], ALU.mult, ALU.add), reads=[rk, 'cw'], writes=[ak])
            P.op('act', lambda e: e.activation(acc[:, :Lt], acc[:, :Lt], AF.Silu), writes=[ak])
            if kind < 2:
                P.op('act', lambda e: e.activation(sq[:, :Lt], acc[:, :Lt], AF.Square), reads=[ak], writes=['sq'])
                for t0 in range(0, Lt, 512):
                    n = min(512, Lt - t0)
                    bank, bk = bk_n()
                    P.op('pe', lambda e, t0=t0, n=n, bank=bank: e.matmul(bank[:, :n], ones[:], sq[:, t0:t0 + n], start=True, stop=True), reads=['sq', 'ones'], writes=[bk])
                    P.op('dve', lambda e, t0=t0, n=n, bank=bank: e.tensor_scalar_add(rn[:, t0:t0 + n], bank[:, :n], 1e-6), writes=[bk, (rnk, t0), rnk])
                rkeys = [(rnk, t0) for t0 in range(0, Lt, 512)]
                yield
                P.op('act', lambda e: e.activation(rn[:, :Lt], rn[:, :Lt], AF.Sqrt), writes=rkeys + [rnk])
                P.op('dve', lambda e: e.reciprocal(rn[:, :Lt], rn[:, :Lt]), writes=[rnk])
                sc = (128 ** -0.5) if kind == 0 else 1.0
                P.op('dve', lambda e: e.scalar_tensor_tensor(ob[:, :Lt], acc[:, :Lt], sc, rn[:, :Lt], ALU.mult, ALU.mult), reads=[ak, rnk], writes=[obk])
                dst = S['dQT'] if kind == 0 else S['dKT']
                P.dma(dq(), dst[h, :, tok0:tok0 + Lt], ob[:, :Lt], reads=[obk], writes=[('dram', 'qk', c, tok0)])
            else:
                yield
                P.op('act', lambda e: e.copy(ob[:, :Lt], acc[:, :Lt]), reads=[ak], writes=[obk])
            if kind >= 1:
                dst = S['dKtok'] if kind == 1 else S['dVtok']
                for g0 in range(0, Lt // 128, 8):
                    ng = min(8, Lt // 128 - g0)
                    tp, tpk = tp_n(); tk, tkk = tk_n()
                    for s in range(ng):
                        P.op('pe', lambda e, s=s, g0=g0, tp=tp: e.transpose(tp[:, s * 128:(s + 1) * 128], ob[:, (g0 + s) * 128:(g0 + s + 1) * 128], ident[:]), reads=[obk, 'ident'], writes=[tpk])
                    P.op('act', lambda e, tp=tp, tk=tk, ng=ng: e.copy(tk[:, :ng, :], tp[:, :ng * 128].rearrange("p (s d) -> p s d", d=128)), writes=[tpk, tkk])
                    ta = tok0 + g0 * 128
                    P.dma(dq(), dst[ta:ta + ng * 128, h, :].rearrange("(s p) d -> p s d", p=128), tk[:, :ng, :], reads=[tkk], writes=[('dram', 'tok', c, ta)])

        items = [(c, S['dnT'], 4096, 0) for c in range(12)] + [(c, S['dncT'], 256, 4096) for c in range(12)]
        pend = None
        for it in items:
            g_ = do_chunk(*it)
            next(g_)
            if pend is not None:
                for _ in pend:
                    pass
            pend = g_
        for _ in pend:
            pass
    P.barrier()


def dn_masks():
    j = np.arange(128)[:, None]; t = np.arange(128)[None, :]
    same = (j // 64) == (t // 64)
    m = np.zeros((8, 128, 128), np.float32)
    m[0] = same & (j <= t)
    m[1] = same & (j >= t)
    m[2] = same & (t < j)
    m[3] = same & (t > j)
    m[4] = (j // 64 == 0) * np.ones((1, 128))
    m[5] = (j // 64 == 1) * np.ones((1, 128))
    m[6] = 1.0
    m[7] = np.eye(128)
    return m


def phase_dn_main(P, T, S, pers):
    g_all, b_all = pers['g'], pers['beta']
    with ExitStack() as st:
        msk = P.sb('msk', [128, 8, 128], F32, st)
        identb = P.sb('identb', [128, 128], BF16, st)
        P.dma('sp', msk[:], T['dn_masks'].rearrange("m p f -> p m f"), writes=['msk'])
        P.op('pool', lambda e: e.tensor_copy(identb[:], msk[:, 7, :]), reads=['msk'], writes=['identb'])
        TRI = [msk[:, 0, :], msk[:, 1, :]]; BM = [msk[:, 2, :], msk[:, 3, :]]; CH = [msk[:, 4, :], msk[:, 5, :]]
        ONES = msk[:, 6, :]; IDF = msk[:, 7, :]

        def bc_h(ap2):
            return ap2.unsqueeze(1).to_broadcast([128, 4, 128])

        def bc_l(ap2):
            return ap2.unsqueeze(2).to_broadcast([128, 4, 128])

        NS = 2
        def mk(name, dt, n=NS, shape=(128, 4, 128)):
            return [[(P.sb('%s%d_%d' % (name, d, i), list(shape), dt, st), '%s%d_%d' % (name, d, i)) for i in range(n)] for d in range(2)]
        QTt = mk('QTt', BF16); KTt = mk('KTt', BF16); Ktk = mk('Ktk', BF16); Vtk = mk('Vtk', BF16)
        Ab = mk('A', F32); Dm = mk('Dm', F32); DTm = mk('DTm', F32); EG = mk('EG', F32)
        Lb = mk('L', F32, 3); Nb = mk('N', F32, 3); XTf = mk('XT', F32, 3)
        XTb = mk('XTb', BF16); vb = mk('vb', BF16); kbg = mk('kbg', BF16); Kd = mk('Kd', BF16)
        Ub = mk('U', F32); WTb = mk('WT', BF16); Aqk = mk('Aqk', BF16); Qg = mk('Qg', BF16)
        stt_ = mk('st', F32, NS, (128, 16)); bgb = mk('bg', F32, NS, (128, 4)); tmpb = mk('tmp', F32)
        vnb = mk('vn', BF16, 1); ob = mk('o', F32, 2)
        Sf = [(P.sb('S%d' % d, [128, 4, 128], F32, st), 'S%d' % d) for d in range(2)]
        Sb = [(P.sb('Sb%d' % d, [128, 4, 128], BF16, st), 'Sb%d' % d) for d in range(2)]
        banks = [(P.ps('db%d' % i, [128, 512], F32, st), 'db%d' % i) for i in range(8)]
        bk_n = rr(banks)
        ctr = {}
        def nxt(pool, d):
            k = (id(pool), d)
            i = ctr.get(k, 0); ctr[k] = i + 1
            lst = pool[d]
            return lst[i % len(lst)]
        for d in range(2):
            P.op('pool', lambda e, d=d: e.memset(Sf[d][0][:], 0.0), writes=[Sf[d][1]])
            P.op('pool', lambda e, d=d: e.memset(Sb[d][0][:], 0.0), writes=[Sb[d][1]])
            P.op('pool', lambda e, d=d: e.memset(vnb[d][0][0][:], 0.0), writes=[vnb[d][0][1]])
        dq = rr(['sp', 'pool'])

        def prepass(tile, d, res):
            t0 = tile * 128
            R = {}
            dbgi = [0]
            def dbg(ap3, key):
                if 'dbg' in S and tile == S['dbg_tile'] and d == S['dbg_d']:
                    P.dma('sp', S['dbg'][dbgi[0]], ap3.rearrange("p h t -> p (h t)"), reads=[key], writes=[('dram', 'dbg', dbgi[0])])
                dbgi[0] += 1
            qt, qtk = nxt(QTt, d); kt, ktk = nxt(KTt, d); ktok, ktokk = nxt(Ktk, d); vtok, vtokk = nxt(Vtk, d)
            P.dma(dq(), qt[:], S['dQT'][:, :, t0:t0 + 128].rearrange("h p t -> p h t"), writes=[qtk])
            P.dma(dq(), kt[:], S['dKT'][:, :, t0:t0 + 128].rearrange("h p t -> p h t"), writes=[ktk])
            P.dma(dq(), ktok[:], S['dKtok'][t0:t0 + 128, :, :], writes=[ktokk])
            P.dma(dq(), vtok[:], S['dVtok'][t0:t0 + 128, :, :], writes=[vtokk])
            gt = g_all[:, tile, d * 4:(d + 1) * 4]; bt = b_all[:, tile, d * 4:(d + 1) * 4]
            z, zk = bk_n()
            P.op('pe', lambda e: e.matmul(z[:, 0:4], TRI[d], gt, start=True, stop=True), reads=['msk', 'g'], writes=[zk])
            P.op('pe', lambda e: e.matmul(z[:, 4:8], BM[d], gt, start=True, stop=True), reads=['msk', 'g'], writes=[zk])
            P.op('pe', lambda e: e.matmul(z[:, 8:12], CH[0], gt, start=True, stop=True), reads=['msk', 'g'], writes=[zk])
            P.op('pe', lambda e: e.matmul(z[:, 12:16], CH[1], gt, start=True, stop=True), reads=['msk', 'g'], writes=[zk])
            stv, stk = nxt(stt_, d)
            P.op('act', lambda e: e.activation(stv[:], z[:, 0:16], AF.Exp), writes=[zk, stk])
            yield
            bg, bgk = nxt(bgb, d)
            P.op('dve', lambda e: e.tensor_tensor(bg[:], bt, stv[:, 0:4], ALU.mult), reads=['beta', stk], writes=[bgk])
            A, Ak = nxt(Ab, d)
            for h in range(4):
                P.op('dve', lambda e, h=h: e.tensor_scalar(A[:, h, :], TRI[d], gt[:, h:h + 1], None, ALU.mult), reads=['msk', 'g'], writes=[Ak])
            d1, d1k = bk_n(); d2, d2k = bk_n(); d3, d3k = bk_n()
            for h in range(4):
                P.op('pe', lambda e, h=h: e.matmul(d1[:, h * 128:(h + 1) * 128], A[:, h, :], BM[d], start=True, stop=True), reads=[Ak, 'msk'], writes=[d1k])
            for h in range(4):
                P.op('pe', lambda e, h=h: e.matmul(d2[:, h * 128:(h + 1) * 128], BM[d], A[:, h, :], start=True, stop=True), reads=[Ak, 'msk'], writes=[d2k])
            for h in range(4):
                P.op('pe', lambda e, h=h: e.matmul(d3[:, h * 128:(h + 1) * 128], ONES, A[:, h, :], start=True, stop=True), reads=[Ak, 'msk'], writes=[d3k])
            dm, dmk = nxt(Dm, d); dtm, dtmk = nxt(DTm, d); eg, egk = nxt(EG, d)
            f3 = "p (h t) -> p h t"
            P.op('act', lambda e: e.activation(dm[:], d1[:].rearrange(f3, h=4), AF.Exp), writes=[d1k, dmk])
            P.op('act', lambda e: e.activation(dtm[:], d2[:].rearrange(f3, h=4), AF.Exp), writes=[d2k, dtmk])
            P.op('act', lambda e: e.activation(eg[:], d3[:].rearrange(f3, h=4), AF.Exp), writes=[d3k, egk])
            yield
            P.op('pool', lambda e: e.tensor_tensor(dm[:], dm[:], bc_h(BM[d]), ALU.mult), reads=['msk'], writes=[dmk])
            P.op('pool', lambda e: e.tensor_tensor(dtm[:], dtm[:], bc_h(TRI[d]), ALU.mult), reads=['msk'], writes=[dtmk])
            e1, e1k = bk_n(); e2, e2k = bk_n()
            for h in range(4):
                P.op('pe', lambda e, h=h: e.matmul(e1[:, h * 128:(h + 1) * 128], kt[:, h, :], kt[:, h, :], start=True, stop=True), reads=[ktk], writes=[e1k])
            for h in range(4):
                P.op('pe', lambda e, h=h: e.matmul(e2[:, h * 128:(h + 1) * 128], kt[:, h, :], qt[:, h, :], start=True, stop=True), reads=[ktk, qtk], writes=[e2k])
            tmp, tmpk = nxt(tmpb, d)
            L0, L0k = nxt(Lb, d)
            P.op('dve', lambda e: e.tensor_tensor(tmp[:], e1[:].rearrange(f3, h=4), dm[:], ALU.mult), reads=[dmk], writes=[e1k, tmpk])
            for h in range(4):
                P.op('dve', lambda e, h=h: e.tensor_scalar(L0[:, h, :], tmp[:, h, :], bt[:, h:h + 1], None, ALU.mult), reads=[tmpk, 'beta'], writes=[L0k])
            dbg(L0[:], L0k)
            aq, aqk = nxt(Aqk, d)
            P.op('dve', lambda e: e.tensor_tensor(aq[:], e2[:].rearrange(f3, h=4), dtm[:], ALU.mult), reads=[dtmk], writes=[e2k, aqk])
            yield
            qg, qgk = nxt(Qg, d)
            P.op('pool', lambda e: e.tensor_tensor(qg[:], qt[:], eg[:], ALU.mult), reads=[qtk, egk], writes=[qgk])
            tb, tbk = bk_n()
            for h in range(4):
                P.op('pe', lambda e, h=h: e.transpose(tb[:, h * 128:(h + 1) * 128], L0[:, h, :], IDF), reads=[L0k, 'msk'], writes=[tbk])
            N0, N0k = nxt(Nb, d)
            P.op('act', lambda e: e.copy(N0[:], tb[:].rearrange(f3, h=4)), writes=[tbk, N0k])
            yield
            XT, XTk = nxt(XTf, d)
            for h in range(4):
                P.op('dve', lambda e, h=h, XT=XT: e.scalar_tensor_tensor(XT[:, h, :], N0[:, h, :], -1.0, IDF, ALU.mult, ALU.add), reads=['msk', N0k], writes=[XTk])
            dbg(N0[:], N0k)
            dbg(XT[:], XTk)
            Lp, Lpk, Np, Npk = L0, L0k, N0, N0k
            for lev in range(1, 6):
                f1, f1k = bk_n()
                for h in range(4):
                    P.op('pe', lambda e, h=h, f1=f1, Lp=Lp, Np=Np: e.matmul(f1[:, h * 128:(h + 1) * 128], Np[:, h, :], Lp[:, h, :], start=True, stop=True), reads=[Lpk, Npk], writes=[f1k])
                Ln, Lnk = nxt(Lb, d)
                P.op('act', lambda e, f1=f1, Ln=Ln: e.copy(Ln[:], f1[:].rearrange(f3, h=4)), writes=[f1k, Lnk])
                if lev < 5:
                    f2, f2k = bk_n()
                    for h in range(4):
                        P.op('pe', lambda e, h=h, f2=f2, Lp=Lp, Np=Np: e.matmul(f2[:, h * 128:(h + 1) * 128], Lp[:, h, :], Np[:, h, :], start=True, stop=True), reads=[Lpk, Npk], writes=[f2k])
                    Nn, Nnk = nxt(Nb, d)
                    P.op('dve', lambda e, f2=f2, Nn=Nn: e.tensor_copy(Nn[:], f2[:].rearrange(f3, h=4)), writes=[f2k, Nnk])
                yield
                f3b, f3k = bk_n()
                for h in range(4):
                    P.op('pe', lambda e, h=h, f3b=f3b, Ln=Ln, XT=XT: e.matmul(f3b[:, h * 128:(h + 1) * 128], Ln[:, h, :], XT[:, h, :], start=True, stop=True), reads=[Lnk, XTk], writes=[f3k])
                XTn, XTnk = nxt(XTf, d)
                P.op('dve', lambda e, f3b=f3b, XT=XT, XTn=XTn: e.tensor_tensor(XTn[:], f3b[:].rearrange(f3, h=4), XT[:], ALU.add), reads=[XTk], writes=[f3k, XTnk])
                XT, XTk = XTn, XTnk
                yield
                if lev == 1:
                    dbg(Ln[:], Lnk)
                    dbg(XT[:], XTk)
                Lp, Lpk = Ln, Lnk
                if lev < 5:
                    Np, Npk = Nn, Nnk
            xtb, xtbk = nxt(XTb, d)
            P.op('act', lambda e: e.copy(xtb[:], XT[:]), reads=[XTk], writes=[xtbk])
            vbt, vbk = nxt(vb, d); kb, kbk = nxt(kbg, d); kd, kdk = nxt(Kd, d)
            for h in range(4):
                P.op('act', lambda e, h=h: e.activation(vbt[:, h, :], vtok[:, h, :], AF.Copy, scale=bt[:, h:h + 1]), reads=[vtokk, 'beta'], writes=[vbk])
                P.op('dve', lambda e, h=h: e.tensor_scalar(kb[:, h, :], ktok[:, h, :], bg[:, h:h + 1], None, ALU.mult), reads=[ktokk, bgk], writes=[kbk])
                P.op('act', lambda e, h=h: e.activation(kd[:, h, :], ktok[:, h, :], AF.Copy, scale=stv[:, 4 + h:5 + h]), reads=[ktokk, stk], writes=[kdk])
            g1, g1k = bk_n(); g2, g2k = bk_n()
            for h in range(4):
                P.op('pe', lambda e, h=h: e.matmul(g1[:, h * 128:(h + 1) * 128], xtb[:, h, :], vbt[:, h, :], start=True, stop=True), reads=[xtbk, vbk], writes=[g1k])
            for h in range(4):
                P.op('pe', lambda e, h=h: e.matmul(g2[:, h * 128:(h + 1) * 128], kb[:, h, :], xtb[:, h, :], start=True, stop=True), reads=[xtbk, kbk], writes=[g2k])
            U, Uk = nxt(Ub, d); WT, WTk = nxt(WTb, d)
            P.op('act', lambda e: e.copy(U[:], g1[:].rearrange(f3, h=4)), writes=[g1k, Uk])
            P.op('dve', lambda e: e.tensor_copy(WT[:], g2[:].rearrange(f3, h=4)), writes=[g2k, WTk])
            dbg(XT[:], XTk)
            dbg(U[:], Uk)
            res.update(U=(U, Uk), WT=(WT, WTk), aq=(aq, aqk), qg=(qg, qgk), kd=(kd, kdk), st=(stv, stk), tile=tile)

        def scan(pp, d, want_out):
            U, Uk = pp['U']; WT, WTk = pp['WT']; aq, aqk = pp['aq']; qg, qgk = pp['qg']; kd, kdk = pp['kd']; stv, stk = pp['st']
            tile = pp['tile']
            Sfd, Sfk = Sf[d]; Sbd, Sbk = Sb[d]
            vn, vnk = vnb[d][0]
            f3 = "p (h t) -> p h t"
            if want_out:
                o, ok = nxt(ob, d)
            def chunk_step(c):
                r0 = 64 * c
                h1, h1k = bk_n()
                for h in range(4):
                    P.op('pe', lambda e, h=h: e.matmul(h1[:, h * 128:(h + 1) * 128], WT[:, h, :], Sbd[:, h, :], start=True, stop=True), reads=[WTk, Sbk], writes=[h1k])
                P.op('dve', lambda e, r0=r0: e.tensor_tensor(vn[r0:r0 + 64, :, :], U[r0:r0 + 64, :, :], h1[r0:r0 + 64, :].rearrange(f3, h=4), ALU.subtract), reads=[Uk], writes=[h1k, vnk])
                yield
                if want_out:
                    h2, h2k = bk_n()
                    for h in range(4):
                        P.op('pe', lambda e, h=h: e.matmul(h2[:, h * 128:(h + 1) * 128], qg[:, h, :], Sbd[:, h, :], start=True, stop=False), reads=[qgk, Sbk], writes=[h2k])
                        P.op('pe', lambda e, h=h, r0=r0: e.matmul(h2[:, h * 128:(h + 1) * 128], aq[r0:r0 + 64, h, :], vn[r0:r0 + 64, h, :], start=False, stop=True), reads=[aqk, vnk], writes=[h2k])
                    P.op('act', lambda e, r0=r0: e.copy(o[r0:r0 + 64, :, :], h2[r0:r0 + 64, :].rearrange(f3, h=4)), writes=[h2k, ok])
                h3, h3k = bk_n()
                for h in range(4):
                    P.op('pe', lambda e, h=h, r0=r0: e.matmul(h3[:, h * 128:(h + 1) * 128], kd[r0:r0 + 64, h, :], vn[r0:r0 + 64, h, :], start=True, stop=True), reads=[kdk, vnk], writes=[h3k])
                for h in range(4):
                    P.op('dve', lambda e, c=c, h=h: e.scalar_tensor_tensor(Sfd[:, h, :], Sfd[:, h, :], stv[:, 8 + 4 * c + h:9 + 4 * c + h], h3[:, h * 128:(h + 1) * 128], ALU.mult, ALU.add), reads=[stk], writes=[h3k, Sfk])
                P.op('act', lambda e: e.copy(Sbd[:], Sfd[:]), reads=[Sfk], writes=[Sbk])
                yield
            for c in ([0, 1] if d == 0 else [1, 0]):
                yield from chunk_step(c)
            if want_out:
                t0 = tile * 128
                P.dma(dq(), S['dno'][d, t0:t0 + 128, :], o[:].rearrange("p h v -> p (h v)"), reads=[ok], writes=[('dram', 'dno', d, tile)])

        order = [[32, 33] + list(range(32)), [33, 32] + list(range(31, -1, -1))]
        nsteps = int(pers.get('dn_steps', 34))
        pp = {}
        def drive(gens):
            gens = list(gens)
            while gens:
                for g_ in list(gens):
                    try:
                        next(g_)
                    except StopIteration:
                        gens.remove(g_)

        for k in range(nsteps + 1):
            gens = []
            if k < nsteps:
                for d in range(2):
                    pp[(k, d)] = {}
                    gens.append(prepass(order[d][k], d, pp[(k, d)]))
            if k >= 1:
                for d in range(2):
                    gens.append(scan(pp[(k - 1, d)], d, order[d][k - 1] < 32))
            drive(gens)
        if 'dS' in S:
            for d in range(2):
                P.dma('sp', S['dS'][d], Sf[d][0][:].rearrange("p h v -> p (h v)"), reads=[Sf[d][1]], writes=[('dram', 'dS', d)])
    P.barrier()


def phase_dn_out(P, T, S):
    with ExitStack() as st:
        ng = P.sb('dng', [128, 128], F32, st)
        P.dma('sp', ng[:], T['e_dn_norm_g'].partition_broadcast(128), writes=['dng'])
        o0s = [(P.sb('oa%d' % i, [128, 4, 128], F32, st), 'oa%d' % i) for i in range(2)]
        o1s = [(P.sb('oc%d' % i, [128, 4, 128], F32, st), 'oc%d' % i) for i in range(2)]
        sqs = [(P.sb('osq%d' % i, [128, 4, 128], F32, st), 'osq%d' % i) for i in range(2)]
        zts = [(P.sb('oz%d' % i, [128, 4, 128], BF16, st), 'oz%d' % i) for i in range(2)]
        mxs = [(P.sb('om%d' % i, [128, 4, 128], BF16, st), 'om%d' % i) for i in range(2)]
        sss = [(P.sb('oss%d' % i, [128, 4], F32, st), 'oss%d' % i) for i in range(2)]
        o0n = rr(o0s); o1n = rr(o1s); sqn = rr(sqs); ztn = rr(zts); mxn = rr(mxs); ssn = rr(sss)

        def do_tile(i):
            t0 = i * 128
            a, ak = o0n(); b, bk = o1n(); sq, sk = sqn(); zt, zk = ztn(); mx, mk_ = mxn(); ss, ssk = ssn()
            P.dma('sp', a[:], S['dno'][0, t0:t0 + 128, :].rearrange("p (h v) -> p h v", h=4), writes=[ak])
            P.dma('pool', b[:], S['dno'][1, t0:t0 + 128, :].rearrange("p (h v) -> p h v", h=4), writes=[bk])
            P.dma('sp', zt[:], S['ztok'][t0:t0 + 128, 512:1024].rearrange("p (h v) -> p h v", h=4), writes=[zk])
            P.op('dve', lambda e: e.tensor_tensor(a[:], a[:], b[:], ALU.add), reads=[bk], writes=[ak])
            P.op('act', lambda e: e.activation(sq[:], a[:], AF.Square), reads=[ak], writes=[sk])
            P.op('dve', lambda e: e.tensor_reduce(ss[:], sq[:], AX.X, ALU.add), reads=[sk], writes=[ssk])
            P.op('dve', lambda e: e.tensor_scalar(ss[:], ss[:], 1.0 / 128, 1e-6, ALU.mult, ALU.add), writes=[ssk])
            P.op('act', lambda e: e.activation(ss[:], ss[:], AF.Sqrt), writes=[ssk])
            P.op('dve', lambda e: e.reciprocal(ss[:], ss[:]), writes=[ssk])
            for h in range(4):
                P.op('dve', lambda e, h=h: e.tensor_scalar(sq[:, h, :], a[:, h, :], ss[:, h:h + 1], None, ALU.mult), reads=[ak, ssk], writes=[sk])
            P.op('pool', lambda e: e.tensor_tensor(sq[:], sq[:], ng[:].unsqueeze(1).to_broadcast([128, 4, 128]), ALU.mult), reads=['dng'], writes=[sk])
            P.op('pool', lambda e: e.tensor_tensor(mx[:], sq[:], zt[:], ALU.mult), reads=[sk, zk], writes=[mk_])
            P.dma('sp', S['mixtok'][t0:t0 + 128, 512:1024], mx[:].rearrange("p h v -> p (h v)"), reads=[mk_], writes=[('dram', 'mixdn', i)])

        for i in range(32):
            do_tile(i)
    P.barrier()


def phase_out(P, T, S, pers, pre, wname, mix_is_tok, src_xT, dst_xT, final_out=None):
    with ExitStack() as st:
        W = P.sb('Wo', [128, 8, 1024], BF16, st)
        wstl = [(P.sb('wsto%d' % i, [128, 1024], F32, st), 'wsto%d' % i) for i in range(2)]
        ident = P.sb('ident', [128, 128], BF16, st)
        identf = P.sb('identf', [128, 128], F32, st)
        mts = [(P.sb('mt%d' % i, [128, 4, 1024], BF16, st), 'mt%d' % i) for i in range(2)]
        mTs = [(P.sb('mT%d' % i, [128, 8, 512], BF16, st), 'mT%d' % i) for i in range(2)]
        xts = [(P.sb('xo%d' % i, [128, 8, 512], F32, st), 'xo%d' % i) for i in range(2)]
        ots = [(P.sb('oo%d' % i, [128, 512], F32, st), 'oo%d' % i) for i in range(3)]
        tps = [(P.ps('tp%d' % i, [128, 1024], BF16, st), 'tp%d' % i) for i in range(2)]
        banks = [(P.ps('ob%d' % i, [128, 512], F32, st), 'ob%d' % i) for i in range(4)]
        if final_out is not None:
            xgs = [(P.sb('xg%d' % i, [128, 8, 512], F32, st), 'xg%d' % i) for i in range(2)]
            sqf = P.sb('sqfin', [128, 8, 512], F32, st)
            onesf = P.sb('onesf', [128, 128], F32, st)
            fg = P.sb('fgf', [128, 8], F32, st)
            rstdf = P.sb('rstdfin', [128, 512], F32, st)
            fbank = (P.ps('fbk', [128, 512], F32, st), 'fbk')
            P.op('pool', lambda e: e.memset(onesf[:], 1.0), writes=['onesf'])
            P.dma('sp', fg[:], T['final_norm_g'], writes=['fgf'])
            xg_n = rr(xgs)
            fo3 = final_out.rearrange("(j p) t -> p j t", p=128)
        load_weights_bf16(P, T[wname], W, rr(wstl), 1024, 'Wo')
        wkeys = [('Wo', k) for k in range(8)]
        P.dma('sp', identf[:], T['ident'], writes=['identf'])
        P.op('pool', lambda e: e.tensor_copy(ident[:], identf[:]), reads=['identf'], writes=['ident'])
        modv = pers['modv']
        mt_n = rr(mts); mT_n = rr(mTs); xt_n = rr(xts); ot_n = rr(ots); tp_n = rr(tps); bk_n = rr(banks)
        xs3 = src_xT.rearrange("(j p) t -> p j t", p=128)
        dq = rr(['sp', 'pool'])

        def do_group(gi):
            t0 = gi * 512
            mT, mTk = mT_n()
            if mix_is_tok:
                mt, mtk = mt_n()
                P.dma('sp', mt[:], S['mixtok'][t0:t0 + 512, :].rearrange("(s p) f -> p s f", p=128), writes=[mtk])
                for s in range(4):
                    tp, tpk = tp_n()
                    for k in range(8):
                        P.op('pe', lambda e, s=s, k=k, tp=tp: e.transpose(tp[:, k * 128:(k + 1) * 128], mt[:, s, k * 128:(k + 1) * 128], ident[:]),
                             reads=[mtk, 'ident'], writes=[tpk])
                    eng = 'act' if s % 2 == 0 else 'dve'
                    if eng == 'act':
                        P.op('act', lambda e, s=s, tp=tp: e.copy(mT[:, :, s * 128:(s + 1) * 128], tp[:].rearrange("p (k t) -> p k t", t=128)), writes=[tpk, (mTk, s)])
                    else:
                        P.op('dve', lambda e, s=s, tp=tp: e.tensor_copy(mT[:, :, s * 128:(s + 1) * 128], tp[:].rearrange("p (k t) -> p k t", t=128)), writes=[tpk, (mTk, s)])
                mkeys = [(mTk, s) for s in range(4)]
            else:
                P.dma('sp', mT[:], S['mixT'].rearrange("(k p) t -> p k t", p=128)[:, :, t0:t0 + 512], writes=[mTk])
                mkeys = [mTk]
            xt, xtk = xt_n()
            P.dma('pool', xt[:], xs3[:, :, t0:t0 + 512], writes=[xtk])
            if final_out is not None:
                xg, xgk = xg_n()
            for mc in range(8):
                bank, bkk = bk_n()
                for k in range(8):
                    P.op('pe', lambda e, k=k, mc=mc, bank=bank: e.matmul(bank[:], W[:, k, mc * 128:(mc + 1) * 128], mT[:, k, :], start=(k == 0), stop=(k == 7)),
                         reads=wkeys + mkeys, writes=[bkk])
                if final_out is None:
                    o, ok = ot_n()
                    P.op('dve', lambda e, mc=mc, bank=bank, o=o: e.scalar_tensor_tensor(o[:], bank[:], modv[:, 16 + mc, 0:1], xt[:, mc, :], ALU.mult, ALU.add),
                         reads=[xtk, ('modv', pre)], writes=[bkk, ok])
                    P.dma(dq(), dst_xT[mc * 128:(mc + 1) * 128, t0:t0 + 512], o[:], reads=[ok], writes=[('dram', 'xn', mc, gi)])
                else:
                    P.op('dve', lambda e, mc=mc, bank=bank: e.scalar_tensor_tensor(xg[:, mc, :], bank[:], modv[:, 16 + mc, 0:1], xt[:, mc, :], ALU.mult, ALU.add),
                         reads=[xtk, ('modv', pre)], writes=[bkk, (xgk, mc)] + ([xgk] if mc == 0 else []))
            if final_out is not None:
                xkeys = [(xgk, mc) for mc in range(8)]
                fb, fbk = fbank
                P.op('act', lambda e: e.activation(sqf[:], xg[:], AF.Square), reads=xkeys, writes=['sqfin'])
                for j in range(8):
                    P.op('pe', lambda e, j=j: e.matmul(fb[:], onesf[:], sqf[:, j, :], start=(j == 0), stop=(j == 7)), reads=['sqfin', 'onesf'], writes=[fbk])
                P.op('dve', lambda e: e.tensor_scalar(rstdf[:], fb[:], 1.0 / 1024, 1e-6, ALU.mult, ALU.add), writes=[fbk, 'rstdfin'])
                P.op('act', lambda e: e.activation(rstdf[:], rstdf[:], AF.Sqrt), writes=['rstdfin'])
                P.op('dve', lambda e: e.reciprocal(rstdf[:], rstdf[:]), writes=['rstdfin'])
                for j in range(8):
                    P.op('dve', lambda e, j=j: e.scalar_tensor_tensor(sqf[:, j, :], xg[:, j, :], fg[:, j:j + 1], rstdf[:], ALU.mult, ALU.mult),
                         reads=xkeys + ['rstdfin', 'fgf'], writes=['sqfin'])
                P.dma(dq(), fo3[:, :, t0:t0 + 512], sqf[:], reads=['sqfin'], writes=[('dram', 'fin', gi), xgk] + xkeys)

        for gi in range(8):
            do_group(gi)
    P.barrier()


def phase_final(P, T, src_xT, dst):
    with ExitStack() as st:
        ones = P.sb('ones', [128, 128], F32, st)
        fg = P.sb('fg', [128, 8], F32, st)
        xts = [(P.sb('xf%d' % i, [128, 8, 512], F32, st), 'xf%d' % i) for i in range(2)]
        sqs = [(P.sb('sf%d' % i, [128, 8, 512], F32, st), 'sf%d' % i) for i in range(2)]
        rstd = P.sb('rstdf', [128, 512], F32, st)
        bank = P.ps('fb', [128, 512], F32, st)
        P.op('pool', lambda e: e.memset(ones[:], 1.0), writes=['ones'])
        P.dma('sp', fg[:], T['final_norm_g'], writes=['fg'])
        xs3 = src_xT.rearrange("(j p) t -> p j t", p=128)
        ds3 = dst.rearrange("(j p) t -> p j t", p=128)
        xt_n = rr(xts); sq_n = rr(sqs)

        def do_tile(tt):
            t0 = tt * 512
            xt, xk = xt_n(); sq, sk = sq_n()
            P.dma('sp', xt[:], xs3[:, :, t0:t0 + 512], writes=[xk])
            P.op('act', lambda e: e.activation(sq[:], xt[:], AF.Square), reads=[xk], writes=[sk])
            for j in range(8):
                P.op('pe', lambda e, j=j: e.matmul(bank[:], ones[:], sq[:, j, :], start=(j == 0), stop=(j == 7)), reads=[sk, 'ones'], writes=['fb'])
            P.op('dve', lambda e: e.tensor_scalar(rstd[:], bank[:], 1.0 / 1024, 1e-6, ALU.mult, ALU.add), writes=['fb', 'rstdf'])
            P.op('act', lambda e: e.activation(rstd[:], rstd[:], AF.Sqrt), writes=['rstdf'])
            P.op('dve', lambda e: e.reciprocal(rstd[:], rstd[:]), writes=['rstdf'])
            for j in range(8):
                P.op('dve', lambda e, j=j: e.scalar_tensor_tensor(sq[:, j, :], xt[:, j, :], fg[:, j:j + 1], rstd[:], ALU.mult, ALU.mult),
                     reads=[xk, 'rstdf', 'fg'], writes=[sk])
            P.dma('pool', ds3[:, :, t0:t0 + 512], sq[:], reads=[sk], writes=[('dram', 'fin', tt)])

        for tt in range(8):
            do_tile(tt)
    P.barrier()


NT = 2176
NMOD = 4096.0


def phase_dft_tables(P, T, S):
    with ExitStack() as st:
        kv = P.sb('kv', [128, NT], F32, st)
        rv = P.sb('rv', [128, 17], F32, st)
        hp = P.sb('hp', [128, 1], F32, st)
        W = NT
        prods = [(P.sb('pr%d' % i, [128, W], F32, st), 'pr%d' % i) for i in range(2)]
        qis = [(P.sb('qi%d' % i, [128, W], I32, st), 'qi%d' % i) for i in range(2)]
        qfs = [(P.sb('qf%d' % i, [128, W], F32, st), 'qf%d' % i) for i in range(2)]
        abss = [(P.sb('ab%d' % i, [128, W], F32, st), 'ab%d' % i) for i in range(2)]
        cts = [(P.sb('ct%d' % i, [128, W], BF16, st), 'ct%d' % i) for i in range(2)]
        sts = [(P.sb('sn%d' % i, [128, W], BF16, st), 'sn%d' % i) for i in range(2)]
        P.dma('sp', kv[:], T['kvec'].partition_broadcast(128), writes=['kv'])
        P.dma('sp', rv[:], T['rvals'], writes=['rv'])
        P.op('pool', lambda e: e.memset(hp[:], math.pi / 2), writes=['hp'])
        pn = rr(prods); qn = rr(qis); fn = rr(qfs); an = rr(abss); cn = rr(cts); sn = rr(sts)
        w0 = 2 * math.pi / NMOD

        def piece(rc, half):
            c0 = half * W
            pr, prk = pn(); qi, qik = qn(); qf, qfk = fn(); ab, abk = an(); ct, ctk = cn(); sn_, snk = sn()
            P.op('dve', lambda e: e.tensor_scalar(pr[:], kv[:, c0:c0 + W], rv[:, rc:rc + 1], None, ALU.mult), reads=['kv', 'rv'], writes=[prk])
            P.op('dve', lambda e: e.tensor_scalar(qi[:], pr[:], 1.0 / NMOD, None, ALU.mult), reads=[prk], writes=[qik])
            P.op('pool', lambda e: e.tensor_copy(qf[:], qi[:]), reads=[qik], writes=[qfk])
            P.op('dve', lambda e: e.scalar_tensor_tensor(pr[:], qf[:], -NMOD, pr[:], ALU.mult, ALU.add), reads=[qfk], writes=[prk])
            P.op('act', lambda e: e.activation(sn_[:], pr[:], AF.Sin, scale=w0), reads=[prk], writes=[snk])
            P.op('act', lambda e: e.activation(ab[:], pr[:], AF.Abs), reads=[prk], writes=[abk])
            P.op('act', lambda e: e.activation(ct[:], ab[:], AF.Sin, bias=hp[:], scale=-w0), reads=[abk, 'hp'], writes=[ctk])
            P.dma('sp', S['Ctab'][rc * 128:(rc + 1) * 128, c0:c0 + W], ct[:], reads=[ctk], writes=[('dram', 'C', rc, half)])
            P.dma('pool', S['Stab'][rc * 128:(rc + 1) * 128, c0:c0 + W], sn_[:], reads=[snk], writes=[('dram', 'S', rc, half)])

        for rc in range(17):
            piece(rc, 0)
    P.barrier()


def dft_consts():
    kvec = np.arange(NT, dtype=np.float32).reshape(1, NT)
    rvals = (np.arange(17)[None, :] * 128 + np.arange(128)[:, None]).astype(np.float32)
    return kvec, rvals


class _View:
    def __init__(self, base, t0):
        self.base, self.t0 = base, t0
    def __getitem__(self, idx):
        p, j, t = idx
        t = slice((t.start or 0) + self.t0, (t.stop if t.stop is not None else 512) + self.t0)
        return self.base[p, j, t]


def phase_l1_proj(P, T, pers, S, src_xT):
    with ExitStack() as st:
        W = P.sb('W1', [128, 8, 4096], BF16, st)
        hT = P.sb('hT1', [128, 8, 4096], BF16, st)
        cw = P.sb('cw1', [128, 72], F32, st)
        P.dma('sp', cw[:], T['o_hy_conv'], writes=['cw1'])
        with ExitStack() as st1:
            wstl = [(P.sb('wst1', [128, 4096], F32, st1), 'wst1')]
            ones = P.sb('ones', [128, 128], F32, st1)
            xts = [(P.sb('xt1_%d' % i, [128, 8, 512], F32, st1), 'xt1_%d' % i) for i in range(2)]
            sq = P.sb('sq1', [128, 8, 512], F32, st1)
            rstd = P.sb('rstd1', [128, 512], F32, st1)
            bank = P.ps('ssb1', [128, 512], F32, st1)
            P.op('pool', lambda e: e.memset(ones[:], 1.0), writes=['ones'])
            load_weights_bf16(P, T['o_w_in'], W, rr(wstl), 4096, 'W1')
            bufs = dict(xt=rr(xts), hT=None, sq=sq, ones=ones, rstd=rstd, ssbank=bank)
            x3 = src_xT.rearrange("(j p) t -> p j t", p=128)
            for tt in range(8):
                adaln_tile(P, x3[:, :, tt * 512:(tt + 1) * 512], 512, 0, pers, 'o', bufs, None, dst=(_View(hT, tt * 512), ('hT1', tt)))
            P.barrier()
        wkeys = []
        with ExitStack() as st2:
            prows = [(P.sb('prow%d' % i, [128, 4098], F32, st2), 'prow%d' % i) for i in range(2)]
            accs = [(P.sb('acc1_%d' % i, [128, 4096], F32, st2), 'acc1_%d' % i) for i in range(2)]
            grows = [(P.sb('grow%d' % i, [128, 4096], BF16, st2), 'grow%d' % i) for i in range(1)]
            banks = [(P.ps('pj%d' % i, [128, 512], F32, st2), 'pj%d' % i) for i in range(6)]
            for (pr, prk) in prows:
                P.op('pool', lambda e, pr=pr: e.memset(pr[:], 0.0), writes=[prk])
            pr_n = rr(prows); acc_n = rr(accs); gr_n = rr(grows); bk_n = rr(banks)
            dq = rr(['sp', 'pool'])

            def do_chunk(cc):
                isconv = cc < 24
                if isconv:
                    pr, prk = pr_n()
                else:
                    gr, grk = gr_n()
                for tt in range(8):
                    bank, bkk = bk_n()
                    for k in range(8):
                        P.op('pe', lambda e, k=k, bank=bank, tt=tt: e.matmul(bank[:], W[:, k, cc * 128:(cc + 1) * 128], hT[:, k, tt * 512:(tt + 1) * 512], start=(k == 0), stop=(k == 7)), writes=[bkk])
                    if isconv:
                        if tt % 2 == 0:
                            P.op('act', lambda e, bank=bank, tt=tt: e.copy(pr[:, 1 + tt * 512:1 + (tt + 1) * 512], bank[:]), writes=[bkk, (prk, tt)])
                        else:
                            P.op('dve', lambda e, bank=bank, tt=tt: e.tensor_copy(pr[:, 1 + tt * 512:1 + (tt + 1) * 512], bank[:]), writes=[bkk, (prk, tt)])
                    else:
                        P.op('act', lambda e, bank=bank, tt=tt: e.activation(gr[:, tt * 512:(tt + 1) * 512], bank[:], AF.Silu), writes=[bkk, (grk, tt)])
                if isconv:
                    acc, ak = acc_n()
                    pk = [(prk, tt) for tt in range(8)]
                    P.op('dve', lambda e: e.tensor_scalar(acc[:], pr[:, 0:4096], cw[:, cc * 3:cc * 3 + 1], None, ALU.mult), reads=['cw1'], writes=[ak, prk] + pk)
                    for j in (1, 2):
                        P.op('dve', lambda e, j=j: e.scalar_tensor_tensor(acc[:], pr[:, j:j + 4096], cw[:, cc * 3 + j:cc * 3 + j + 1], acc[:], ALU.mult, ALU.add), reads=['cw1'], writes=[ak, prk] + pk)
                    P.dma(dq(), S['uT'][cc * 128:(cc + 1) * 128, :], acc[:], reads=[ak], writes=[('dram', 'u', cc)])
                else:
                    g = cc - 24
                    P.dma(dq(), S['gT'][g * 128:(g + 1) * 128, :], gr[:], reads=[(grk, tt) for tt in range(8)], writes=[('dram', 'g', g)])

            for cc in range(32):
                do_chunk(cc)
    P.barrier()


HY_DMIN = math.log(1e-2) / 1.5
HY_DMAX = math.log(1e-2) / 0.3


def hy_consts():
    L = 4096
    t = np.linspace(0.0, 1.0, L, dtype=np.float32)[:, None]
    w = (2.0 * math.pi * np.arange(L, dtype=np.float32)[:, None] / L).astype(np.float32)
    f = np.linspace(1e-4, 15, 16, dtype=np.float32)[None, :]
    feats = np.concatenate([t, np.cos(f * w), -np.sin(f * w)], axis=-1).astype(np.float32)
    perm = np.concatenate([np.arange(0, L, 2), np.arange(1, L, 2)])
    featsT = np.ascontiguousarray(feats[perm].T)
    ntpos = -np.ascontiguousarray(t[:, 0].reshape(32, 128).T)
    decay = np.abs(np.linspace(HY_DMIN, HY_DMAX, 1024, dtype=np.float32)).reshape(1, 1024).astype(np.float32)
    k = np.arange(33 * 128)
    wk = np.where(k > 4096, 0.0, np.where((k == 0) | (k == 4096), 1.0, 2.0)) / 8192.0
    wk = np.ascontiguousarray(wk.reshape(33, 128).T).astype(np.float32)
    E = np.exp(-(t.astype(np.float32)) * decay).astype(np.float32)[perm]
    Etab = np.ascontiguousarray(E.reshape(32, 128, 8, 128).transpose(2, 1, 0, 3))
    kk = np.arange(17 * 128)
    wP = np.where(kk > 2048, 0.0, np.where(kk == 0, 1.0, 2.0)) / 8192.0
    wM = np.where(kk >= 2048, 0.0, np.where(kk == 0, 1.0, 2.0)) / 8192.0
    sl = lambda v: np.ascontiguousarray(v.reshape(17, 128).T)
    wts = np.concatenate([sl(wP), sl(wM), -sl(wP)], axis=1).astype(np.float32)
    return featsT, ntpos.astype(np.float32), decay, wts, Etab


def sin_reduced(P, dst, src_ps, bvec, fvec, bufs, key_ps, key_dst, np_, n):
    arg, argk = bufs['arg']; qi, qik = bufs['qi']; qf, qfk = bufs['qf']
    P.op('dve', lambda e: e.tensor_scalar(arg[:np_, :n], src_ps, bvec, fvec, ALU.add, ALU.mult), writes=[key_ps, argk])
    P.op('dve', lambda e: e.tensor_scalar(qi[:np_, :n], arg[:np_, :n], 1.0 / (2 * math.pi), None, ALU.mult), reads=[argk], writes=[qik])
    P.op('pool', lambda e: e.tensor_copy(qf[:np_, :n], qi[:np_, :n]), reads=[qik], writes=[qfk])
    P.op('dve', lambda e: e.scalar_tensor_tensor(arg[:np_, :n], qf[:np_, :n], -2 * math.pi, arg[:np_, :n], ALU.mult, ALU.add), reads=[qfk], writes=[argk])
    P.op('act', lambda e: e.activation(dst, arg[:np_, :n], AF.Sin), reads=[argk], writes=[key_dst])


def phase_hy_filters(P, T, S):
    with ExitStack() as st:
        fv = P.sb('fv', [64, 4], F32, st)
        P.dma('sp', fv[:], T['o_ffn_vec'], writes=['fv'])
        with ExitStack() as st1:
            featsT = P.sb('featsT', [33, 4096], F32, st1)
            w1 = P.sb('fw1', [33, 64], F32, st1); w2 = P.sb('fw2', [64, 64], F32, st1)
            h1 = P.sb('hid1T', [64, 4096], F32, st1)
            h2 = P.sb('hid2T', [64, 4096], F32, st1)
            arg = (P.sb('arg', [64, 512], F32, st1), 'arg'); qi = (P.sb('qi', [64, 512], I32, st1), 'qi'); qf = (P.sb('qf', [64, 512], F32, st1), 'qf')
            bufs = dict(arg=arg, qi=qi, qf=qf)
            pbs1 = [(P.ps('fh1_%d' % i, [128, 512], F32, st1), 'fh1_%d' % i) for i in range(2)]
            pb1 = rr(pbs1)
            P.dma('sp', featsT[:], T['featsT'], writes=['featsT'])
            P.dma('sp', w1[:], T['o_ffn_w1'], writes=['fw1']); P.dma('sp', w2[:], T['o_ffn_w2'], writes=['fw2'])
            for tt in range(8):
                ps, psk = pb1()
                P.op('pe', lambda e, ps=ps, tt=tt: e.matmul(ps[0:64, :], w1[:], featsT[:, tt * 512:(tt + 1) * 512], start=True, stop=True), reads=['fw1', 'featsT'], writes=[psk])
                sin_reduced(P, h1[:, tt * 512:(tt + 1) * 512], ps[0:64, :], fv[:, 0:1], fv[:, 1:2], bufs, psk, ('h1', tt), 64, 512)
            for tt in range(8):
                ps, psk = pb1()
                P.op('pe', lambda e, ps=ps, tt=tt: e.matmul(ps[0:64, :], w2[:], h1[:, tt * 512:(tt + 1) * 512], start=True, stop=True), reads=['fw2', ('h1', tt)], writes=[psk])
                sin_reduced(P, h2[:, tt * 512:(tt + 1) * 512], ps[0:64, :], fv[:, 2:3], fv[:, 3:4], bufs, psk, ('h2', tt), 64, 512)
            P.dma('sp', S['h2sc'], h2[:], reads=[('h2', tt) for tt in range(8)], writes=[('dram', 'h2sc')])
            P.barrier()
        wk = P.sb('wk', [128, 51], F32, st)
        tw = P.sb('twf', [128, 51], F32, st)
        ones = P.sb('ones', [128, 128], F32, st)
        av = P.sb('av', [128, 32, 1024], BF16, st); dv = P.sb('dv', [128, 32, 1024], BF16, st)
        P.dma('sp', wk[:], T['wk'], writes=['wk'])
        P.dma('sp', tw[:], T['hy_tw'], writes=['twf'])
        P.op('pool', lambda e: e.memset(ones[:], 1.0), writes=['ones'])
        dq = rr(['sp', 'act'])

        def gen_order(o):
            with ExitStack() as sg:
                hb = [(P.sb('hb%d' % i, [128, 32, 128], F32, sg), 'hb%d' % i) for i in range(2)]
                Et = (P.sb('Etab', [128, 32, 128], F32, sg), 'Etab')
                w3 = P.sb('fw3', [64, 2048], F32, sg)
                h2 = P.sb('hid2T', [64, 4096], F32, sg)
                P.dma('sp', h2[:], S['h2sc'], writes=['h2r'])
                P.dma('act', w3[:], T['o_ffn_w3'][:, o * 2048:(o + 1) * 2048], writes=['fw3'])
                prt = [rr([(P.sb('prt%d_%d' % (d_, i), [128, 128], F32, sg), 'prt%d_%d' % (d_, i)) for i in range(1)]) for d_ in range(2)]
                rns = [(P.sb('rnf%d' % i, [128, 128], F32, sg), 'rnf%d' % i) for i in range(2)]
                skp = P.sb('skp', [1, 128], F32, sg)
                pbs = [(P.ps('fh%d' % i, [128, 1024], F32, sg), 'fh%d' % i) for i in range(3)]
                nbs = [(P.ps('fn%d' % i, [128, 512], F32, sg), 'fn%d' % i) for i in range(2)]
                pb_n = rr(pbs)

                def do_dir(cg, dr):
                    hbt, hbk = hb[dr]
                    E, Ek = Et
                    rn, rnk = rns[dr]
                    col0 = dr * 1024 + cg * 128
                    nbank, nk = nbs[dr]
                    for q in range(4):
                        ps, psk = pb_n(); pr, prk = prt[dr]()
                        for jj in range(8):
                            j = q * 8 + jj
                            P.op('pe', lambda e, ps=ps, j=j, jj=jj: e.matmul(ps[:, jj * 128:(jj + 1) * 128], h2[:, j * 128:(j + 1) * 128], w3[:, col0:col0 + 128], start=True, stop=True), reads=['fw3', 'h2r'], writes=[psk])
                        js = slice(q * 8, (q + 1) * 8)
                        P.op('dve', lambda e, ps=ps, js=js: e.tensor_tensor(hbt[:, js, :], ps[:].rearrange("p (j c) -> p j c", c=128), E[:, js, :], ALU.mult), reads=[Ek], writes=[psk, (hbk, q)] + ([hbk] if q == 0 else []))
                        yield
                        P.op('act', lambda e, ps=ps, js=js: e.activation(ps[:].rearrange("p (j c) -> p j c", c=128), hbt[:, js, :], AF.Square), reads=[(hbk, q)], writes=[psk])
                        P.op('dve', lambda e, ps=ps, pr=pr: e.tensor_reduce(pr[:], ps[:].rearrange("p (j c) -> p c j", c=128), AX.X, ALU.add), writes=[psk, prk])
                        P.op('pe', lambda e, pr=pr, q=q: e.matmul(nbank[:, 0:128], ones[:], pr[:], start=(q == 0), stop=(q == 3)), reads=[prk, 'ones'], writes=[nk])
                        yield
                    P.op('dve', lambda e: e.tensor_scalar_add(rn[:], nbank[:, 0:128], 1e-6), writes=[nk, rnk])
                    P.op('act', lambda e: e.activation(rn[:], rn[:], AF.Sqrt), writes=[rnk])
                    P.op('dve', lambda e: e.reciprocal(rn[:], rn[:]), writes=[rnk])
                    yield
                    hk_all = [(hbk, q) for q in range(4)]
                    P.op('pool', lambda e: e.tensor_tensor(hbt[:], hbt[:], rn[:].unsqueeze(1).to_broadcast([128, 32, 128]), ALU.mult), reads=[rnk], writes=[hbk] + hk_all)

                def do_cg(cg):
                    P.dma('act', Et[0][:], T['Etab'][cg], writes=[Et[1]])
                    P.dma('sp', skp[:], T['o_hy_skip'][0:1, o * 1024 + cg * 128:o * 1024 + (cg + 1) * 128], writes=['skp'])
                    gens = [do_dir(cg, 0), do_dir(cg, 1)]
                    while gens:
                        for g_ in list(gens):
                            try:
                                next(g_)
                            except StopIteration:
                                gens.remove(g_)
                    (h0, h0k), (h1_, h1k) = hb
                    cs = slice(cg * 128, (cg + 1) * 128)
                    t0r = rns[0][0][0:1, :]; t0k = rns[0][1]
                    P.op('dve', lambda e: e.tensor_tensor(av[:, :, cs], h0[:], h1_[:], ALU.add), reads=[h0k, h1k], writes=[('av', cg)])
                    P.op('pool', lambda e: e.tensor_tensor(dv[:, :, cs], h0[:], h1_[:], ALU.subtract), reads=[h0k, h1k], writes=[('dv', cg)])
                    P.op('dve', lambda e: e.tensor_tensor(t0r, h0[0:1, 0, :], h1_[0:1, 0, :], ALU.add), reads=[h0k, h1k], writes=[t0k])
                    P.op('dve', lambda e: e.tensor_tensor(t0r, t0r, skp[0:1, :], ALU.add), reads=['skp'], writes=[t0k])
                    P.op('dve', lambda e: e.tensor_copy(av[0:1, 0, cs], t0r), reads=[t0k], writes=[('av', cg)])
                    P.op('dve', lambda e: e.tensor_copy(dv[0:1, 0, cs], t0r), reads=[t0k], writes=[('dv', cg)])

                for cg in range(8):
                    do_cg(cg)
                P.barrier()

        def xform_order(o):
            with ExitStack() as sx:
                cts = [(P.sb('ctf%d' % i, [128, 16, 128], BF16, sx), 'ctf%d' % i) for i in range(2)]
                sts = [(P.sb('stf%d' % i, [128, 16, 128], BF16, sx), 'stf%d' % i) for i in range(2)]
                hos = [(P.sb('ho%d' % i, [128, 4, 1024], F32, sx), 'ho%d' % i) for i in range(2)]
                tmps = [[(P.sb('xt%d_%d' % (q, i), [128, 512], F32, sx), 'xt%d_%d' % (q, i)) for i in range(5)] for q in range(2)]
                cbs = [(P.ps('fc%d' % i, [128, 512], F32, sx), 'fc%d' % i) for i in range(8)]
                cb_n = rr(cbs); ct_n = rr(cts); st_n = rr(sts); ho_n = rr(hos); tm_n = rr(tmps)

                def do_kc(kc):
                    ct, ctk = ct_n(); stt, stk = st_n(); ho, hok = ho_n()
                    P.dma('sp', ct[:], S['Ctab'][0:2048, kc * 128:(kc + 1) * 128].rearrange("(j p) q -> p j q", p=128), writes=[ctk])
                    P.dma('act', stt[:], S['Stab'][0:2048, kc * 128:(kc + 1) * 128].rearrange("(j p) q -> p j q", p=128), writes=[stk])
                    ck = tw[:, kc:kc + 1]; sk = tw[:, 17 + kc:18 + kc]; nsk = tw[:, 34 + kc:35 + kc]
                    wP = wk[:, kc:kc + 1]; wM = wk[:, 17 + kc:18 + kc]; nwP = wk[:, 34 + kc:35 + kc]

                    def half(ch, src, is_a):
                        cs = slice(ch * 512, (ch + 1) * 512)
                        (b0, kb0), (b1, kb1), (b2, kb2) = cb_n(), cb_n(), cb_n()
                        plan = ((b0, kb0, ct, ctk, 0), (b1, kb1, ct, ctk, 16), (b2, kb2, stt, stk, 16)) if is_a else ((b0, kb0, stt, stk, 0), (b1, kb1, stt, stk, 16), (b2, kb2, ct, ctk, 16))
                        for (bank, bkey, tab, tkey, joff) in plan:
                            for j in range(16):
                                P.op('pe', lambda e, j=j, bank=bank, tab=tab, joff=joff: e.matmul(bank[:], tab[:, j, :], src[:, joff + j, cs], start=(j == 0), stop=(j == 15)), reads=[tkey], writes=[bkey])
                        (u, uk), (tt_, ttk), (ev, evk), (pp_, ppk), (mm_, mmk) = tm_n()
                        P.op('act', lambda e: e.activation(u[:], b1[:], AF.Copy, scale=ck), reads=['twf'], writes=[kb1, uk])
                        P.op('dve', lambda e: e.scalar_tensor_tensor(tt_[:], b2[:], (nsk if is_a else sk), u[:], ALU.mult, ALU.add), reads=['twf', uk], writes=[kb2, ttk])
                        P.op('act', lambda e: e.copy(ev[:], b0[:]), writes=[kb0, evk])
                        P.op('pool', lambda e: e.tensor_tensor(pp_[:], ev[:], tt_[:], ALU.add), reads=[evk, ttk], writes=[ppk])
                        P.op('dve', lambda e: e.tensor_tensor(mm_[:], ev[:], tt_[:], ALU.subtract), reads=[evk, ttk], writes=[mmk])
                        fP, fM = (0, 2) if is_a else (1, 3)
                        P.op('act', lambda e: e.activation(ho[:, fP, cs], pp_[:], AF.Copy, scale=(wP if is_a else nwP)), reads=['wk', ppk], writes=[(hok, fP, ch)])
                        P.op('act', lambda e: e.activation(ho[:, fM, cs], mm_[:], AF.Copy, scale=wM), reads=['wk', mmk], writes=[(hok, fM, ch)])
                    for ch in range(2):
                        half(ch, av, True)
                        half(ch, dv, False)
                    allk = [(hok, f, ch) for f in range(4) for ch in range(2)]
                    P.dma('sp', S['Hsc'][o, :, kc * 128:(kc + 1) * 128, :].rearrange("f p c -> p f c"), ho[:], reads=allk, writes=[('dram', 'H', o, kc)] + allk)

                for kc in range(17):
                    do_kc(kc)
                P.barrier()

        for o in range(int(S.get('hy_orders', 2))):
            gen_order(o)
            if not S.get('skip_xform'):
                xform_order(o)
    P.barrier()


NKC = 17


def hy_twiddles():
    k = np.arange(NKC * 128, dtype=np.float64)
    th = 2 * np.pi * k / 8192.0
    ck = np.cos(th).reshape(NKC, 128).T; sk = np.sin(th).reshape(NKC, 128).T
    return np.ascontiguousarray(np.concatenate([ck, sk, -sk], axis=1)).astype(np.float32)


def phase_hy_conv(P, T, S, o, srcT, mul1T, gateT, dstT, dst_bf16):
    with ExitStack() as st:
        xtok = P.sb('xtok', [128, 32, 1024], BF16, st)
        with ExitStack() as st1:
            identf = P.sb('identf', [128, 128], F32, st1)
            ident = P.sb('identb', [128, 128], BF16, st1)
            srcs = [(P.sb('src%d' % i, [128, 8, 512], F32, st1), 'src%d' % i) for i in range(2)]
            sbfs = [(P.sb('sbf%d' % i, [128, 8, 512], BF16, st1), 'sbf%d' % i) for i in range(2)]
            tps = [(P.ps('tpx%d' % i, [128, 1024], BF16, st1), 'tpx%d' % i) for i in range(4)]
            P.dma('sp', identf[:], T['ident'], writes=['identf'])
            P.op('pool', lambda e: e.tensor_copy(ident[:], identf[:]), reads=['identf'], writes=['identb'])
            src_n = rr(srcs); sbf_n = rr(sbfs); tp_n = rr(tps)
            s3 = srcT.rearrange("(f p) t -> p f t", p=128)

            def load_tile(tt):
                sr, srk = src_n(); sb_, sbk = sbf_n()
                P.dma('sp' if tt % 2 == 0 else 'act', sr[:], s3[:, :, tt * 512:(tt + 1) * 512], writes=[srk])
                P.op('act', lambda e: e.copy(sb_[:, 0:4, :], sr[:, 0:4, :]), reads=[srk], writes=[(sbk, 0)])
                P.op('dve', lambda e: e.tensor_copy(sb_[:, 4:8, :], sr[:, 4:8, :]), reads=[srk], writes=[(sbk, 1)])
                cnt = 0
                for s2 in range(2):
                    for r in range(2):
                        tp, tpk = tp_n()
                        j = tt * 2 + s2 + 16 * r
                        a0 = s2 * 256 + r
                        for f in range(8):
                            P.op('pe', lambda e, f=f, a0=a0, s2=s2, tp=tp: e.transpose(tp[:, f * 128:(f + 1) * 128], sb_[:, f, a0:s2 * 256 + 256:2], ident[:]), reads=[(sbk, 0), (sbk, 1), 'identb'], writes=[tpk])
                        if cnt % 2 == 0:
                            P.op('act', lambda e, tp=tp, j=j: e.copy(xtok[:, j, :], tp[:]), writes=[tpk, ('xtok', j)])
                        else:
                            P.op('dve', lambda e, tp=tp, j=j: e.tensor_copy(xtok[:, j, :], tp[:]), writes=[tpk, ('xtok', j)])
                        cnt += 1
            for tt in range(8):
                load_tile(tt)
            P.barrier()
        tw = P.sb('tw', [128, 51], F32, st)
        P.dma('sp', tw[:], T['hy_tw'], writes=['tw'])
        cts = [(P.sb('ctc%d' % i, [128, 16, 128], BF16, st), 'ctc%d' % i) for i in range(2)]
        sts = [(P.sb('stc%d' % i, [128, 16, 128], BF16, st), 'stc%d' % i) for i in range(2)]
        hts = [(P.sb('ht%d' % i, [128, 4, 1024], F32, st), 'ht%d' % i) for i in range(2)]
        sets = [[(P.sb('sl%d_%d' % (q, i), [128, 512], F32, st), 'sl%d_%d' % (q, i)) for i in range(12)] for q in range(2)]
        outs = [(P.sb('yo%d' % i, [128, 4, 1024], BF16, st), 'yo%d' % i) for i in range(2)]
        banks = [(P.ps('cb%d' % i, [128, 512], F32, st), 'cb%d' % i) for i in range(8)]
        ct_n = rr(cts); st_n = rr(sts); ht_n = rr(hts); set_n = rr(sets); out_n = rr(outs); bk_n = rr(banks)

        def fwd_kc(kc):
            ct, ctk = ct_n(); stt, stk = st_n(); ht, htk = ht_n(); yo, yok = out_n()
            P.dma('sp', ct[:], S['Ctab'][0:2048, kc * 128:(kc + 1) * 128].rearrange("(j p) q -> p j q", p=128), writes=[ctk])
            P.dma('act', stt[:], S['Stab'][0:2048, kc * 128:(kc + 1) * 128].rearrange("(j p) q -> p j q", p=128), writes=[stk])
            P.dma('sp', ht[:], S['Hsc'][o, :, kc * 128:(kc + 1) * 128, :].rearrange("f p c -> p f c"), writes=[htk])
            ck = tw[:, kc:kc + 1]; sk = tw[:, 17 + kc:18 + kc]; nsk = tw[:, 34 + kc:35 + kc]

            def do_ch(ch):
                cs = slice(ch * 512, (ch + 1) * 512)
                (bEc, kEc), (bEs, kEs), (bOc, kOc), (bOs, kOs) = bk_n(), bk_n(), bk_n(), bk_n()
                for (bank, bkey, tab, tkey, joff) in ((bEc, kEc, ct, ctk, 0), (bEs, kEs, stt, stk, 0), (bOc, kOc, ct, ctk, 16), (bOs, kOs, stt, stk, 16)):
                    for j in range(16):
                        P.op('pe', lambda e, j=j, bank=bank, tab=tab, joff=joff: e.matmul(bank[:], tab[:, j, :], xtok[:, joff + j, cs], start=(j == 0), stop=(j == 15)), reads=[tkey], writes=[bkey])
                sl = set_n()
                (s0, k0), (s1, k1), (s2, k2), (s3_, k3), (s4, k4), (s5, k5), (s6, k6), (s7, k7), (s8, k8), (s9, k9), (s10, k10), (s11, k11) = sl
                A = P.op
                A('act', lambda e: e.copy(s0[:], bEc[:]), writes=[kEc, k0])
                A('act', lambda e: e.copy(s1[:], bEs[:]), writes=[kEs, k1])
                A('act', lambda e: e.activation(s2[:], bOc[:], AF.Copy, scale=ck), reads=['tw'], writes=[kOc, k2])
                A('act', lambda e: e.activation(s3_[:], bOs[:], AF.Copy, scale=ck), reads=['tw'], writes=[kOs, k3])
                A('dve', lambda e: e.scalar_tensor_tensor(s4[:], bOs[:], nsk, s2[:], ALU.mult, ALU.add), reads=['tw', k2], writes=[kOs, k4])
                A('dve', lambda e: e.scalar_tensor_tensor(s5[:], bOc[:], sk, s3_[:], ALU.mult, ALU.add), reads=['tw', k3], writes=[kOc, k5])
                yield
                hrP = ht[:, 0, cs]; hiP = ht[:, 1, cs]; hrM = ht[:, 2, cs]; hiM = ht[:, 3, cs]
                TT = lambda eng, out, okey, a_, akey, b_, bkey, op, extra=(): A(eng, lambda e: e.tensor_tensor(out, a_, b_, op), reads=[akey, bkey] + list(extra), writes=[okey])
                TT('dve', s6[:], k6, s0[:], k0, s4[:], k4, ALU.add)
                TT('dve', s8[:], k8, s1[:], k1, s5[:], k5, ALU.add)
                TT('dve', s2[:], k2, s6[:], k6, hrP, htk, ALU.mult)
                TT('dve', s3_[:], k3, s8[:], k8, hiP, htk, ALU.mult)
                TT('dve', s2[:], k2, s2[:], k2, s3_[:], k3, ALU.add)
                TT('dve', s3_[:], k3, s6[:], k6, hiP, htk, ALU.mult)
                TT('dve', s6[:], k6, s8[:], k8, hrP, htk, ALU.mult)
                TT('dve', s8[:], k8, s6[:], k6, s3_[:], k3, ALU.subtract)
                TT('pool', s7[:], k7, s0[:], k0, s4[:], k4, ALU.subtract)
                TT('pool', s9[:], k9, s5[:], k5, s1[:], k1, ALU.subtract)
                TT('pool', s10[:], k10, s7[:], k7, hrM, htk, ALU.mult)
                TT('pool', s11[:], k11, s9[:], k9, hiM, htk, ALU.mult)
                TT('pool', s10[:], k10, s10[:], k10, s11[:], k11, ALU.add)
                TT('pool', s11[:], k11, s7[:], k7, hiM, htk, ALU.mult)
                TT('pool', s7[:], k7, s9[:], k9, hrM, htk, ALU.mult)
                TT('pool', s9[:], k9, s7[:], k7, s11[:], k11, ALU.subtract)
                TT('pool', yo[:, 0, cs], (yok, ch, 0), s2[:], k2, s10[:], k10, ALU.add)
                TT('dve', yo[:, 1, cs], (yok, ch, 1), s8[:], k8, s9[:], k9, ALU.subtract)
                TT('pool', s3_[:], k3, s2[:], k2, s10[:], k10, ALU.subtract)
                TT('dve', s6[:], k6, s8[:], k8, s9[:], k9, ALU.add)
                A('act', lambda e: e.activation(s11[:], s6[:], AF.Copy, scale=sk), reads=['tw', k6], writes=[k11])
                A('act', lambda e: e.activation(s7[:], s3_[:], AF.Copy, scale=nsk), reads=['tw', k3], writes=[k7])
                A('dve', lambda e: e.scalar_tensor_tensor(yo[:, 2, cs], s3_[:], ck, s11[:], ALU.mult, ALU.add), reads=['tw', k3, k11], writes=[(yok, ch, 2)])
                A('dve', lambda e: e.scalar_tensor_tensor(yo[:, 3, cs], s6[:], ck, s7[:], ALU.mult, ALU.add), reads=['tw', k6, k7], writes=[(yok, ch, 3)])
            def store():
                allk = [(yok, ch, q) for ch in range(2) for q in range(4)]
                P.dma('act', S['Ysc'][:, kc * 128:(kc + 1) * 128, :].rearrange("f p c -> p f c"), yo[:], reads=allk, writes=[('dram', 'Y', kc)] + allk)
            return [(do_ch(0), None), (do_ch(1), store)]

        pend = None
        for kc in S.get('fwd_list', range(NKC)):
            for (g_, fin) in fwd_kc(kc):
                next(g_)
                if pend is not None:
                    for _ in pend[0]:
                        pass
                    if pend[1] is not None:
                        pend[1]()
                pend = (g_, fin)
        if pend is not None:
            for _ in pend[0]:
                pass
            if pend[1] is not None:
                pend[1]()
    P.barrier()
    with ExitStack() as st:
        Yc = P.sb('Yc', [128, 4, NKC, 512], BF16, st)
        cts = [(P.sb('cti%d' % i, [128, NKC, 256], BF16, st), 'cti%d' % i) for i in range(2)]
        sts = [(P.sb('sti%d' % i, [128, NKC, 256], BF16, st), 'sti%d' % i) for i in range(2)]
        m1s = [(P.sb('m1_%d' % i, [128, 4, 512], F32, st), 'm1_%d' % i) for i in range(2)]
        gts = [(P.sb('gt_%d' % i, [128, 4, 512], BF16, st), 'gt_%d' % i) for i in range(2)]
        ofs = [(P.sb('of_%d' % i, [128, 4, 512], F32, st), 'of_%d' % i) for i in range(2)]
        obs = [(P.sb('ob_%d' % i, [128, 4, 512], BF16, st), 'ob_%d' % i) for i in range(2)]
        banks = [(P.ps('ib%d' % i, [128, 512], F32, st), 'ib%d' % i) for i in range(6)]
        ct_n = rr(cts); st_n = rr(sts); m1_n = rr(m1s); gt_n = rr(gts); of_n = rr(ofs); ob_n = rr(obs); bk_n = rr(banks)
        m13 = mul1T.rearrange("(f p) t -> p f t", p=128)
        d3 = dstT.rearrange("(f p) t -> p f t", p=128)
        g3 = gateT.rearrange("(f p) t -> p f t", p=128) if gateT is not None else None

        def inv_half(chh):
            c0 = chh * 512
            for q in range(4):
                P.dma('sp' if q % 2 == 0 else 'act', Yc[:, q, :, :], S['Ysc'][q, :, c0:c0 + 512].rearrange("(k p) c -> p k c", p=128), writes=[('Yc', q)])
            ykeys = [('Yc', q) for q in range(4)]

            def inv_nt(nt):
                n0 = nt * 512; m0 = nt * 256
                ct, ctk = ct_n(); stt, stk = st_n(); m1, m1k = m1_n(); of, ofk = of_n()
                P.dma('sp', ct[:], S['Ctab'][:, m0:m0 + 256].rearrange("(k p) n -> p k n", p=128), writes=[ctk])
                P.dma('act', stt[:], S['Stab'][:, m0:m0 + 256].rearrange("(k p) n -> p k n", p=128), writes=[stk])
                P.dma('sp', m1[:], m13[:, chh * 4:(chh + 1) * 4, n0:n0 + 512], writes=[m1k])
                if g3 is not None:
                    gt, gtk = gt_n(); ob, obk = ob_n()
                    P.dma('act', gt[:], g3[:, chh * 4:(chh + 1) * 4, n0:n0 + 512], writes=[gtk])
                okeys = []
                for r in range(2):
                    for cc in range(4):
                        bank, bkk = bk_n()
                        for kc in range(NKC):
                            P.op('pe', lambda e, kc=kc, cc=cc, bank=bank, r=r: e.matmul(bank[:, 0:256], Yc[:, 2 * r, kc, cc * 128:(cc + 1) * 128], ct[:, kc, :], start=(kc == 0), stop=False), reads=ykeys + [ctk], writes=[bkk])
                        for kc in range(NKC):
                            P.op('pe', lambda e, kc=kc, cc=cc, bank=bank, r=r: e.matmul(bank[:, 0:256], Yc[:, 2 * r + 1, kc, cc * 128:(cc + 1) * 128], stt[:, kc, :], start=False, stop=(kc == NKC - 1)), reads=ykeys + [stk], writes=[bkk])
                        P.op('dve', lambda e, cc=cc, bank=bank, r=r: e.tensor_tensor(of[:, cc, r:512:2], bank[:, 0:256], m1[:, cc, r:512:2], ALU.mult), reads=[m1k], writes=[bkk, (ofk, cc, r)])
                        okeys.append((ofk, cc, r))
                if g3 is not None:
                    P.op('pool', lambda e: e.tensor_tensor(ob[:], of[:], gt[:], ALU.mult), reads=okeys + [gtk], writes=[obk])
                    P.dma('sp', d3[:, chh * 4:(chh + 1) * 4, n0:n0 + 512], ob[:], reads=[obk], writes=[('dram', 'cv', chh, nt), ofk] + okeys)
                else:
                    P.dma('sp', d3[:, chh * 4:(chh + 1) * 4, n0:n0 + 512], of[:], reads=okeys, writes=[('dram', 'cv', chh, nt), ofk] + okeys)
            for nt in range(8):
                inv_nt(nt)

        for chh in range(int(S.get('n_inv', 2))):
            inv_half(chh)
    P.barrier()


def build_program():
    nc = bass.Bass("TRN2", target_bir_lowering=False)

    def din(name, shape, dt=F32):
        return nc.dram_tensor(name, list(shape), dt, kind="ExternalInput").ap()

    def dint(name, shape, dt=F32):
        return nc.dram_tensor(name, list(shape), dt, kind="Internal").ap()

    T = dict(cc=din('cc', [128, 16]), xT=din('xT', [1024, 4096]), ctxT=din('ctxT', [1024, 256]),
             e_mod_w=din('e_mod_w', [1024, 3072]), e_mod_b=din('e_mod_b', [128, 24]), e_norm_g=din('e_norm_g', [128, 8]),
             e_w_in=din('e_w_in', [1024, 4112]), e_w_out=din('e_w_out', [1024, 1024]),
             na_bias=din('na_bias', [5, 128, 5120]), ident=din('ident', [128, 128]),
             e_dn_conv=din('e_dn_conv', [128, 60]), e_dn_a_log=din('e_dn_a_log', [1, 8]), e_dn_dt_bias=din('e_dn_dt_bias', [1, 8]),
             e_dn_norm_g=din('e_dn_norm_g', [1, 128]), dn_masks=din('dn_masks', [8, 128, 128]),
             o_mod_w=din('o_mod_w', [1024, 3072]), o_mod_b=din('o_mod_b', [128, 24]), o_norm_g=din('o_norm_g', [128, 8]),
             o_w_in=din('o_w_in', [1024, 4096]), o_hy_conv=din('o_hy_conv', [128, 72]), o_w_out=din('o_w_out', [1024, 1024]),
             kvec=din('kvec', [1, 2176]), rvals=din('rvals', [128, 17]), hy_tw=din('hy_tw', [128, 51]), featsT=din('featsT', [33, 4096]), ntpos=din('ntpos', [128, 32]),
             decay=din('decay', [1, 1024]), wk=din('wk', [128, 51]), Etab=din('Etab', [8, 128, 32, 128]),
             o_ffn_w1=din('o_ffn_w1', [33, 64]), o_ffn_w2=din('o_ffn_w2', [64, 64]), o_ffn_vec=din('o_ffn_vec', [64, 4]),
             o_ffn_w3=din('o_ffn_w3', [64, 4096]), o_hy_skip=din('o_hy_skip', [1, 2048]),
             final_norm_g=din('final_norm_g', [128, 8]))
    S = dict(qT=dint('qT', [512, 4096], BF16), kT=dint('kT', [512, 4352], BF16), vtok=dint('vtok', [4352, 512], BF16),
             dnT=dint('dnT', [1536, 4096]), dncT=dint('dncT', [1536, 256]), abtok=dint('abtok', [4352, 16]),
             ztok=dint('ztok', [4096, 1024], BF16), mixtok=dint('mixtok', [4096, 1024], BF16),
             dQT=dint('dQT', [4, 128, 4352], BF16), dKT=dint('dKT', [4, 128, 4352], BF16),
             dKtok=dint('dKtok', [4352, 4, 128], BF16), dVtok=dint('dVtok', [4352, 4, 128], BF16),
             dno=dint('dno', [2, 4096, 512]),
             x1T=dint('x1T', [1024, 4096]),
             Ctab=dint('Ctab', [2176, 2176], BF16), Stab=dint('Stab', [2176, 2176], BF16),
             Hsc=dint('Hsc', [2, 4, 2176, 1024]), h2sc=dint('h2sc', [64, 4096]), Ysc=dint('Ysc', [4, 2176, 1024], BF16),
             uT=dint('uT', [3072, 4096]), gT=dint('gT', [1024, 4096], BF16), zT=dint('zT', [1024, 4096]),
             mixT=dint('mixT', [1024, 4096], BF16), x2T=dint('x2T', [1024, 4096]))
    outT = nc.dram_tensor('outT', [1024, 4096], F32, kind="ExternalOutput").ap()
    P = Prog(nc)
    pers = dict(modv=P.sb('modv', [128, 24, 2], F32), gs=P.sb('gs', [128, 8, 2], F32))
    phase_mod(P, T, 'e', pers)
    phase_l0_proj(P, T, pers, S)
    phase_na(P, T, S)
    with ExitStack() as dn_stack:
        pers['g'] = P.sb('g', [128, 34, 8], F32, dn_stack)
        pers['beta'] = P.sb('beta', [128, 34, 8], F32, dn_stack)
        phase_dn_gb(P, T, S, pers)
        phase_dn_prep(P, T, S)
        phase_dn_main(P, T, S, pers)
    phase_dn_out(P, T, S)
    phase_out(P, T, S, pers, 'e', 'e_w_out', True, T['xT'], S['x1T'])
    phase_dft_tables(P, T, S)
    phase_mod(P, T, 'o', pers)
    phase_l1_proj(P, T, pers, S, S['x1T'])
    phase_hy_filters(P, T, S)
    phase_hy_conv(P, T, S, 0, S['uT'][0:1024, :], S['uT'][1024:2048, :], None, S['zT'], False)
    phase_hy_conv(P, T, S, 1, S['zT'], S['uT'][2048:3072, :], S['gT'], S['mixT'], True)
    phase_out(P, T, S, pers, 'o', 'o_w_out', False, S['x1T'], S['x2T'], final_out=outT)
    P.finalize()
    return nc


def _pl(v, k):
    return np.ascontiguousarray(np.asarray(v, np.float32).reshape(k, 128).T)


def kernel(**inp):
    inp = {k: np.asarray(v) for k, v in inp.items()}
    nc = build_program()
    f32 = np.float32
    tab = na_bias_table(inp['e_na_rpb'][0]).reshape(5, 128, 5120)
    ident = np.eye(128, dtype=f32)
    kvec, rvals = dft_consts()
    featsT, ntpos, decay, wk, Etab = hy_consts()
    dncw = np.ascontiguousarray(inp['e_dn_conv'][0].T.reshape(12, 128, 5).transpose(1, 0, 2).reshape(128, 60)).astype(f32)
    hycw = np.ascontiguousarray(inp['o_hy_conv'][0].T.reshape(24, 128, 3).transpose(1, 0, 2).reshape(128, 72)).astype(f32)
    fvec = np.stack([inp['o_ffn_b1'][0], inp['o_ffn_f1'][0], inp['o_ffn_b2'][0], inp['o_ffn_f2'][0]], axis=1).astype(f32)
    shared = dict(
        e_mod_w=inp['e_mod_w'][0], e_mod_b=_pl(inp['e_mod_b'][0], 24), e_norm_g=_pl(inp['e_norm_g'][0], 8),
        e_w_in=inp['e_w_in'][0], e_w_out=inp['e_w_out'][0], na_bias=tab, ident=ident,
        e_dn_conv=dncw, e_dn_a_log=inp['e_dn_a_log'][0].reshape(1, 8).astype(f32), e_dn_dt_bias=inp['e_dn_dt_bias'][0].reshape(1, 8).astype(f32),
        e_dn_norm_g=inp['e_dn_norm_g'][0].reshape(1, 128).astype(f32), dn_masks=dn_masks(),
        o_mod_w=inp['o_mod_w'][0], o_mod_b=_pl(inp['o_mod_b'][0], 24), o_norm_g=_pl(inp['o_norm_g'][0], 8),
        o_w_in=inp['o_w_in'][0], o_hy_conv=hycw, o_w_out=inp['o_w_out'][0],
        kvec=kvec, rvals=rvals, hy_tw=hy_twiddles(), featsT=featsT, ntpos=ntpos, decay=decay, wk=wk, Etab=Etab,
        o_ffn_w1=inp['o_ffn_w1'][0], o_ffn_w2=inp['o_ffn_w2'][0], o_ffn_vec=fvec, o_ffn_w3=inp['o_ffn_w3'][0],
        o_hy_skip=inp['o_hy_skip'][0].reshape(1, 2048).astype(f32), final_norm_g=_pl(inp['final_norm_g'], 8))
    shared = {k: np.ascontiguousarray(v, dtype=f32) for k, v in shared.items()}
    in_maps = []
    for b in range(8):
        cc = np.zeros((128, 16), f32)
        cc[:, 0::2] = _pl(inp['c'][b], 8)
        cc[:, 1::2] = _pl(inp['c_ctx'], 8)
        m = dict(shared)
        m.update(cc=cc, xT=np.ascontiguousarray(inp['x'][b].T, dtype=f32), ctxT=np.ascontiguousarray(inp['ctx'][b].T, dtype=f32))
        in_maps.append(m)
    res = run_bass_kernel_spmd(nc, in_maps, core_ids=list(range(8)))
    out = np.stack([np.ascontiguousarray(np.asarray(r['outT']).T) for r in res.results], axis=0)
    return out.astype(np.float32)
```

```python
import math
import numpy as np
import ml_dtypes
from contextlib import ExitStack
import concourse.bass as bass
import concourse.mybir as mybir
from concourse.bass_utils import run_bass_kernel_spmd

F32 = mybir.dt.float32
BF16 = mybir.dt.bfloat16
I32 = mybir.dt.int32
AF = mybir.ActivationFunctionType
ALU = mybir.AluOpType
AX = mybir.AxisListType

ENGS = ('pe', 'act', 'dve', 'pool', 'sp')
NDMA = {'sp': 8, 'act': 8}


class Prog:
    def __init__(self, nc, same_eng_sync=('pool', 'act', 'dve')):
        self.nc = nc
        self.ops = {e: [] for e in ENGS}
        self.ccount = {e: 0 for e in ENGS}
        self.dcount = {e: 0 for e in ENGS}
        self.last_w = {}
        self.readers = {}
        self.waited = {e: {} for e in ENGS}
        self.pending = {e: set() for e in ENGS}
        self.same = same_eng_sync
        self.stack = ExitStack()

    def _nm(self, name):
        self._nmc = getattr(self, '_nmc', 0) + 1
        return '%s_%d' % (name, self._nmc)

    def sb(self, name, shape, dt, stack=None):
        return (stack or self.stack).enter_context(self.nc.sbuf_tensor(self._nm('s_' + name), list(shape), dt))

    def ps(self, name, shape, dt=F32, stack=None):
        return (stack or self.stack).enter_context(self.nc.psum_tensor(self._nm('p_' + name), list(shape), dt))

    def _deps(self, reads, writes):
        deps = set()
        for k in reads:
            t = self.last_w.get(k)
            if t is not None:
                deps.add(t)
        for k in writes:
            t = self.last_w.get(k)
            if t is not None:
                deps.add(t)
            for r in self.readers.get(k, ()):
                deps.add(r)
        return deps

    def _record(self, eng, fn, deps, tok, inc, reads, writes, extra_waits=()):
        deps = set(deps) | self.pending[eng]
        self.pending[eng] = set()
        waits = list(extra_waits)
        w = self.waited[eng]
        for (sk, val) in sorted(deps, key=lambda t: (str(t[0]), t[1])):
            if sk == eng and (eng == 'pe' or eng not in self.same):
                continue
            if w.get(sk, 0) >= val:
                continue
            w[sk] = val
            waits.append((sk, val))
        self.ops[eng].append((waits, fn, tok[0], inc))
        for k in reads:
            self.readers.setdefault(k, []).append(tok)
        for k in writes:
            self.last_w[k] = tok
            self.readers[k] = []

    def op(self, eng, fn, reads=(), writes=()):
        deps = self._deps(reads, writes)
        self.ccount[eng] += 1
        tok = (eng, self.ccount[eng])
        self._record(eng, fn, deps, tok, 1, reads, writes)

    def dma(self, eng, out, in_, reads=(), writes=(), **kw):
        if eng == 'pool':
            eng = 'act'
        deps = self._deps(reads, writes)
        j = self.dcount[eng]
        self.dcount[eng] += 1
        nd = NDMA[eng]
        slot, rnd = j % nd, j // nd
        sk = ('d', eng, slot)
        tok = (sk, 16 * (rnd + 1))
        extra = []
        if rnd > 0:
            w = self.waited[eng]
            if w.get(sk, 0) < 16 * rnd:
                w[sk] = 16 * rnd
                extra.append((sk, 16 * rnd))
        self._record(eng, lambda e: e.dma_start(out=out, in_=in_, **kw), deps, tok, 16, reads, writes, extra)

    def barrier(self):
        toks = set()
        for e in ENGS:
            if self.ccount[e] > 0:
                toks.add((e, self.ccount[e]))
            nd = NDMA.get(e)
            if nd:
                j = self.dcount[e]
                for s in range(min(nd, j)):
                    last_j = ((j - 1 - s) // nd) * nd + s if False else None
                for jj in range(max(0, j - nd), j):
                    toks.add((('d', e, jj % nd), 16 * (jj // nd + 1)))
        for e in ENGS:
            self.pending[e] |= toks
        self.last_w = {}
        self.readers = {}

    def finalize(self):
        nc = self.nc
        self.barrier()
        with ExitStack() as st:
            sems = {}
            for e in ('pe', 'act', 'dve', 'pool'):
                sems[e] = st.enter_context(nc.semaphore('c_' + e))
            for e, nd in NDMA.items():
                for s in range(nd):
                    sems[('d', e, s)] = st.enter_context(nc.semaphore('d_%s_%d' % (e, s)))
            block = st.enter_context(nc.Block())

            def run(eng_name):
                def body(e):
                    for (waits, fn, sk, inc) in self.ops[eng_name]:
                        for (wk, val) in waits:
                            e.wait_ge(sems[wk], val)
                        ins = fn(e)
                        ins.then_inc(sems[sk], inc)
                    w = self.waited[eng_name]
                    for (wk, val) in sorted(self.pending[eng_name], key=lambda t: (str(t[0]), t[1])):
                        if wk == eng_name:
                            continue
                        if w.get(wk, 0) >= val:
                            continue
                        e.wait_ge(sems[wk], val)
                return body

            block.tensor(run('pe'))
            block.scalar(run('act'))
            block.vector(run('dve'))
            block.gpsimd(run('pool'))
            block.sync(run('sp'))

    def stats(self):
        return {e: len(self.ops[e]) for e in ENGS}


D = 1024; L = 4096; LC = 256; IN0 = 4112
OFF_DN = 1536; OFF_AB = 3072; OFF_Z = 3088


def phase_mod(P, T, pre, pers):
    nc = P.nc
    modw = T[pre + '_mod_w']
    with ExitStack() as st:
        cc = P.sb('cc', [128, 16], F32, st)
        sc = P.sb('sc', [128, 16], F32, st)
        mb = P.sb('mb', [128, 24], F32, st)
        ng = P.sb('ng', [128, 8], F32, st)
        wb = [P.sb('mw%d' % k, [128, 3072], F32, st) for k in range(8)]
        ps = P.ps('modps', [128, 512], F32, st)
        modv = pers['modv']
        P.dma('sp', cc[:], T['cc'], writes=['cc'])
        P.dma('sp', mb[:], T[pre + '_mod_b'], writes=['mb'])
        P.dma('sp', ng[:], T[pre + '_norm_g'], writes=['ng'])
        for k in range(8):
            P.dma('sp' if k % 2 == 0 else 'pool', wb[k][:], modw[k * 128:(k + 1) * 128, :], writes=[('mw', k)])
        P.op('act', lambda e: e.activation(sc[:], cc[:], AF.Silu), reads=['cc'], writes=['sc'])
        for m in range(24):
            for k in range(8):
                P.op('pe', lambda e, m=m, k=k: e.matmul(ps[:, 2 * m:2 * m + 2], wb[k][:, m * 128:(m + 1) * 128], sc[:, 2 * k:2 * k + 2], start=(k == 0), stop=(k == 7)),
                     reads=[('mw', k), 'sc'], writes=['modps'])
        for n in range(2):
            P.op('dve', lambda e, n=n: e.tensor_tensor(modv[:, :, n], ps[:, n:48:2], mb[:], ALU.add), reads=['mb'], writes=['modps', ('modv', pre)])
        gs = pers['gs']
        for n in range(2):
            P.op('dve', lambda e, n=n: e.scalar_tensor_tensor(gs[:, :, n], modv[:, 8:16, n], 1.0, ng[:], ALU.add, ALU.mult), reads=['ng', ('modv', pre)], writes=[('gs', pre)])
    P.barrier()


def rr(lst):
    i = [0]
    def nxt():
        v = lst[i[0] % len(lst)]
        i[0] += 1
        return v
    return nxt


def adaln_tile(P, xsrc, ntok, n, pers, pre, bufs, key, dst=None):
    xt, xk = bufs['xt']()
    if dst is not None:
        hT, hk = dst
    else:
        hT, hk = bufs['hT']()
    sq = bufs['sq']; ones = bufs['ones']; rstd = bufs['rstd']; bank = bufs['ssbank']
    modv, gs = pers['modv'], pers['gs']
    P.dma('sp', xt[:, :, :ntok], xsrc, writes=[xk])
    P.op('act', lambda e: e.activation(sq[:, :, :ntok], xt[:, :, :ntok], AF.Square), reads=[xk], writes=['sq'] + [('sqj', j) for j in range(8)])
    for j in range(8):
        P.op('pe', lambda e, j=j: e.matmul(bank[:, :ntok], ones[:], sq[:, j, :ntok], start=(j == 0), stop=(j == 7)),
             reads=['sq', 'ones'], writes=['ssbank'])
    P.op('dve', lambda e: e.tensor_scalar(rstd[:, :ntok], bank[:, :ntok], 1.0 / 1024, 1e-6, ALU.mult, ALU.add), writes=['ssbank', 'rstd'])
    P.op('act', lambda e: e.activation(rstd[:, :ntok], rstd[:, :ntok], AF.Sqrt), writes=['rstd'])
    P.op('dve', lambda e: e.vector.reciprocal(rstd[:, :ntok], rstd[:, :ntok]) if False else e.reciprocal(rstd[:, :ntok], rstd[:, :ntok]), writes=['rstd'])
    for j in range(8):
        P.op('dve', lambda e, j=j: e.scalar_tensor_tensor(sq[:, j, :ntok], xt[:, j, :ntok], gs[:, j, n:n + 1], rstd[:, :ntok], ALU.mult, ALU.mult),
             reads=[xk, 'rstd', ('gs', pre)], writes=[('sqj', j)] + (['sq'] if j == 0 else []))
        P.op('act', lambda e, j=j: e.activation(hT[:, j, :ntok], sq[:, j, :ntok], AF.Identity, bias=modv[:, j, n:n + 1], scale=1.0),
             reads=[('sqj', j), ('modv', pre)], writes=[(hk, j)] + ([hk] if j == 0 else []))
    return hT, [(hk, j) for j in range(8)], xt, xk


def load_weights_bf16(P, wdram, W, wst, ncols, wkey):
    for k in range(8):
        ws, wsk = wst()
        P.dma('sp' if k % 2 == 0 else 'pool', ws[:, :ncols], wdram[k * 128:(k + 1) * 128, :], writes=[wsk])
        eng = 'pool' if k % 2 == 0 else 'act'
        if eng == 'pool':
            P.op('pool', lambda e, k=k, ws=ws: e.tensor_copy(W[:, k, :ncols], ws[:, :ncols]), reads=[wsk], writes=[(wkey, k)])
        else:
            P.op('act', lambda e, k=k, ws=ws: e.copy(W[:, k, :ncols], ws[:, :ncols]), reads=[wsk], writes=[(wkey, k)])


def phase_l0_proj(P, T, pers, S):
    nc = P.nc
    with ExitStack() as st:
        W = P.sb('W', [128, 8, IN0], BF16, st)
        wstl = [(P.sb('wst%d' % i, [128, IN0], F32, st), 'wst%d' % i) for i in range(1)]
        ones = P.sb('ones', [128, 128], F32, st)
        xts = [(P.sb('xt%d' % i, [128, 8, 512], F32, st), 'xt%d' % i) for i in range(2)]
        hTs = [(P.sb('hT%d' % i, [128, 8, 512], BF16, st), 'hT%d' % i) for i in range(2)]
        sq = P.sb('sq', [128, 8, 512], F32, st)
        rstd = P.sb('rstd', [128, 512], F32, st)
        ofs = [(P.sb('of%d' % i, [128, 512], F32, st), 'of%d' % i) for i in range(3)]
        obs = [(P.sb('ob%d' % i, [128, 512], BF16, st), 'ob%d' % i) for i in range(4)]
        banks = [(P.ps('pb%d' % i, [128, 512], F32, st), 'pb%d' % i) for i in range(8)]
        P.op('pool', lambda e: e.memset(ones[:], 1.0), writes=['ones'])
        load_weights_bf16(P, T['e_w_in'], W, rr(wstl), IN0, 'W')
        wkeys = [('W', k) for k in range(8)]
        bufs = dict(xt=rr(xts), hT=rr(hTs), sq=sq, ones=ones, rstd=rstd, ssbank=banks[0][0])
        pbank = rr(banks[1:])
        of = rr(ofs); ob = rr(obs)
        evac_i = [0]

        def evac(dst, src_bank, bkey, okey, func=None, scale=None):
            i = evac_i[0]; evac_i[0] += 1
            if func is not None or scale is not None or i % 2 == 0:
                f = func if func is not None else AF.Copy
                if scale is not None:
                    P.op('act', lambda e: e.activation(dst, src_bank, f, scale=scale), writes=[bkey, okey])
                else:
                    P.op('act', lambda e: e.activation(dst, src_bank, f), writes=[bkey, okey])
            else:
                P.op('dve', lambda e: e.tensor_copy(dst, src_bank), writes=[bkey, okey])

        xT3 = T['xT'].rearrange("(j p) t -> p j t", p=128)
        cT3 = T['ctxT'].rearrange("(j p) t -> p j t", p=128)
        tiles = [(xT3[:, :, tt * 512:(tt + 1) * 512], 512, 0, tt * 512) for tt in range(8)] + [(cT3, 256, 1, 4096)]
        dq = rr(['sp', 'pool'])
        def do_tile(src, ntok, n, t0):
            isx = (n == 0)
            hT, hkeys, _, _ = adaln_tile(P, src, ntok, n, pers, 'e', bufs, None)
            fm = []
            if isx:
                fm += [('q', c) for c in range(4)]
            fm += [('k', c) for c in range(4)] + [('dn', c) for c in range(12)]
            for (kind, c) in fm:
                col0 = {'q': 0, 'k': 512, 'dn': OFF_DN}[kind] + c * 128
                bank, bkey = pbank()
                for k in range(8):
                    P.op('pe', lambda e, k=k, bank=bank, col0=col0: e.matmul(bank[:, :ntok], W[:, k, col0:col0 + 128], hT[:, k, :ntok], start=(k == 0), stop=(k == 7)),
                         reads=wkeys + hkeys, writes=[bkey])
                if kind == 'dn':
                    o, okey = of()
                    evac(o[:, :ntok], bank[:, :ntok], bkey, okey)
                    dst = (S['dnT'][c * 128:(c + 1) * 128, t0:t0 + ntok] if isx else S['dncT'][c * 128:(c + 1) * 128, :])
                else:
                    o, okey = ob()
                    evac(o[:, :ntok], bank[:, :ntok], bkey, okey, scale=(0.125 if kind == 'q' else None))
                    dst = (S['qT'][c * 128:(c + 1) * 128, t0:t0 + ntok] if kind == 'q' else S['kT'][c * 128:(c + 1) * 128, t0:t0 + ntok])
                P.dma(dq(), dst, o[:, :ntok], reads=[okey], writes=[('dram', kind, c, t0)])
            for s in range(ntok // 128):
                tk0 = t0 + s * 128
                groups = [('v', 1024, 512), ('ab', OFF_AB, 16)]
                if isx:
                    groups += [('z0', OFF_Z, 512), ('z1', OFF_Z + 512, 512)]
                for (kind, col0, ncol) in groups:
                    bank, bkey = pbank()
                    for k in range(8):
                        P.op('pe', lambda e, k=k, bank=bank, col0=col0, ncol=ncol, s=s: e.matmul(bank[:, :ncol], hT[:, k, s * 128:(s + 1) * 128], W[:, k, col0:col0 + ncol], start=(k == 0), stop=(k == 7)),
                             reads=wkeys + hkeys, writes=[bkey])
                    if kind == 'ab':
                        o, okey = of()
                        evac(o[:, :16], bank[:, :16], bkey, okey)
                        P.dma(dq(), S['abtok'][tk0:tk0 + 128, :], o[:, :16], reads=[okey], writes=[('dram', 'ab', tk0)])
                    elif kind == 'v':
                        o, okey = ob()
                        evac(o[:, :], bank[:, :], bkey, okey)
                        P.dma(dq(), S['vtok'][tk0:tk0 + 128, :], o[:, :], reads=[okey], writes=[('dram', 'v', tk0)])
                    else:
                        zc = 0 if kind == 'z0' else 512
                        o, okey = ob()
                        evac(o[:, :], bank[:, :], bkey, okey, func=AF.Silu)
                        P.dma(dq(), S['ztok'][tk0:tk0 + 128, zc:zc + 512], o[:, :], reads=[okey], writes=[('dram', 'z', tk0, zc)])
        for tl in tiles:
            do_tile(*tl)
    P.barrier()


NEG = -30000.0

def na_bias_table(rpb):
    out = np.full((5, 128, 8, 5, 128), NEG, np.float32)
    blocks = [0, 1, 2, 30, 31]
    for vi, i in enumerate(blocks):
        cs = min(max(i - 2, 0), 27)
        for ql in range(128):
            r = 2 * i + ql // 64; qc = ql % 64
            r0 = min(max(r - 4, 0), 56); c0 = min(max(qc - 8, 0), 48)
            for ch in range(5):
                for kr_l in range(2):
                    kr = 2 * (cs + ch) + kr_l
                    if not (r0 <= kr < r0 + 8):
                        continue
                    kcs = np.arange(c0, c0 + 16)
                    out[vi, kr_l * 64 + kcs, :, ch, ql] = rpb[:, kr - r + 7, kcs - qc + 15].T
    return out

def variant_of(i):
    return {0: 0, 1: 1, 30: 3, 31: 4}.get(i, 2)


def phase_na(P, T, S):
    with ExitStack() as st:
        kT = P.sb('kT', [128, 4, 4352], BF16, st)
        qT = P.sb('qT', [128, 4, 4096], BF16, st)
        va = P.sb('va', [128, 34, 8, 65], BF16, st)
        bt = P.sb('bt', [128, 5, 8 * 5 * 128], BF16, st)
        btf = P.sb('btf', [128, 8 * 5 * 128], F32, st)
        ident = P.sb('ident', [128, 128], BF16, st)
        identf = P.sb('identf', [128, 128], F32, st)
        pts = [(P.sb('pt%d' % i, [128, 896], BF16, st), 'pt%d' % i) for i in range(2)]
        zts = [(P.sb('zt%d' % i, [128, 512], BF16, st), 'zt%d' % i) for i in range(2)]
        nas = [(P.sb('na%d' % i, [128, 8, 64], F32, st), 'na%d' % i) for i in range(2)]
        mxs = [(P.sb('mx%d' % i, [128, 512], BF16, st), 'mx%d' % i) for i in range(2)]
        rcs = [(P.sb('rc%d' % i, [128, 8], F32, st), 'rc%d' % i) for i in range(2)]
        sA = [(P.ps('sA%d' % i, [128, 512], F32, st), 'sA%d' % i) for i in range(2)]
        sB = [(P.ps('sB%d' % i, [128, 512], F32, st), 'sB%d' % i) for i in range(2)]
        oC = [(P.ps('oC%d' % i, [128, 512], F32, st), 'oC%d' % i) for i in range(4)]
        for hp in range(4):
            P.dma('sp', kT[:, hp, :], S['kT'][hp * 128:(hp + 1) * 128, :], writes=['kT'])
            P.dma('pool', qT[:, hp, :], S['qT'][hp * 128:(hp + 1) * 128, :], writes=['qT'])
        P.op('pool', lambda e: e.memset(va[:], 1.0), writes=['va'])
        vsrc = S['vtok'].rearrange("(c p) f -> p c f", p=128)
        for h in range(8):
            for (c0, c1) in ((0, 17), (17, 34)):
                P.dma('sp' if h % 2 == 0 else 'act', va[:, c0:c1, h, 0:64], vsrc[:, c0:c1, h * 64:(h + 1) * 64], writes=['va'])
        P.dma('sp', identf[:], T['ident'], writes=['identf'])
        P.op('pool', lambda e: e.tensor_copy(ident[:], identf[:]), reads=['identf'], writes=['ident'])
        for v in range(5):
            P.dma('sp', btf[:], T['na_bias'][v], writes=['btf'])
            P.op('act', lambda e, v=v: e.copy(bt[:, v, :], btf[:]), reads=['btf'], writes=['bt'])
        pt_n = rr(pts); zt_n = rr(zts); na_n = rr(nas); mx_n = rr(mxs); rc_n = rr(rcs)
        sA_n = rr(sA); sB_n = rr(sB); oC_n = rr(oC)

        blk = {}

        def stage1(i, h):
            cs = min(max(i - 2, 0), 27)
            v = variant_of(i)
            chunks = [cs + c for c in range(5)] + [32, 33]
            q0 = i * 128
            if h == 0:
                zt, ztk = zt_n()
                P.dma('act', zt[:], S['ztok'][q0:q0 + 128, 0:512], writes=[ztk])
                blk[i] = dict(zt=(zt, ztk), ocs=[oC_n(), oC_n()])
            hp, hb = h // 2, (h % 2) * 64
            a, ak = sA_n(); b, bk = sB_n()
            for ci, ch in enumerate(chunks):
                bank, bkk = (a, ak) if ci < 4 else (b, bk)
                col = (ci % 4) * 128
                has_bias = ci < 5
                P.op('pe', lambda e, bank=bank, col=col, ch=ch, has_bias=has_bias: e.matmul(
                    bank[:, col:col + 128], kT[hb:hb + 64, hp, ch * 128:(ch + 1) * 128], qT[hb:hb + 64, hp, q0:q0 + 128], start=True, stop=not has_bias),
                    reads=['kT', 'qT'], writes=[bkk])
                if has_bias:
                    off = (h * 5 + ci) * 128
                    P.op('pe', lambda e, bank=bank, col=col, off=off: e.matmul(bank[:, col:col + 128], ident[:], bt[:, v, off:off + 128], start=False, stop=True),
                         reads=['ident', 'bt'], writes=[bkk])
            pt, ptk = pt_n()
            P.op('act', lambda e: e.activation(pt[:, 0:512], a[:, :], AF.Exp), writes=[ak, (ptk, 0)])
            P.op('act', lambda e: e.activation(pt[:, 512:896], b[:, 0:384], AF.Exp), writes=[bk, (ptk, 1)])
            return dict(i=i, h=h, chunks=chunks, pt=pt, ptk=ptk, q0=q0)

        def stage2(c):
            i, h, chunks, pt, ptk, q0 = c['i'], c['h'], c['chunks'], c['pt'], c['ptk'], c['q0']
            ocs = blk[i]['ocs']
            oc, ock = ocs[h // 4]
            hh = h % 4
            for ci, ch in enumerate(chunks):
                P.op('pe', lambda e, ci=ci, ch=ch: e.matmul(oc[:, hh * 65:hh * 65 + 65], pt[:, ci * 128:(ci + 1) * 128], va[:, ch, h, :], start=(ci == 0), stop=(ci == 6)),
                     reads=[(ptk, 0), (ptk, 1), 'va'], writes=[ock])
            if h < 7:
                return
            zt, ztk = blk[i]['zt']
            na, nak = na_n(); rc, rck = rc_n(); mx, mxk = mx_n()
            for g in range(2):
                ocg, ocgk = ocs[g]
                ocv = ocg[:, 0:260].rearrange("p (h d) -> p h d", d=65)
                P.op('dve', lambda e, ocv=ocv, g=g: e.reciprocal(rc[:, g * 4:(g + 1) * 4], ocv[:, :, 64]), writes=[ocgk, (rck, g)])
                for h2 in range(4):
                    P.op('dve', lambda e, ocv=ocv, g=g, h2=h2: e.tensor_scalar(na[:, g * 4 + h2, :], ocv[:, h2, 0:64], rc[:, g * 4 + h2:g * 4 + h2 + 1], None, ALU.mult),
                         reads=[(rck, g)], writes=[ocgk, (nak, g)])
            P.op('pool', lambda e: e.tensor_tensor(mx[:], na[:].rearrange("p h d -> p (h d)"), zt[:], ALU.mult), reads=[(nak, 0), (nak, 1), ztk], writes=[mxk])
            P.dma('sp', S['mixtok'][q0:q0 + 128, 0:512], mx[:], reads=[mxk], writes=[('dram', 'mix', i)])
            if 'na_dbg' in S:
                P.dma('sp', S['na_dbg'][q0:q0 + 128, :], na[:].rearrange("p h d -> p (h d)"), reads=[(nak, 0), (nak, 1)], writes=[('dram', 'nadbg', i)])

        items = [(i, h) for i in range(32) for h in range(8)]
        prev = None
        for (i, h) in items:
            cur = stage1(i, h)
            if prev is not None:
                stage2(prev)
            prev = cur
        stage2(prev)
    P.barrier()


def phase_dn_gb(P, T, S, pers):
    with ExitStack() as st:
        ab = P.sb('ab', [128, 34, 16], F32, st)
        xa = P.sb('xa', [128, 34, 8], F32, st)
        t1 = P.sb('t1', [128, 34, 8], F32, st)
        t2 = P.sb('t2', [128, 34, 8], F32, st)
        al = P.sb('al', [128, 8], F32, st)
        dtb = P.sb('dtb', [128, 8], F32, st)
        g, beta = pers['g'], pers['beta']
        P.dma('sp', ab[:], S['abtok'].rearrange("(c p) f -> p c f", p=128), writes=['ab'])
        P.dma('sp', al[:], T['e_dn_a_log'].partition_broadcast(128), writes=['al'])
        P.dma('sp', dtb[:], T['e_dn_dt_bias'].partition_broadcast(128), writes=['dtb'])
        P.op('act', lambda e: e.activation(al[:], al[:], AF.Exp), writes=['al'])
        P.op('dve', lambda e: e.tensor_tensor(xa[:], ab[:, :, 0:8], dtb[:].unsqueeze(1).to_broadcast([128, 34, 8]), ALU.add), reads=['ab', 'dtb'], writes=['xa'])
        P.op('act', lambda e: e.activation(t1[:], xa[:], AF.Abs), reads=['xa'], writes=['t1'])
        P.op('act', lambda e: e.activation(t1[:], t1[:], AF.Exp, scale=-1.0), writes=['t1'])
        P.op('dve', lambda e: e.tensor_scalar_add(t1[:], t1[:], 1.0), writes=['t1'])
        P.op('act', lambda e: e.activation(t1[:], t1[:], AF.Ln), writes=['t1'])
        P.op('dve', lambda e: e.tensor_scalar_max(t2[:], xa[:], 0.0), reads=['xa'], writes=['t2'])
        P.op('dve', lambda e: e.tensor_tensor(t2[:], t2[:], t1[:], ALU.add), reads=['t1'], writes=['t2'])
        P.op('dve', lambda e: e.scalar_tensor_tensor(g[:], t2[:], -1.0, al[:].unsqueeze(1).to_broadcast([128, 34, 8]), ALU.mult, ALU.mult), reads=['t2', 'al'], writes=['g'])
        P.op('act', lambda e: e.activation(beta[:], ab[:, :, 8:16], AF.Sigmoid), reads=['ab'], writes=['beta'])
    P.barrier()


def phase_dn_prep(P, T, S):
    with ExitStack() as st:
        cw = P.sb('cw', [128, 60], F32, st)
        ones = P.sb('ones', [128, 128], F32, st)
        ident = P.sb('ident', [128, 128], BF16, st)
        identf = P.sb('identf', [128, 128], F32, st)
        raws = [(P.sb('raw%d' % i, [128, 4100], F32, st), 'raw%d' % i) for i in range(2)]
        accs = [(P.sb('acc%d' % i, [128, 4096], F32, st), 'acc%d' % i) for i in range(2)]
        sq = P.sb('sq', [128, 4096], F32, st)
        rns = [(P.sb('rn%d' % i, [128, 4096], F32, st), 'rn%d' % i) for i in range(2)]
        rn_n = rr(rns)
        obs = [(P.sb('obf%d' % i, [128, 4096], BF16, st), 'obf%d' % i) for i in range(2)]
        tks = [(P.sb('tk%d' % i, [128, 8, 128], BF16, st), 'tk%d' % i) for i in range(2)]
        banks = [(P.ps('nb%d' % i, [128, 512], F32, st), 'nb%d' % i) for i in range(2)]
        tps = [(P.ps('tpd%d' % i, [128, 1024], BF16, st), 'tpd%d' % i) for i in range(2)]
        P.dma('sp', cw[:], T['e_dn_conv'], writes=['cw'])
        P.op('pool', lambda e: e.memset(ones[:], 1.0), writes=['ones'])
        P.dma('sp', identf[:], T['ident'], writes=['identf'])
        P.op('pool', lambda e: e.tensor_copy(ident[:], identf[:]), reads=['identf'], writes=['ident'])
        for (r, rk) in raws:
            P.op('pool', lambda e, r=r: e.memset(r[:], 0.0), writes=[rk])
        raw_n = rr(raws); acc_n = rr(accs); ob_n = rr(obs); tk_n = rr(tks); bk_n = rr(banks); tp_n = rr(tps)
        dq = rr(['sp', 'pool'])

        def do_chunk(c, src, Lt, tok0):
            kind = c // 4; h = c % 4
            raw, rk = raw_n(); acc, ak = acc_n(); ob, obk = ob_n()
            rn, rnk = rn_n()
            if Lt < 4096:
                P.op('pool', lambda e: e.memset(raw[:, 2 + Lt:4 + Lt], 0.0), writes=[rk])
            P.dma(dq(), raw[:, 2:2 + Lt], src[c * 128:(c + 1) * 128, :], writes=[rk])
            P.op('dve', lambda e: e.tensor_scalar(acc[:, :Lt], raw[:, 0:Lt], cw[:, c * 5:c * 5 + 1], None, ALU.mult), reads=[rk, 'cw'], writes=[ak])
            for j in range(1, 5):
                P.op('dve', lambda e, j=j: e.scalar_tensor_tensor(acc[:, :Lt], raw[:, j:j + Lt], cw[:, c * 5 + j:c * 5 + j + 1], acc[:, :Lt], ALU.mult, ALU.add), reads=[rk, 'cw'], writes=[ak])
            P.op('act', lambda e: e.activation(acc[:, :Lt], acc[:, :Lt], AF.Silu), writes=[ak])
            if kind < 2:
                P.op('act', lambda e: e.activation(sq[:, :Lt], acc[:, :Lt], AF.Square), reads=[ak], writes=['sq'])
                for t0 in range(0, Lt, 512):
                    n = min(512, Lt - t0)
                    bank, bk = bk_n()
                    P.op('pe', lambda e, t0=t0, n=n, bank=bank: e.matmul(bank[:, :n], ones[:], sq[:, t0:t0 + n], start=True, stop=True), reads=['sq', 'ones'], writes=[bk])
                    P.op('dve', lambda e, t0=t0, n=n, bank=bank: e.tensor_scalar_add(rn[:, t0:t0 + n], bank[:, :n], 1e-6), writes=[bk, (rnk, t0), rnk])
                rkeys = [(rnk, t0) for t0 in range(0, Lt, 512)]
                yield
                P.op('act', lambda e: e.activation(rn[:, :Lt], rn[:, :Lt], AF.Sqrt), writes=rkeys + [rnk])
                P.op('dve', lambda e: e.reciprocal(rn[:, :Lt], rn[:, :Lt]), writes=[rnk])
                sc = (128 ** -0.5) if kind == 0 else 1.0
                P.op('dve', lambda e: e.scalar_tensor_tensor(ob[:, :Lt], acc[:, :Lt], sc, rn[:, :Lt], ALU.mult, ALU.mult), reads=[ak, rnk], writes=[obk])
                dst = S['dQT'] if kind == 0 else S['dKT']
                P.dma(dq(), dst[h, :, tok0:tok0 + Lt], ob[:, :Lt], reads=[obk], writes=[('dram', 'qk', c, tok0)])
            else:
                yield
                P.op('act', lambda e: e.copy(ob[:, :Lt], acc[:, :Lt]), reads=[ak], writes=[obk])
            if kind >= 1:
                dst = S['dKtok'] if kind == 1 else S['dVtok']
                for g0 in range(0, Lt // 128, 8):
                    ng = min(8, Lt // 128 - g0)
                    tp, tpk = tp_n(); tk, tkk = tk_n()
                    for s in range(ng):
                        P.op('pe', lambda e, s=s, g0=g0, tp=tp: e.transpose(tp[:, s * 128:(s + 1) * 128], ob[:, (g0 + s) * 128:(g0 + s + 1) * 128], ident[:]), reads=[obk, 'ident'], writes=[tpk])
                    P.op('act', lambda e, tp=tp, tk=tk, ng=ng: e.copy(tk[:, :ng, :], tp[:, :ng * 128].rearrange("p (s d) -> p s d", d=128)), writes=[tpk, tkk])
                    ta = tok0 + g0 * 128
                    P.dma(dq(), dst[ta:ta + ng * 128, h, :].rearrange("(s p) d -> p s d", p=128), tk[:, :ng, :], reads=[tkk], writes=[('dram', 'tok', c, ta)])

        items = [(c, S['dnT'], 4096, 0) for c in range(12)] + [(c, S['dncT'], 256, 4096) for c in range(12)]
        pend = None
        for it in items:
            g_ = do_chunk(*it)
            next(g_)
            if pend is not None:
                for _ in pend:
                    pass
            pend = g_
        for _ in pend:
            pass
    P.barrier()


def dn_masks():
    j = np.arange(128)[:, None]; t = np.arange(128)[None, :]
    same = (j // 64) == (t // 64)
    m = np.zeros((8, 128, 128), np.float32)
    m[0] = same & (j <= t)
    m[1] = same & (j >= t)
    m[2] = same & (t < j)
    m[3] = same & (t > j)
    m[4] = (j // 64 == 0) * np.ones((1, 128))
    m[5] = (j // 64 == 1) * np.ones((1, 128))
    m[6] = 1.0
    m[7] = np.eye(128)
    return m


def phase_dn_main(P, T, S, pers):
    g_all, b_all = pers['g'], pers['beta']
    with ExitStack() as st:
        msk = P.sb('msk', [128, 8, 128], F32, st)
        identb = P.sb('identb', [128, 128], BF16, st)
        P.dma('sp', msk[:], T['dn_masks'].rearrange("m p f -> p m f"), writes=['msk'])
        P.op('pool', lambda e: e.tensor_copy(identb[:], msk[:, 7, :]), reads=['msk'], writes=['identb'])
        TRI = [msk[:, 0, :], msk[:, 1, :]]; BM = [msk[:, 2, :], msk[:, 3, :]]; CH = [msk[:, 4, :], msk[:, 5, :]]
        ONES = msk[:, 6, :]; IDF = msk[:, 7, :]

        def bc_h(ap2):
            return ap2.unsqueeze(1).to_broadcast([128, 4, 128])

        def bc_l(ap2):
            return ap2.unsqueeze(2).to_broadcast([128, 4, 128])

        NS = 2
        def mk(name, dt, n=NS, shape=(128, 4, 128)):
            return [[(P.sb('%s%d_%d' % (name, d, i), list(shape), dt, st), '%s%d_%d' % (name, d, i)) for i in range(n)] for d in range(2)]
        QTt = mk('QTt', BF16); KTt = mk('KTt', BF16); Ktk = mk('Ktk', BF16); Vtk = mk('Vtk', BF16)
        Ab = mk('A', F32); Dm = mk('Dm', F32); DTm = mk('DTm', F32); EG = mk('EG', F32)
        Lb = mk('L', F32, 3); Nb = mk('N', F32, 3); XTf = mk('XT', F32, 3)
        XTb = mk('XTb', BF16); vb = mk('vb', BF16); kbg = mk('kbg', BF16); Kd = mk('Kd', BF16)
        Ub = mk('U', F32); WTb = mk('WT', BF16); Aqk = mk('Aqk', BF16); Qg = mk('Qg', BF16)
        stt_ = mk('st', F32, NS, (128, 16)); bgb = mk('bg', F32, NS, (128, 4)); tmpb = mk('tmp', F32)
        vnb = mk('vn', BF16, 1); ob = mk('o', F32, 2)
        Sf = [(P.sb('S%d' % d, [128, 4, 128], F32, st), 'S%d' % d) for d in range(2)]
        Sb = [(P.sb('Sb%d' % d, [128, 4, 128], BF16, st), 'Sb%d' % d) for d in range(2)]
        banks = [(P.ps('db%d' % i, [128, 512], F32, st), 'db%d' % i) for i in range(8)]
        bk_n = rr(banks)
        ctr = {}
        def nxt(pool, d):
            k = (id(pool), d)
            i = ctr.get(k, 0); ctr[k] = i + 1
            lst = pool[d]
            return lst[i % len(lst)]
        for d in range(2):
            P.op('pool', lambda e, d=d: e.memset(Sf[d][0][:], 0.0), writes=[Sf[d][1]])
            P.op('pool', lambda e, d=d: e.memset(Sb[d][0][:], 0.0), writes=[Sb[d][1]])
            P.op('pool', lambda e, d=d: e.memset(vnb[d][0][0][:], 0.0), writes=[vnb[d][0][1]])
        dq = rr(['sp', 'pool'])

        def prepass(tile, d, res):
            t0 = tile * 128
            R = {}
            dbgi = [0]
            def dbg(ap3, key):
                if 'dbg' in S and tile == S['dbg_tile'] and d == S['dbg_d']:
                    P.dma('sp', S['dbg'][dbgi[0]], ap3.rearrange("p h t -> p (h t)"), reads=[key], writes=[('dram', 'dbg', dbgi[0])])
                dbgi[0] += 1
            qt, qtk = nxt(QTt, d); kt, ktk = nxt(KTt, d); ktok, ktokk = nxt(Ktk, d); vtok, vtokk = nxt(Vtk, d)
            P.dma(dq(), qt[:], S['dQT'][:, :, t0:t0 + 128].rearrange("h p t -> p h t"), writes=[qtk])
            P.dma(dq(), kt[:], S['dKT'][:, :, t0:t0 + 128].rearrange("h p t -> p h t"), writes=[ktk])
            P.dma(dq(), ktok[:], S['dKtok'][t0:t0 + 128, :, :], writes=[ktokk])
            P.dma(dq(), vtok[:], S['dVtok'][t0:t0 + 128, :, :], writes=[vtokk])
            gt = g_all[:, tile, d * 4:(d + 1) * 4]; bt = b_all[:, tile, d * 4:(d + 1) * 4]
            z, zk = bk_n()
            P.op('pe', lambda e: e.matmul(z[:, 0:4], TRI[d], gt, start=True, stop=True), reads=['msk', 'g'], writes=[zk])
            P.op('pe', lambda e: e.matmul(z[:, 4:8], BM[d], gt, start=True, stop=True), reads=['msk', 'g'], writes=[zk])
            P.op('pe', lambda e: e.matmul(z[:, 8:12], CH[0], gt, start=True, stop=True), reads=['msk', 'g'], writes=[zk])
            P.op('pe', lambda e: e.matmul(z[:, 12:16], CH[1], gt, start=True, stop=True), reads=['msk', 'g'], writes=[zk])
            stv, stk = nxt(stt_, d)
            P.op('act', lambda e: e.activation(stv[:], z[:, 0:16], AF.Exp), writes=[zk, stk])
            yield
            bg, bgk = nxt(bgb, d)
            P.op('dve', lambda e: e.tensor_tensor(bg[:], bt, stv[:, 0:4], ALU.mult), reads=['beta', stk], writes=[bgk])
            A, Ak = nxt(Ab, d)
            for h in range(4):
                P.op('dve', lambda e, h=h: e.tensor_scalar(A[:, h, :], TRI[d], gt[:, h:h + 1], None, ALU.mult), reads=['msk', 'g'], writes=[Ak])
            d1, d1k = bk_n(); d2, d2k = bk_n(); d3, d3k = bk_n()
            for h in range(4):
                P.op('pe', lambda e, h=h: e.matmul(d1[:, h * 128:(h + 1) * 128], A[:, h, :], BM[d], start=True, stop=True), reads=[Ak, 'msk'], writes=[d1k])
            for h in range(4):
                P.op('pe', lambda e, h=h: e.matmul(d2[:, h * 128:(h + 1) * 128], BM[d], A[:, h, :], start=True, stop=True), reads=[Ak, 'msk'], writes=[d2k])
            for h in range(4):
                P.op('pe', lambda e, h=h: e.matmul(d3[:, h * 128:(h + 1) * 128], ONES, A[:, h, :], start=True, stop=True), reads=[Ak, 'msk'], writes=[d3k])
            dm, dmk = nxt(Dm, d); dtm, dtmk = nxt(DTm, d); eg, egk = nxt(EG, d)
            f3 = "p (h t) -> p h t"
            P.op('act', lambda e: e.activation(dm[:], d1[:].rearrange(f3, h=4), AF.Exp), writes=[d1k, dmk])
            P.op('act', lambda e: e.activation(dtm[:], d2[:].rearrange(f3, h=4), AF.Exp), writes=[d2k, dtmk])
            P.op('act', lambda e: e.activation(eg[:], d3[:].rearrange(f3, h=4), AF.Exp), writes=[d3k, egk])
            yield
            P.op('pool', lambda e: e.tensor_tensor(dm[:], dm[:], bc_h(BM[d]), ALU.mult), reads=['msk'], writes=[dmk])
            P.op('pool', lambda e: e.tensor_tensor(dtm[:], dtm[:], bc_h(TRI[d]), ALU.mult), reads=['msk'], writes=[dtmk])
            e1, e1k = bk_n(); e2, e2k = bk_n()
            for h in range(4):
                P.op('pe', lambda e, h=h: e.matmul(e1[:, h * 128:(h + 1) * 128], kt[:, h, :], kt[:, h, :], start=True, stop=True), reads=[ktk], writes=[e1k])
            for h in range(4):
                P.op('pe', lambda e, h=h: e.matmul(e2[:, h * 128:(h + 1) * 128], kt[:, h, :], qt[:, h, :], start=True, stop=True), reads=[ktk, qtk], writes=[e2k])
            tmp, tmpk = nxt(tmpb, d)
            L0, L0k = nxt(Lb, d)
            P.op('dve', lambda e: e.tensor_tensor(tmp[:], e1[:].rearrange(f3, h=4), dm[:], ALU.mult), reads=[dmk], writes=[e1k, tmpk])
            for h in range(4):
                P.op('dve', lambda e, h=h: e.tensor_scalar(L0[:, h, :], tmp[:, h, :], bt[:, h:h + 1], None, ALU.mult), reads=[tmpk, 'beta'], writes=[L0k])
            dbg(L0[:], L0k)
            aq, aqk = nxt(Aqk, d)
            P.op('dve', lambda e: e.tensor_tensor(aq[:], e2[:].rearrange(f3, h=4), dtm[:], ALU.mult), reads=[dtmk], writes=[e2k, aqk])
            yield
            qg, qgk = nxt(Qg, d)
            P.op('pool', lambda e: e.tensor_tensor(qg[:], qt[:], eg[:], ALU.mult), reads=[qtk, egk], writes=[qgk])
            tb, tbk = bk_n()
            for h in range(4):
                P.op('pe', lambda e, h=h: e.transpose(tb[:, h * 128:(h + 1) * 128], L0[:, h, :], IDF), reads=[L0k, 'msk'], writes=[tbk])
            N0, N0k = nxt(Nb, d)
            P.op('act', lambda e: e.copy(N0[:], tb[:].rearrange(f3, h=4)), writes=[tbk, N0k])
            yield
            XT, XTk = nxt(XTf, d)
            for h in range(4):
                P.op('dve', lambda e, h=h, XT=XT: e.scalar_tensor_tensor(XT[:, h, :], N0[:, h, :], -1.0, IDF, ALU.mult, ALU.add), reads=['msk', N0k], writes=[XTk])
            dbg(N0[:], N0k)
            dbg(XT[:], XTk)
            Lp, Lpk, Np, Npk = L0, L0k, N0, N0k
            for lev in range(1, 6):
                f1, f1k = bk_n()
                for h in range(4):
                    P.op('pe', lambda e, h=h, f1=f1, Lp=Lp, Np=Np: e.matmul(f1[:, h * 128:(h + 1) * 128], Np[:, h, :], Lp[:, h, :], start=True, stop=True), reads=[Lpk, Npk], writes=[f1k])
                Ln, Lnk = nxt(Lb, d)
                P.op('act', lambda e, f1=f1, Ln=Ln: e.copy(Ln[:], f1[:].rearrange(f3, h=4)), writes=[f1k, Lnk])
                if lev < 5:
                    f2, f2k = bk_n()
                    for h in range(4):
                        P.op('pe', lambda e, h=h, f2=f2, Lp=Lp, Np=Np: e.matmul(f2[:, h * 128:(h + 1) * 128], Lp[:, h, :], Np[:, h, :], start=True, stop=True), reads=[Lpk, Npk], writes=[f2k])
                    Nn, Nnk = nxt(Nb, d)
                    P.op('dve', lambda e, f2=f2, Nn=Nn: e.tensor_copy(Nn[:], f2[:].rearrange(f3, h=4)), writes=[f2k, Nnk])
                yield
                f3b, f3k = bk_n()
                for h in range(4):
                    P.op('pe', lambda e, h=h, f3b=f3b, Ln=Ln, XT=XT: e.matmul(f3b[:, h * 128:(h + 1) * 128], Ln[:, h, :], XT[:, h, :], start=True, stop=True), reads=[Lnk, XTk], writes=[f3k])
                XTn, XTnk = nxt(XTf, d)
                P.op('dve', lambda e, f3b=f3b, XT=XT, XTn=XTn: e.tensor_tensor(XTn[:], f3b[:].rearrange(f3, h=4), XT[:], ALU.add), reads=[XTk], writes=[f3k, XTnk])
                XT, XTk = XTn, XTnk
                yield
                if lev == 1:
                    dbg(Ln[:], Lnk)
                    dbg(XT[:], XTk)
                Lp, Lpk = Ln, Lnk
                if lev < 5:
                    Np, Npk = Nn, Nnk
            xtb, xtbk = nxt(XTb, d)
            P.op('act', lambda e: e.copy(xtb[:], XT[:]), reads=[XTk], writes=[xtbk])
            vbt, vbk = nxt(vb, d); kb, kbk = nxt(kbg, d); kd, kdk = nxt(Kd, d)
            for h in range(4):
                P.op('act', lambda e, h=h: e.activation(vbt[:, h, :], vtok[:, h, :], AF.Copy, scale=bt[:, h:h + 1]), reads=[vtokk, 'beta'], writes=[vbk])
                P.op('dve', lambda e, h=h: e.tensor_scalar(kb[:, h, :], ktok[:, h, :], bg[:, h:h + 1], None, ALU.mult), reads=[ktokk, bgk], writes=[kbk])
                P.op('act', lambda e, h=h: e.activation(kd[:, h, :], ktok[:, h, :], AF.Copy, scale=stv[:, 4 + h:5 + h]), reads=[ktokk, stk], writes=[kdk])
            g1, g1k = bk_n(); g2, g2k = bk_n()
            for h in range(4):
                P.op('pe', lambda e, h=h: e.matmul(g1[:, h * 128:(h + 1) * 128], xtb[:, h, :], vbt[:, h, :], start=True, stop=True), reads=[xtbk, vbk], writes=[g1k])
            for h in range(4):
                P.op('pe', lambda e, h=h: e.matmul(g2[:, h * 128:(h + 1) * 128], kb[:, h, :], xtb[:, h, :], start=True, stop=True), reads=[xtbk, kbk], writes=[g2k])
            U, Uk = nxt(Ub, d); WT, WTk = nxt(WTb, d)
            P.op('act', lambda e: e.copy(U[:], g1[:].rearrange(f3, h=4)), writes=[g1k, Uk])
            P.op('dve', lambda e: e.tensor_copy(WT[:], g2[:].rearrange(f3, h=4)), writes=[g2k, WTk])
            dbg(XT[:], XTk)
            dbg(U[:], Uk)
            res.update(U=(U, Uk), WT=(WT, WTk), aq=(aq, aqk), qg=(qg, qgk), kd=(kd, kdk), st=(stv, stk), tile=tile)

        def scan(pp, d, want_out):
            U, Uk = pp['U']; WT, WTk = pp['WT']; aq, aqk = pp['aq']; qg, qgk = pp['qg']; kd, kdk = pp['kd']; stv, stk = pp['st']
            tile = pp['tile']
            Sfd, Sfk = Sf[d]; Sbd, Sbk = Sb[d]
            vn, vnk = vnb[d][0]
            f3 = "p (h t) -> p h t"
            if want_out:
                o, ok = nxt(ob, d)
            def chunk_step(c):
                r0 = 64 * c
                h1, h1k = bk_n()
                for h in range(4):
                    P.op('pe', lambda e, h=h: e.matmul(h1[:, h * 128:(h + 1) * 128], WT[:, h, :], Sbd[:, h, :], start=True, stop=True), reads=[WTk, Sbk], writes=[h1k])
                P.op('dve', lambda e, r0=r0: e.tensor_tensor(vn[r0:r0 + 64, :, :], U[r0:r0 + 64, :, :], h1[r0:r0 + 64, :].rearrange(f3, h=4), ALU.subtract), reads=[Uk], writes=[h1k, vnk])
                yield
                if want_out:
                    h2, h2k = bk_n()
                    for h in range(4):
                        P.op('pe', lambda e, h=h: e.matmul(h2[:, h * 128:(h + 1) * 128], qg[:, h, :], Sbd[:, h, :], start=True, stop=False), reads=[qgk, Sbk], writes=[h2k])
                        P.op('pe', lambda e, h=h, r0=r0: e.matmul(h2[:, h * 128:(h + 1) * 128], aq[r0:r0 + 64, h, :], vn[r0:r0 + 64, h, :], start=False, stop=True), reads=[aqk, vnk], writes=[h2k])
                    P.op('act', lambda e, r0=r0: e.copy(o[r0:r0 + 64, :, :], h2[r0:r0 + 64, :].rearrange(f3, h=4)), writes=[h2k, ok])
                h3, h3k = bk_n()
                for h in range(4):
                    P.op('pe', lambda e, h=h, r0=r0: e.matmul(h3[:, h * 128:(h + 1) * 128], kd[r0:r0 + 64, h, :], vn[r0:r0 + 64, h, :], start=True, stop=True), reads=[kdk, vnk], writes=[h3k])
                for h in range(4):
                    P.op('dve', lambda e, c=c, h=h: e.scalar_tensor_tensor(Sfd[:, h, :], Sfd[:, h, :], stv[:, 8 + 4 * c + h:9 + 4 * c + h], h3[:, h * 128:(h + 1) * 128], ALU.mult, ALU.add), reads=[stk], writes=[h3k, Sfk])
                P.op('act', lambda e: e.copy(Sbd[:], Sfd[:]), reads=[Sfk], writes=[Sbk])
                yield
            for c in ([0, 1] if d == 0 else [1, 0]):
                yield from chunk_step(c)
            if want_out:
                t0 = tile * 128
                P.dma(dq(), S['dno'][d, t0:t0 + 128, :], o[:].rearrange("p h v -> p (h v)"), reads=[ok], writes=[('dram', 'dno', d, tile)])

        order = [[32, 33] + list(range(32)), [33, 32] + list(range(31, -1, -1))]
        nsteps = int(pers.get('dn_steps', 34))
        pp = {}
        def drive(gens):
            gens = list(gens)
            while gens:
                for g_ in list(gens):
                    try:
                        next(g_)
                    except StopIteration:
                        gens.remove(g_)

        for k in range(nsteps + 1):
            gens = []
            if k < nsteps:
                for d in range(2):
                    pp[(k, d)] = {}
                    gens.append(prepass(order[d][k], d, pp[(k, d)]))
            if k >= 1:
                for d in range(2):
                    gens.append(scan(pp[(k - 1, d)], d, order[d][k - 1] < 32))
            drive(gens)
        if 'dS' in S:
            for d in range(2):
                P.dma('sp', S['dS'][d], Sf[d][0][:].rearrange("p h v -> p (h v)"), reads=[Sf[d][1]], writes=[('dram', 'dS', d)])
    P.barrier()


def phase_dn_out(P, T, S):
    with ExitStack() as st:
        ng = P.sb('dng', [128, 128], F32, st)
        P.dma('sp', ng[:], T['e_dn_norm_g'].partition_broadcast(128), writes=['dng'])
        o0s = [(P.sb('oa%d' % i, [128, 4, 128], F32, st), 'oa%d' % i) for i in range(2)]
        o1s = [(P.sb('oc%d' % i, [128, 4, 128], F32, st), 'oc%d' % i) for i in range(2)]
        sqs = [(P.sb('osq%d' % i, [128, 4, 128], F32, st), 'osq%d' % i) for i in range(2)]
        zts = [(P.sb('oz%d' % i, [128, 4, 128], BF16, st), 'oz%d' % i) for i in range(2)]
        mxs = [(P.sb('om%d' % i, [128, 4, 128], BF16, st), 'om%d' % i) for i in range(2)]
        sss = [(P.sb('oss%d' % i, [128, 4], F32, st), 'oss%d' % i) for i in range(2)]
        o0n = rr(o0s); o1n = rr(o1s); sqn = rr(sqs); ztn = rr(zts); mxn = rr(mxs); ssn = rr(sss)

        def do_tile(i):
            t0 = i * 128
            a, ak = o0n(); b, bk = o1n(); sq, sk = sqn(); zt, zk = ztn(); mx, mk_ = mxn(); ss, ssk = ssn()
            P.dma('sp', a[:], S['dno'][0, t0:t0 + 128, :].rearrange("p (h v) -> p h v", h=4), writes=[ak])
            P.dma('pool', b[:], S['dno'][1, t0:t0 + 128, :].rearrange("p (h v) -> p h v", h=4), writes=[bk])
            P.dma('sp', zt[:], S['ztok'][t0:t0 + 128, 512:1024].rearrange("p (h v) -> p h v", h=4), writes=[zk])
            P.op('dve', lambda e: e.tensor_tensor(a[:], a[:], b[:], ALU.add), reads=[bk], writes=[ak])
            P.op('act', lambda e: e.activation(sq[:], a[:], AF.Square), reads=[ak], writes=[sk])
            P.op('dve', lambda e: e.tensor_reduce(ss[:], sq[:], AX.X, ALU.add), reads=[sk], writes=[ssk])
            P.op('dve', lambda e: e.tensor_scalar(ss[:], ss[:], 1.0 / 128, 1e-6, ALU.mult, ALU.add), writes=[ssk])
            P.op('act', lambda e: e.activation(ss[:], ss[:], AF.Sqrt), writes=[ssk])
            P.op('dve', lambda e: e.reciprocal(ss[:], ss[:]), writes=[ssk])
            for h in range(4):
                P.op('dve', lambda e, h=h: e.tensor_scalar(sq[:, h, :], a[:, h, :], ss[:, h:h + 1], None, ALU.mult), reads=[ak, ssk], writes=[sk])
            P.op('pool', lambda e: e.tensor_tensor(sq[:], sq[:], ng[:].unsqueeze(1).to_broadcast([128, 4, 128]), ALU.mult), reads=['dng'], writes=[sk])
            P.op('pool', lambda e: e.tensor_tensor(mx[:], sq[:], zt[:], ALU.mult), reads=[sk, zk], writes=[mk_])
            P.dma('sp', S['mixtok'][t0:t0 + 128, 512:1024], mx[:].rearrange("p h v -> p (h v)"), reads=[mk_], writes=[('dram', 'mixdn', i)])

        for i in range(32):
            do_tile(i)
    P.barrier()


def phase_out(P, T, S, pers, pre, wname, mix_is_tok, src_xT, dst_xT, final_out=None):
    with ExitStack() as st:
        W = P.sb('Wo', [128, 8, 1024], BF16, st)
        wstl = [(P.sb('wsto%d' % i, [128, 1024], F32, st), 'wsto%d' % i) for i in range(2)]
        ident = P.sb('ident', [128, 128], BF16, st)
        identf = P.sb('identf', [128, 128], F32, st)
        mts = [(P.sb('mt%d' % i, [128, 4, 1024], BF16, st), 'mt%d' % i) for i in range(2)]
        mTs = [(P.sb('mT%d' % i, [128, 8, 512], BF16, st), 'mT%d' % i) for i in range(2)]
        xts = [(P.sb('xo%d' % i, [128, 8, 512], F32, st), 'xo%d' % i) for i in range(2)]
        ots = [(P.sb('oo%d' % i, [128, 512], F32, st), 'oo%d' % i) for i in range(3)]
        tps = [(P.ps('tp%d' % i, [128, 1024], BF16, st), 'tp%d' % i) for i in range(2)]
        banks = [(P.ps('ob%d' % i, [128, 512], F32, st), 'ob%d' % i) for i in range(4)]
        if final_out is not None:
            xgs = [(P.sb('xg%d' % i, [128, 8, 512], F32, st), 'xg%d' % i) for i in range(2)]
            sqf = P.sb('sqfin', [128, 8, 512], F32, st)
            onesf = P.sb('onesf', [128, 128], F32, st)
            fg = P.sb('fgf', [128, 8], F32, st)
            rstdf = P.sb('rstdfin', [128, 512], F32, st)
            fbank = (P.ps('fbk', [128, 512], F32, st), 'fbk')
            P.op('pool', lambda e: e.memset(onesf[:], 1.0), writes=['onesf'])
            P.dma('sp', fg[:], T['final_norm_g'], writes=['fgf'])
            xg_n = rr(xgs)
            fo3 = final_out.rearrange("(j p) t -> p j t", p=128)
        load_weights_bf16(P, T[wname], W, rr(wstl), 1024, 'Wo')
        wkeys = [('Wo', k) for k in range(8)]
        P.dma('sp', identf[:], T['ident'], writes=['identf'])
        P.op('pool', lambda e: e.tensor_copy(ident[:], identf[:]), reads=['identf'], writes=['ident'])
        modv = pers['modv']
        mt_n = rr(mts); mT_n = rr(mTs); xt_n = rr(xts); ot_n = rr(ots); tp_n = rr(tps); bk_n = rr(banks)
        xs3 = src_xT.rearrange("(j p) t -> p j t", p=128)
        dq = rr(['sp', 'pool'])

        def do_group(gi):
            t0 = gi * 512
            mT, mTk = mT_n()
            if mix_is_tok:
                mt, mtk = mt_n()
                P.dma('sp', mt[:], S['mixtok'][t0:t0 + 512, :].rearrange("(s p) f -> p s f", p=128), writes=[mtk])
                for s in range(4):
                    tp, tpk = tp_n()
                    for k in range(8):
                        P.op('pe', lambda e, s=s, k=k, tp=tp: e.transpose(tp[:, k * 128:(k + 1) * 128], mt[:, s, k * 128:(k + 1) * 128], ident[:]),
                             reads=[mtk, 'ident'], writes=[tpk])
                    eng = 'act' if s % 2 == 0 else 'dve'
                    if eng == 'act':
                        P.op('act', lambda e, s=s, tp=tp: e.copy(mT[:, :, s * 128:(s + 1) * 128], tp[:].rearrange("p (k t) -> p k t", t=128)), writes=[tpk, (mTk, s)])
                    else:
                        P.op('dve', lambda e, s=s, tp=tp: e.tensor_copy(mT[:, :, s * 128:(s + 1) * 128], tp[:].rearrange("p (k t) -> p k t", t=128)), writes=[tpk, (mTk, s)])
                mkeys = [(mTk, s) for s in range(4)]
            else:
                P.dma('sp', mT[:], S['mixT'].rearrange("(k p) t -> p k t", p=128)[:, :, t0:t0 + 512], writes=[mTk])
                mkeys = [mTk]
            xt, xtk = xt_n()
            P.dma('pool', xt[:], xs3[:, :, t0:t0 + 512], writes=[xtk])
            if final_out is not None:
                xg, xgk = xg_n()
            for mc in range(8):
                bank, bkk = bk_n()
                for k in range(8):
                    P.op('pe', lambda e, k=k, mc=mc, bank=bank: e.matmul(bank[:], W[:, k, mc * 128:(mc + 1) * 128], mT[:, k, :], start=(k == 0), stop=(k == 7)),
                         reads=wkeys + mkeys, writes=[bkk])
                if final_out is None:
                    o, ok = ot_n()
                    P.op('dve', lambda e, mc=mc, bank=bank, o=o: e.scalar_tensor_tensor(o[:], bank[:], modv[:, 16 + mc, 0:1], xt[:, mc, :], ALU.mult, ALU.add),
                         reads=[xtk, ('modv', pre)], writes=[bkk, ok])
                    P.dma(dq(), dst_xT[mc * 128:(mc + 1) * 128, t0:t0 + 512], o[:], reads=[ok], writes=[('dram', 'xn', mc, gi)])
                else:
                    P.op('dve', lambda e, mc=mc, bank=bank: e.scalar_tensor_tensor(xg[:, mc, :], bank[:], modv[:, 16 + mc, 0:1], xt[:, mc, :], ALU.mult, ALU.add),
                         reads=[xtk, ('modv', pre)], writes=[bkk, (xgk, mc)] + ([xgk] if mc == 0 else []))
            if final_out is not None:
                xkeys = [(xgk, mc) for mc in range(8)]
                fb, fbk = fbank
                P.op('act', lambda e: e.activation(sqf[:], xg[:], AF.Square), reads=xkeys, writes=['sqfin'])
                for j in range(8):
                    P.op('pe', lambda e, j=j: e.matmul(fb[:], onesf[:], sqf[:, j, :], start=(j == 0), stop=(j == 7)), reads=['sqfin', 'onesf'], writes=[fbk])
                P.op('dve', lambda e: e.tensor_scalar(rstdf[:], fb[:], 1.0 / 1024, 1e-6, ALU.mult, ALU.add), writes=[fbk, 'rstdfin'])
                P.op('act', lambda e: e.activation(rstdf[:], rstdf[:], AF.Sqrt), writes=['rstdfin'])
                P.op('dve', lambda e: e.reciprocal(rstdf[:], rstdf[:]), writes=['rstdfin'])
                for j in range(8):
                    P.op('dve', lambda e, j=j: e.scalar_tensor_tensor(sqf[:, j, :], xg[:, j, :], fg[:, j:j + 1], rstdf[:], ALU.mult, ALU.mult),
                         reads=xkeys + ['rstdfin', 'fgf'], writes=['sqfin'])
                P.dma(dq(), fo3[:, :, t0:t0 + 512], sqf[:], reads=['sqfin'], writes=[('dram', 'fin', gi), xgk] + xkeys)

        for gi in range(8):
            do_group(gi)
    P.barrier()


def phase_final(P, T, src_xT, dst):
    with ExitStack() as st:
        ones = P.sb('ones', [128, 128], F32, st)
        fg = P.sb('fg', [128, 8], F32, st)
        xts = [(P.sb('xf%d' % i, [128, 8, 512], F32, st), 'xf%d' % i) for i in range(2)]
        sqs = [(P.sb('sf%d' % i, [128, 8, 512], F32, st), 'sf%d' % i) for i in range(2)]
        rstd = P.sb('rstdf', [128, 512], F32, st)
        bank = P.ps('fb', [128, 512], F32, st)
        P.op('pool', lambda e: e.memset(ones[:], 1.0), writes=['ones'])
        P.dma('sp', fg[:], T['final_norm_g'], writes=['fg'])
        xs3 = src_xT.rearrange("(j p) t -> p j t", p=128)
        ds3 = dst.rearrange("(j p) t -> p j t", p=128)
        xt_n = rr(xts); sq_n = rr(sqs)

        def do_tile(tt):
            t0 = tt * 512
            xt, xk = xt_n(); sq, sk = sq_n()
            P.dma('sp', xt[:], xs3[:, :, t0:t0 + 512], writes=[xk])
            P.op('act', lambda e: e.activation(sq[:], xt[:], AF.Square), reads=[xk], writes=[sk])
            for j in range(8):
                P.op('pe', lambda e, j=j: e.matmul(bank[:], ones[:], sq[:, j, :], start=(j == 0), stop=(j == 7)), reads=[sk, 'ones'], writes=['fb'])
            P.op('dve', lambda e: e.tensor_scalar(rstd[:], bank[:], 1.0 / 1024, 1e-6, ALU.mult, ALU.add), writes=['fb', 'rstdf'])
            P.op('act', lambda e: e.activation(rstd[:], rstd[:], AF.Sqrt), writes=['rstdf'])
            P.op('dve', lambda e: e.reciprocal(rstd[:], rstd[:]), writes=['rstdf'])
            for j in range(8):
                P.op('dve', lambda e, j=j: e.scalar_tensor_tensor(sq[:, j, :], xt[:, j, :], fg[:, j:j + 1], rstd[:], ALU.mult, ALU.mult),
                     reads=[xk, 'rstdf', 'fg'], writes=[sk])
            P.dma('pool', ds3[:, :, t0:t0 + 512], sq[:], reads=[sk], writes=[('dram', 'fin', tt)])

        for tt in range(8):
            do_tile(tt)
    P.barrier()


NT = 2176
NMOD = 4096.0


def phase_dft_tables(P, T, S):
    with ExitStack() as st:
        kv = P.sb('kv', [128, NT], F32, st)
        rv = P.sb('rv', [128, 17], F32, st)
        hp = P.sb('hp', [128, 1], F32, st)
        W = NT
        prods = [(P.sb('pr%d' % i, [128, W], F32, st), 'pr%d' % i) for i in range(2)]
        qis = [(P.sb('qi%d' % i, [128, W], I32, st), 'qi%d' % i) for i in range(2)]
        qfs = [(P.sb('qf%d' % i, [128, W], F32, st), 'qf%d' % i) for i in range(2)]
        abss = [(P.sb('ab%d' % i, [128, W], F32, st), 'ab%d' % i) for i in range(2)]
        cts = [(P.sb('ct%d' % i, [128, W], BF16, st), 'ct%d' % i) for i in range(2)]
        sts = [(P.sb('sn%d' % i, [128, W], BF16, st), 'sn%d' % i) for i in range(2)]
        P.dma('sp', kv[:], T['kvec'].partition_broadcast(128), writes=['kv'])
        P.dma('sp', rv[:], T['rvals'], writes=['rv'])
        P.op('pool', lambda e: e.memset(hp[:], math.pi / 2), writes=['hp'])
        pn = rr(prods); qn = rr(qis); fn = rr(qfs); an = rr(abss); cn = rr(cts); sn = rr(sts)
        w0 = 2 * math.pi / NMOD

        def piece(rc, half):
            c0 = half * W
            pr, prk = pn(); qi, qik = qn(); qf, qfk = fn(); ab, abk = an(); ct, ctk = cn(); sn_, snk = sn()
            P.op('dve', lambda e: e.tensor_scalar(pr[:], kv[:, c0:c0 + W], rv[:, rc:rc + 1], None, ALU.mult), reads=['kv', 'rv'], writes=[prk])
            P.op('dve', lambda e: e.tensor_scalar(qi[:], pr[:], 1.0 / NMOD, None, ALU.mult), reads=[prk], writes=[qik])
            P.op('pool', lambda e: e.tensor_copy(qf[:], qi[:]), reads=[qik], writes=[qfk])
            P.op('dve', lambda e: e.scalar_tensor_tensor(pr[:], qf[:], -NMOD, pr[:], ALU.mult, ALU.add), reads=[qfk], writes=[prk])
            P.op('act', lambda e: e.activation(sn_[:], pr[:], AF.Sin, scale=w0), reads=[prk], writes=[snk])
            P.op('act', lambda e: e.activation(ab[:], pr[:], AF.Abs), reads=[prk], writes=[abk])
            P.op('act', lambda e: e.activation(ct[:], ab[:], AF.Sin, bias=hp[:], scale=-w0), reads=[abk, 'hp'], writes=[ctk])
            P.dma('sp', S['Ctab'][rc * 128:(rc + 1) * 128, c0:c0 + W], ct[:], reads=[ctk], writes=[('dram', 'C', rc, half)])
            P.dma('pool', S['Stab'][rc * 128:(rc + 1) * 128, c0:c0 + W], sn_[:], reads=[snk], writes=[('dram', 'S', rc, half)])

        for rc in range(17):
            piece(rc, 0)
    P.barrier()


def dft_consts():
    kvec = np.arange(NT, dtype=np.float32).reshape(1, NT)
    rvals = (np.arange(17)[None, :] * 128 + np.arange(128)[:, None]).astype(np.float32)
    return kvec, rvals


class _View:
    def __init__(self, base, t0):
        self.base, self.t0 = base, t0
    def __getitem__(self, idx):
        p, j, t = idx
        t = slice((t.start or 0) + self.t0, (t.stop if t.stop is not None else 512) + self.t0)
        return self.base[p, j, t]


def phase_l1_proj(P, T, pers, S, src_xT):
    with ExitStack() as st:
        W = P.sb('W1', [128, 8, 4096], BF16, st)
        hT = P.sb('hT1', [128, 8, 4096], BF16, st)
        cw = P.sb('cw1', [128, 72], F32, st)
        P.dma('sp', cw[:], T['o_hy_conv'], writes=['cw1'])
        with ExitStack() as st1:
            wstl = [(P.sb('wst1', [128, 4096], F32, st1), 'wst1')]
            ones = P.sb('ones', [128, 128], F32, st1)
            xts = [(P.sb('xt1_%d' % i, [128, 8, 512], F32, st1), 'xt1_%d' % i) for i in range(2)]
            sq = P.sb('sq1', [128, 8, 512], F32, st1)
            rstd = P.sb('rstd1', [128, 512], F32, st1)
            bank = P.ps('ssb1', [128, 512], F32, st1)
            P.op('pool', lambda e: e.memset(ones[:], 1.0), writes=['ones'])
            load_weights_bf16(P, T['o_w_in'], W, rr(wstl), 4096, 'W1')
            bufs = dict(xt=rr(xts), hT=None, sq=sq, ones=ones, rstd=rstd, ssbank=bank)
            x3 = src_xT.rearrange("(j p) t -> p j t", p=128)
            for tt in range(8):
                adaln_tile(P, x3[:, :, tt * 512:(tt + 1) * 512], 512, 0, pers, 'o', bufs, None, dst=(_View(hT, tt * 512), ('hT1', tt)))
            P.barrier()
        wkeys = []
        with ExitStack() as st2:
            prows = [(P.sb('prow%d' % i, [128, 4098], F32, st2), 'prow%d' % i) for i in range(2)]
            accs = [(P.sb('acc1_%d' % i, [128, 4096], F32, st2), 'acc1_%d' % i) for i in range(2)]
            grows = [(P.sb('grow%d' % i, [128, 4096], BF16, st2), 'grow%d' % i) for i in range(1)]
            banks = [(P.ps('pj%d' % i, [128, 512], F32, st2), 'pj%d' % i) for i in range(6)]
            for (pr, prk) in prows:
                P.op('pool', lambda e, pr=pr: e.memset(pr[:], 0.0), writes=[prk] + [(prk, tt) for tt in range(8)])
            pr_n = rr(prows); acc_n = rr(accs); gr_n = rr(grows); bk_n = rr(banks)
            dq = rr(['sp', 'pool'])

            def do_chunk(cc):
                isconv = cc < 24
                if isconv:
                    pr, prk = pr_n()
                else:
                    gr, grk = gr_n()
                for tt in range(8):
                    bank, bkk = bk_n()
                    for k in range(8):
                        P.op('pe', lambda e, k=k, bank=bank, tt=tt: e.matmul(bank[:], W[:, k, cc * 128:(cc + 1) * 128], hT[:, k, tt * 512:(tt + 1) * 512], start=(k == 0), stop=(k == 7)), writes=[bkk])
                    if isconv:
                        if tt % 2 == 0:
                            P.op('act', lambda e, bank=bank, tt=tt: e.copy(pr[:, 1 + tt * 512:1 + (tt + 1) * 512], bank[:]), writes=[bkk, (prk, tt)])
                        else:
                            P.op('dve', lambda e, bank=bank, tt=tt: e.tensor_copy(pr[:, 1 + tt * 512:1 + (tt + 1) * 512], bank[:]), writes=[bkk, (prk, tt)])
                    else:
                        P.op('act', lambda e, bank=bank, tt=tt: e.activation(gr[:, tt * 512:(tt + 1) * 512], bank[:], AF.Silu), writes=[bkk, (grk, tt)])
                if isconv:
                    acc, ak = acc_n()
                    pk = [(prk, tt) for tt in range(8)]
                    P.op('dve', lambda e: e.tensor_scalar(acc[:], pr[:, 0:4096], cw[:, cc * 3:cc * 3 + 1], None, ALU.mult), reads=['cw1'], writes=[ak, prk] + pk)
                    for j in (1, 2):
                        P.op('dve', lambda e, j=j: e.scalar_tensor_tensor(acc[:], pr[:, j:j + 4096], cw[:, cc * 3 + j:cc * 3 + j + 1], acc[:], ALU.mult, ALU.add), reads=['cw1'], writes=[ak, prk] + pk)
                    P.dma(dq(), S['uT'][cc * 128:(cc + 1) * 128, :], acc[:], reads=[ak], writes=[('dram', 'u', cc)])
                else:
                    g = cc - 24
                    P.dma(dq(), S['gT'][g * 128:(g + 1) * 128, :], gr[:], reads=[(grk, tt) for tt in range(8)], writes=[('dram', 'g', g)])

            for cc in range(32):
                do_chunk(cc)
    P.barrier()


HY_DMIN = math.log(1e-2) / 1.5
HY_DMAX = math.log(1e-2) / 0.3


def hy_consts():
    L = 4096
    t = np.linspace(0.0, 1.0, L, dtype=np.float32)[:, None]
    w = (2.0 * math.pi * np.arange(L, dtype=np.float32)[:, None] / L).astype(np.float32)
    f = np.linspace(1e-4, 15, 16, dtype=np.float32)[None, :]
    feats = np.concatenate([t, np.cos(f * w), -np.sin(f * w)], axis=-1).astype(np.float32)
    perm = np.concatenate([np.arange(0, L, 2), np.arange(1, L, 2)])
    featsT = np.ascontiguousarray(feats[perm].T)
    ntpos = -np.ascontiguousarray(t[:, 0].reshape(32, 128).T)
    decay = np.abs(np.linspace(HY_DMIN, HY_DMAX, 1024, dtype=np.float32)).reshape(1, 1024).astype(np.float32)
    k = np.arange(33 * 128)
    wk = np.where(k > 4096, 0.0, np.where((k == 0) | (k == 4096), 1.0, 2.0)) / 8192.0
    wk = np.ascontiguousarray(wk.reshape(33, 128).T).astype(np.float32)
    E = np.exp(-(t.astype(np.float32)) * decay).astype(np.float32)[perm]
    Etab = np.ascontiguousarray(E.reshape(32, 128, 8, 128).transpose(2, 1, 0, 3))
    kk = np.arange(17 * 128)
    wP = np.where(kk > 2048, 0.0, np.where(kk == 0, 1.0, 2.0)) / 8192.0
    wM = np.where(kk >= 2048, 0.0, np.where(kk == 0, 1.0, 2.0)) / 8192.0
    sl = lambda v: np.ascontiguousarray(v.reshape(17, 128).T)
    wts = np.concatenate([sl(wP), sl(wM), -sl(wP)], axis=1).astype(np.float32)
    return featsT, ntpos.astype(np.float32), decay, wts, Etab


def sin_reduced(P, dst, src_ps, bvec, fvec, bufs, key_ps, key_dst, np_, n):
    arg, argk = bufs['arg']; qi, qik = bufs['qi']; qf, qfk = bufs['qf']
    P.op('dve', lambda e: e.tensor_scalar(arg[:np_, :n], src_ps, bvec, fvec, ALU.add, ALU.mult), writes=[key_ps, argk])
    P.op('dve', lambda e: e.tensor_scalar(qi[:np_, :n], arg[:np_, :n], 1.0 / (2 * math.pi), None, ALU.mult), reads=[argk], writes=[qik])
    P.op('pool', lambda e: e.tensor_copy(qf[:np_, :n], qi[:np_, :n]), reads=[qik], writes=[qfk])
    P.op('dve', lambda e: e.scalar_tensor_tensor(arg[:np_, :n], qf[:np_, :n], -2 * math.pi, arg[:np_, :n], ALU.mult, ALU.add), reads=[qfk], writes=[argk])
    P.op('act', lambda e: e.activation(dst, arg[:np_, :n], AF.Sin), reads=[argk], writes=[key_dst])


def phase_hy_filters(P, T, S):
    with ExitStack() as st:
        fv = P.sb('fv', [64, 4], F32, st)
        P.dma('sp', fv[:], T['o_ffn_vec'], writes=['fv'])
        with ExitStack() as st1:
            featsT = P.sb('featsT', [33, 4096], F32, st1)
            w1 = P.sb('fw1', [33, 64], F32, st1); w2 = P.sb('fw2', [64, 64], F32, st1)
            h1 = P.sb('hid1T', [64, 4096], F32, st1)
            h2 = P.sb('hid2T', [64, 4096], F32, st1)
            arg = (P.sb('arg', [64, 512], F32, st1), 'arg'); qi = (P.sb('qi', [64, 512], I32, st1), 'qi'); qf = (P.sb('qf', [64, 512], F32, st1), 'qf')
            bufs = dict(arg=arg, qi=qi, qf=qf)
            pbs1 = [(P.ps('fh1_%d' % i, [128, 512], F32, st1), 'fh1_%d' % i) for i in range(2)]
            pb1 = rr(pbs1)
            P.dma('sp', featsT[:], T['featsT'], writes=['featsT'])
            P.dma('sp', w1[:], T['o_ffn_w1'], writes=['fw1']); P.dma('sp', w2[:], T['o_ffn_w2'], writes=['fw2'])
            for tt in range(8):
                ps, psk = pb1()
                P.op('pe', lambda e, ps=ps, tt=tt: e.matmul(ps[0:64, :], w1[:], featsT[:, tt * 512:(tt + 1) * 512], start=True, stop=True), reads=['fw1', 'featsT'], writes=[psk])
                sin_reduced(P, h1[:, tt * 512:(tt + 1) * 512], ps[0:64, :], fv[:, 0:1], fv[:, 1:2], bufs, psk, ('h1', tt), 64, 512)
            for tt in range(8):
                ps, psk = pb1()
                P.op('pe', lambda e, ps=ps, tt=tt: e.matmul(ps[0:64, :], w2[:], h1[:, tt * 512:(tt + 1) * 512], start=True, stop=True), reads=['fw2', ('h1', tt)], writes=[psk])
                sin_reduced(P, h2[:, tt * 512:(tt + 1) * 512], ps[0:64, :], fv[:, 2:3], fv[:, 3:4], bufs, psk, ('h2', tt), 64, 512)
            P.dma('sp', S['h2sc'], h2[:], reads=[('h2', tt) for tt in range(8)], writes=[('dram', 'h2sc')])
            P.barrier()
        wk = P.sb('wk', [128, 51], F32, st)
        tw = P.sb('twf', [128, 51], F32, st)
        ones = P.sb('ones', [128, 128], F32, st)
        av = P.sb('av', [128, 32, 1024], BF16, st); dv = P.sb('dv', [128, 32, 1024], BF16, st)
        P.dma('sp', wk[:], T['wk'], writes=['wk'])
        P.dma('sp', tw[:], T['hy_tw'], writes=['twf'])
        P.op('pool', lambda e: e.memset(ones[:], 1.0), writes=['ones'])
        dq = rr(['sp', 'act'])

        def gen_order(o):
            with ExitStack() as sg:
                hb = [(P.sb('hb%d' % i, [128, 32, 128], F32, sg), 'hb%d' % i) for i in range(2)]
                Et = (P.sb('Etab', [128, 32, 128], F32, sg), 'Etab')
                w3 = P.sb('fw3', [64, 2048], F32, sg)
                h2 = P.sb('hid2T', [64, 4096], F32, sg)
                P.dma('sp', h2[:], S['h2sc'], writes=['h2r'])
                P.dma('act', w3[:], T['o_ffn_w3'][:, o * 2048:(o + 1) * 2048], writes=['fw3'])
                prt = [rr([(P.sb('prt%d_%d' % (d_, i), [128, 128], F32, sg), 'prt%d_%d' % (d_, i)) for i in range(1)]) for d_ in range(2)]
                rns = [(P.sb('rnf%d' % i, [128, 128], F32, sg), 'rnf%d' % i) for i in range(2)]
                skp = P.sb('skp', [1, 128], F32, sg)
                pbs = [(P.ps('fh%d' % i, [128, 1024], F32, sg), 'fh%d' % i) for i in range(3)]
                nbs = [(P.ps('fn%d' % i, [128, 512], F32, sg), 'fn%d' % i) for i in range(2)]
                pb_n = rr(pbs)

                def do_dir(cg, dr):
                    hbt, hbk = hb[dr]
                    E, Ek = Et
                    rn, rnk = rns[dr]
                    col0 = dr * 1024 + cg * 128
                    nbank, nk = nbs[dr]
                    for q in range(4):
                        ps, psk = pb_n(); pr, prk = prt[dr]()
                        for jj in range(8):
                            j = q * 8 + jj
                            P.op('pe', lambda e, ps=ps, j=j, jj=jj: e.matmul(ps[:, jj * 128:(jj + 1) * 128], h2[:, j * 128:(j + 1) * 128], w3[:, col0:col0 + 128], start=True, stop=True), reads=['fw3', 'h2r'], writes=[psk])
                        js = slice(q * 8, (q + 1) * 8)
                        P.op('dve', lambda e, ps=ps, js=js: e.tensor_tensor(hbt[:, js, :], ps[:].rearrange("p (j c) -> p j c", c=128), E[:, js, :], ALU.mult), reads=[Ek], writes=[psk, (hbk, q)] + ([hbk] if q == 0 else []))
                        yield
                        P.op('act', lambda e, ps=ps, js=js: e.activation(ps[:].rearrange("p (j c) -> p j c", c=128), hbt[:, js, :], AF.Square), reads=[(hbk, q)], writes=[psk])
                        P.op('dve', lambda e, ps=ps, pr=pr: e.tensor_reduce(pr[:], ps[:].rearrange("p (j c) -> p c j", c=128), AX.X, ALU.add), writes=[psk, prk])
                        P.op('pe', lambda e, pr=pr, q=q: e.matmul(nbank[:, 0:128], ones[:], pr[:], start=(q == 0), stop=(q == 3)), reads=[prk, 'ones'], writes=[nk])
                        yield
                    P.op('dve', lambda e: e.tensor_scalar_add(rn[:], nbank[:, 0:128], 1e-6), writes=[nk, rnk])
                    P.op('act', lambda e: e.activation(rn[:], rn[:], AF.Sqrt), writes=[rnk])
                    P.op('dve', lambda e: e.reciprocal(rn[:], rn[:]), writes=[rnk])
                    yield
                    hk_all = [(hbk, q) for q in range(4)]
                    P.op('pool', lambda e: e.tensor_tensor(hbt[:], hbt[:], rn[:].unsqueeze(1).to_broadcast([128, 32, 128]), ALU.mult), reads=[rnk], writes=[hbk] + hk_all)

                def do_cg(cg):
                    P.dma('act', Et[0][:], T['Etab'][cg], writes=[Et[1]])
                    P.dma('sp', skp[:], T['o_hy_skip'][0:1, o * 1024 + cg * 128:o * 1024 + (cg + 1) * 128], writes=['skp'])
                    gens = [do_dir(cg, 0), do_dir(cg, 1)]
                    while gens:
                        for g_ in list(gens):
                            try:
                                next(g_)
                            except StopIteration:
                                gens.remove(g_)
                    (h0, h0k), (h1_, h1k) = hb
                    cs = slice(cg * 128, (cg + 1) * 128)
                    t0r = rns[0][0][0:1, :]; t0k = rns[0][1]
                    P.op('dve', lambda e: e.tensor_tensor(av[:, :, cs], h0[:], h1_[:], ALU.add), reads=[h0k, h1k], writes=[('av', cg)])
                    P.op('pool', lambda e: e.tensor_tensor(dv[:, :, cs], h0[:], h1_[:], ALU.subtract), reads=[h0k, h1k], writes=[('dv', cg)])
                    P.op('dve', lambda e: e.tensor_tensor(t0r, h0[0:1, 0, :], h1_[0:1, 0, :], ALU.add), reads=[h0k, h1k], writes=[t0k])
                    P.op('dve', lambda e: e.tensor_tensor(t0r, t0r, skp[0:1, :], ALU.add), reads=['skp'], writes=[t0k])
                    P.op('dve', lambda e: e.tensor_copy(av[0:1, 0, cs], t0r), reads=[t0k], writes=[('av', cg)])
                    P.op('dve', lambda e: e.tensor_copy(dv[0:1, 0, cs], t0r), reads=[t0k], writes=[('dv', cg)])

                for cg in range(8):
                    do_cg(cg)
                P.barrier()

        def xform_order(o):
            with ExitStack() as sx:
                cts = [(P.sb('ctf%d' % i, [128, 16, 128], BF16, sx), 'ctf%d' % i) for i in range(2)]
                sts = [(P.sb('stf%d' % i, [128, 16, 128], BF16, sx), 'stf%d' % i) for i in range(2)]
                hos = [(P.sb('ho%d' % i, [128, 4, 1024], F32, sx), 'ho%d' % i) for i in range(2)]
                tmps = [[(P.sb('xt%d_%d' % (q, i), [128, 512], F32, sx), 'xt%d_%d' % (q, i)) for i in range(5)] for q in range(2)]
                cbs = [(P.ps('fc%d' % i, [128, 512], F32, sx), 'fc%d' % i) for i in range(8)]
                cb_n = rr(cbs); ct_n = rr(cts); st_n = rr(sts); ho_n = rr(hos); tm_n = rr(tmps)

                def do_kc(kc):
                    ct, ctk = ct_n(); stt, stk = st_n(); ho, hok = ho_n()
                    P.dma('sp', ct[:], S['Ctab'][0:2048, kc * 128:(kc + 1) * 128].rearrange("(j p) q -> p j q", p=128), writes=[ctk])
                    P.dma('act', stt[:], S['Stab'][0:2048, kc * 128:(kc + 1) * 128].rearrange("(j p) q -> p j q", p=128), writes=[stk])
                    ck = tw[:, kc:kc + 1]; sk = tw[:, 17 + kc:18 + kc]; nsk = tw[:, 34 + kc:35 + kc]
                    wP = wk[:, kc:kc + 1]; wM = wk[:, 17 + kc:18 + kc]; nwP = wk[:, 34 + kc:35 + kc]

                    def half(ch, src, is_a):
                        cs = slice(ch * 512, (ch + 1) * 512)
                        (b0, kb0), (b1, kb1), (b2, kb2) = cb_n(), cb_n(), cb_n()
                        plan = ((b0, kb0, ct, ctk, 0), (b1, kb1, ct, ctk, 16), (b2, kb2, stt, stk, 16)) if is_a else ((b0, kb0, stt, stk, 0), (b1, kb1, stt, stk, 16), (b2, kb2, ct, ctk, 16))
                        for (bank, bkey, tab, tkey, joff) in plan:
                            for j in range(16):
                                P.op('pe', lambda e, j=j, bank=bank, tab=tab, joff=joff: e.matmul(bank[:], tab[:, j, :], src[:, joff + j, cs], start=(j == 0), stop=(j == 15)), reads=[tkey], writes=[bkey])
                        (u, uk), (tt_, ttk), (ev, evk), (pp_, ppk), (mm_, mmk) = tm_n()
                        P.op('act', lambda e: e.activation(u[:], b1[:], AF.Copy, scale=ck), reads=['twf'], writes=[kb1, uk])
                        P.op('dve', lambda e: e.scalar_tensor_tensor(tt_[:], b2[:], (nsk if is_a else sk), u[:], ALU.mult, ALU.add), reads=['twf', uk], writes=[kb2, ttk])
                        P.op('act', lambda e: e.copy(ev[:], b0[:]), writes=[kb0, evk])
                        P.op('pool', lambda e: e.tensor_tensor(pp_[:], ev[:], tt_[:], ALU.add), reads=[evk, ttk], writes=[ppk])
                        P.op('dve', lambda e: e.tensor_tensor(mm_[:], ev[:], tt_[:], ALU.subtract), reads=[evk, ttk], writes=[mmk])
                        fP, fM = (0, 2) if is_a else (1, 3)
                        P.op('act', lambda e: e.activation(ho[:, fP, cs], pp_[:], AF.Copy, scale=(wP if is_a else nwP)), reads=['wk', ppk], writes=[(hok, fP, ch)])
                        P.op('act', lambda e: e.activation(ho[:, fM, cs], mm_[:], AF.Copy, scale=wM), reads=['wk', mmk], writes=[(hok, fM, ch)])
                    for ch in range(2):
                        half(ch, av, True)
                        half(ch, dv, False)
                    allk = [(hok, f, ch) for f in range(4) for ch in range(2)]
                    P.dma('sp', S['Hsc'][o, :, kc * 128:(kc + 1) * 128, :].rearrange("f p c -> p f c"), ho[:], reads=allk, writes=[('dram', 'H', o, kc)] + allk)

                for kc in range(17):
                    do_kc(kc)
                P.barrier()

        for o in range(int(S.get('hy_orders', 2))):
            gen_order(o)
            if not S.get('skip_xform'):
                xform_order(o)
    P.barrier()


NKC = 17


def hy_twiddles():
    k = np.arange(NKC * 128, dtype=np.float64)
    th = 2 * np.pi * k / 8192.0
    ck = np.cos(th).reshape(NKC, 128).T; sk = np.sin(th).reshape(NKC, 128).T
    return np.ascontiguousarray(np.concatenate([ck, sk, -sk], axis=1)).astype(np.float32)


def phase_hy_conv(P, T, S, o, srcT, mul1T, gateT, dstT, dst_bf16):
    with ExitStack() as st:
        xtok = P.sb('xtok', [128, 32, 1024], BF16, st)
        with ExitStack() as st1:
            identf = P.sb('identf', [128, 128], F32, st1)
            ident = P.sb('identb', [128, 128], BF16, st1)
            srcs = [(P.sb('src%d' % i, [128, 8, 512], F32, st1), 'src%d' % i) for i in range(2)]
            sbfs = [(P.sb('sbf%d' % i, [128, 8, 512], BF16, st1), 'sbf%d' % i) for i in range(2)]
            tps = [(P.ps('tpx%d' % i, [128, 1024], BF16, st1), 'tpx%d' % i) for i in range(4)]
            P.dma('sp', identf[:], T['ident'], writes=['identf'])
            P.op('pool', lambda e: e.tensor_copy(ident[:], identf[:]), reads=['identf'], writes=['identb'])
            src_n = rr(srcs); sbf_n = rr(sbfs); tp_n = rr(tps)
            s3 = srcT.rearrange("(f p) t -> p f t", p=128)

            def load_tile(tt):
                sr, srk = src_n(); sb_, sbk = sbf_n()
                P.dma('sp' if tt % 2 == 0 else 'act', sr[:], s3[:, :, tt * 512:(tt + 1) * 512], writes=[srk])
                P.op('act', lambda e: e.copy(sb_[:, 0:4, :], sr[:, 0:4, :]), reads=[srk], writes=[(sbk, 0)])
                P.op('dve', lambda e: e.tensor_copy(sb_[:, 4:8, :], sr[:, 4:8, :]), reads=[srk], writes=[(sbk, 1)])
                cnt = 0
                for s2 in range(2):
                    for r in range(2):
                        tp, tpk = tp_n()
                        j = tt * 2 + s2 + 16 * r
                        a0 = s2 * 256 + r
                        for f in range(8):
                            P.op('pe', lambda e, f=f, a0=a0, s2=s2, tp=tp: e.transpose(tp[:, f * 128:(f + 1) * 128], sb_[:, f, a0:s2 * 256 + 256:2], ident[:]), reads=[(sbk, 0), (sbk, 1), 'identb'], writes=[tpk])
                        if cnt % 2 == 0:
                            P.op('act', lambda e, tp=tp, j=j: e.copy(xtok[:, j, :], tp[:]), writes=[tpk, ('xtok', j)])
                        else:
                            P.op('dve', lambda e, tp=tp, j=j: e.tensor_copy(xtok[:, j, :], tp[:]), writes=[tpk, ('xtok', j)])
                        cnt += 1
            for tt in range(8):
                load_tile(tt)
            P.barrier()
        tw = P.sb('tw', [128, 51], F32, st)
        P.dma('sp', tw[:], T['hy_tw'], writes=['tw'])
        cts = [(P.sb('ctc%d' % i, [128, 16, 128], BF16, st), 'ctc%d' % i) for i in range(2)]
        sts = [(P.sb('stc%d' % i, [128, 16, 128], BF16, st), 'stc%d' % i) for i in range(2)]
        hts = [(P.sb('ht%d' % i, [128, 4, 1024], F32, st), 'ht%d' % i) for i in range(2)]
        sets = [[(P.sb('sl%d_%d' % (q, i), [128, 512], F32, st), 'sl%d_%d' % (q, i)) for i in range(12)] for q in range(2)]
        outs = [(P.sb('yo%d' % i, [128, 4, 1024], BF16, st), 'yo%d' % i) for i in range(2)]
        banks = [(P.ps('cb%d' % i, [128, 512], F32, st), 'cb%d' % i) for i in range(8)]
        ct_n = rr(cts); st_n = rr(sts); ht_n = rr(hts); set_n = rr(sets); out_n = rr(outs); bk_n = rr(banks)

        def fwd_kc(kc):
            ct, ctk = ct_n(); stt, stk = st_n(); ht, htk = ht_n(); yo, yok = out_n()
            P.dma('sp', ct[:], S['Ctab'][0:2048, kc * 128:(kc + 1) * 128].rearrange("(j p) q -> p j q", p=128), writes=[ctk])
            P.dma('act', stt[:], S['Stab'][0:2048, kc * 128:(kc + 1) * 128].rearrange("(j p) q -> p j q", p=128), writes=[stk])
            P.dma('sp', ht[:], S['Hsc'][o, :, kc * 128:(kc + 1) * 128, :].rearrange("f p c -> p f c"), writes=[htk])
            ck = tw[:, kc:kc + 1]; sk = tw[:, 17 + kc:18 + kc]; nsk = tw[:, 34 + kc:35 + kc]

            def do_ch(ch):
                cs = slice(ch * 512, (ch + 1) * 512)
                (bEc, kEc), (bEs, kEs), (bOc, kOc), (bOs, kOs) = bk_n(), bk_n(), bk_n(), bk_n()
                for (bank, bkey, tab, tkey, joff) in ((bEc, kEc, ct, ctk, 0), (bEs, kEs, stt, stk, 0), (bOc, kOc, ct, ctk, 16), (bOs, kOs, stt, stk, 16)):
                    for j in range(16):
                        P.op('pe', lambda e, j=j, bank=bank, tab=tab, joff=joff: e.matmul(bank[:], tab[:, j, :], xtok[:, joff + j, cs], start=(j == 0), stop=(j == 15)), reads=[tkey], writes=[bkey])
                sl = set_n()
                (s0, k0), (s1, k1), (s2, k2), (s3_, k3), (s4, k4), (s5, k5), (s6, k6), (s7, k7), (s8, k8), (s9, k9), (s10, k10), (s11, k11) = sl
                A = P.op
                A('act', lambda e: e.copy(s0[:], bEc[:]), writes=[kEc, k0])
                A('act', lambda e: e.copy(s1[:], bEs[:]), writes=[kEs, k1])
                A('act', lambda e: e.activation(s2[:], bOc[:], AF.Copy, scale=ck), reads=['tw'], writes=[kOc, k2])
                A('act', lambda e: e.activation(s3_[:], bOs[:], AF.Copy, scale=ck), reads=['tw'], writes=[kOs, k3])
                A('dve', lambda e: e.scalar_tensor_tensor(s4[:], bOs[:], nsk, s2[:], ALU.mult, ALU.add), reads=['tw', k2], writes=[kOs, k4])
                A('dve', lambda e: e.scalar_tensor_tensor(s5[:], bOc[:], sk, s3_[:], ALU.mult, ALU.add), reads=['tw', k3], writes=[kOc, k5])
                yield
                hrP = ht[:, 0, cs]; hiP = ht[:, 1, cs]; hrM = ht[:, 2, cs]; hiM = ht[:, 3, cs]
                TT = lambda eng, out, okey, a_, akey, b_, bkey, op, extra=(): A(eng, lambda e: e.tensor_tensor(out, a_, b_, op), reads=[akey, bkey] + list(extra), writes=[okey])
                TT('dve', s6[:], k6, s0[:], k0, s4[:], k4, ALU.add)
                TT('dve', s8[:], k8, s1[:], k1, s5[:], k5, ALU.add)
                TT('dve', s2[:], k2, s6[:], k6, hrP, htk, ALU.mult)
                TT('dve', s3_[:], k3, s8[:], k8, hiP, htk, ALU.mult)
                TT('dve', s2[:], k2, s2[:], k2, s3_[:], k3, ALU.add)
                TT('dve', s3_[:], k3, s6[:], k6, hiP, htk, ALU.mult)
                TT('dve', s6[:], k6, s8[:], k8, hrP, htk, ALU.mult)
                TT('dve', s8[:], k8, s6[:], k6, s3_[:], k3, ALU.subtract)
                TT('pool', s7[:], k7, s0[:], k0, s4[:], k4, ALU.subtract)
                TT('pool', s9[:], k9, s5[:], k5, s1[:], k1, ALU.subtract)
                TT('pool', s10[:], k10, s7[:], k7, hrM, htk, ALU.mult)
                TT('pool', s11[:], k11, s9[:], k9, hiM, htk, ALU.mult)
                TT('pool', s10[:], k10, s10[:], k10, s11[:], k11, ALU.add)
                TT('pool', s11[:], k11, s7[:], k7, hiM, htk, ALU.mult)
                TT('pool', s7[:], k7, s9[:], k9, hrM, htk, ALU.mult)
                TT('pool', s9[:], k9, s7[:], k7, s11[:], k11, ALU.subtract)
                TT('pool', yo[:, 0, cs], (yok, ch, 0), s2[:], k2, s10[:], k10, ALU.add)
                TT('dve', yo[:, 1, cs], (yok, ch, 1), s8[:], k8, s9[:], k9, ALU.subtract)
                TT('pool', s3_[:], k3, s2[:], k2, s10[:], k10, ALU.subtract)
                TT('dve', s6[:], k6, s8[:], k8, s9[:], k9, ALU.add)
                A('act', lambda e: e.activation(s11[:], s6[:], AF.Copy, scale=sk), reads=['tw', k6], writes=[k11])
                A('act', lambda e: e.activation(s7[:], s3_[:], AF.Copy, scale=nsk), reads=['tw', k3], writes=[k7])
                A('dve', lambda e: e.scalar_tensor_tensor(yo[:, 2, cs], s3_[:], ck, s11[:], ALU.mult, ALU.add), reads=['tw', k3, k11], writes=[(yok, ch, 2)])
                A('dve', lambda e: e.scalar_tensor_tensor(yo[:, 3, cs], s6[:], ck, s7[:], ALU.mult, ALU.add), reads=['tw', k6, k7], writes=[(yok, ch, 3)])
            def store():
                allk = [(yok, ch, q) for ch in range(2) for q in range(4)]
                P.dma('act', S['Ysc'][:, kc * 128:(kc + 1) * 128, :].rearrange("f p c -> p f c"), yo[:], reads=allk, writes=[('dram', 'Y', kc)] + allk)
            return [(do_ch(0), None), (do_ch(1), store)]

        pend = None
        for kc in S.get('fwd_list', range(NKC)):
            for (g_, fin) in fwd_kc(kc):
                next(g_)
                if pend is not None:
                    for _ in pend[0]:
                        pass
                    if pend[1] is not None:
                        pend[1]()
                pend = (g_, fin)
        if pend is not None:
            for _ in pend[0]:
                pass
            if pend[1] is not None:
                pend[1]()
    P.barrier()
    with ExitStack() as st:
        Yc = P.sb('Yc', [128, 4, NKC, 512], BF16, st)
        cts = [(P.sb('cti%d' % i, [128, NKC, 256], BF16, st), 'cti%d' % i) for i in range(2)]
        sts = [(P.sb('sti%d' % i, [128, NKC, 256], BF16, st), 'sti%d' % i) for i in range(2)]
        m1s = [(P.sb('m1_%d' % i, [128, 4, 512], F32, st), 'm1_%d' % i) for i in range(2)]
        gts = [(P.sb('gt_%d' % i, [128, 4, 512], BF16, st), 'gt_%d' % i) for i in range(2)]
        ofs = [(P.sb('of_%d' % i, [128, 4, 512], F32, st), 'of_%d' % i) for i in range(2)]
        obs = [(P.sb('ob_%d' % i, [128, 4, 512], BF16, st), 'ob_%d' % i) for i in range(2)]
        banks = [(P.ps('ib%d' % i, [128, 512], F32, st), 'ib%d' % i) for i in range(6)]
        ct_n = rr(cts); st_n = rr(sts); m1_n = rr(m1s); gt_n = rr(gts); of_n = rr(ofs); ob_n = rr(obs); bk_n = rr(banks)
        m13 = mul1T.rearrange("(f p) t -> p f t", p=128)
        d3 = dstT.rearrange("(f p) t -> p f t", p=128)
        g3 = gateT.rearrange("(f p) t -> p f t", p=128) if gateT is not None else None

        def inv_half(chh):
            c0 = chh * 512
            for q in range(4):
                P.dma('sp' if q % 2 == 0 else 'act', Yc[:, q, :, :], S['Ysc'][q, :, c0:c0 + 512].rearrange("(k p) c -> p k c", p=128), writes=[('Yc', q)])
            ykeys = [('Yc', q) for q in range(4)]

            def inv_nt(nt):
                n0 = nt * 512; m0 = nt * 256
                ct, ctk = ct_n(); stt, stk = st_n(); m1, m1k = m1_n(); of, ofk = of_n()
                P.dma('sp', ct[:], S['Ctab'][:, m0:m0 + 256].rearrange("(k p) n -> p k n", p=128), writes=[ctk])
                P.dma('act', stt[:], S['Stab'][:, m0:m0 + 256].rearrange("(k p) n -> p k n", p=128), writes=[stk])
                P.dma('sp', m1[:], m13[:, chh * 4:(chh + 1) * 4, n0:n0 + 512], writes=[m1k])
                if g3 is not None:
                    gt, gtk = gt_n(); ob, obk = ob_n()
                    P.dma('act', gt[:], g3[:, chh * 4:(chh + 1) * 4, n0:n0 + 512], writes=[gtk])
                okeys = []
                for r in range(2):
                    for cc in range(4):
                        bank, bkk = bk_n()
                        for kc in range(NKC):
                            P.op('pe', lambda e, kc=kc, cc=cc, bank=bank, r=r: e.matmul(bank[:, 0:256], Yc[:, 2 * r, kc, cc * 128:(cc + 1) * 128], ct[:, kc, :], start=(kc == 0), stop=False), reads=ykeys + [ctk], writes=[bkk])
                        for kc in range(NKC):
                            P.op('pe', lambda e, kc=kc, cc=cc, bank=bank, r=r: e.matmul(bank[:, 0:256], Yc[:, 2 * r + 1, kc, cc * 128:(cc + 1) * 128], stt[:, kc, :], start=False, stop=(kc == NKC - 1)), reads=ykeys + [stk], writes=[bkk])
                        P.op('dve', lambda e, cc=cc, bank=bank, r=r: e.tensor_tensor(of[:, cc, r:512:2], bank[:, 0:256], m1[:, cc, r:512:2], ALU.mult), reads=[m1k], writes=[bkk, (ofk, cc, r)])
                        okeys.append((ofk, cc, r))
                if g3 is not None:
                    P.op('pool', lambda e: e.tensor_tensor(ob[:], of[:], gt[:], ALU.mult), reads=okeys + [gtk], writes=[obk])
                    P.dma('sp', d3[:, chh * 4:(chh + 1) * 4, n0:n0 + 512], ob[:], reads=[obk], writes=[('dram', 'cv', chh, nt), ofk] + okeys)
                else:
                    P.dma('sp', d3[:, chh * 4:(chh + 1) * 4, n0:n0 + 512], of[:], reads=okeys, writes=[('dram', 'cv', chh, nt), ofk] + okeys)
            for nt in range(8):
                inv_nt(nt)

        for chh in range(int(S.get('n_inv', 2))):
            inv_half(chh)
    P.barrier()


def build_program():
    nc = bass.Bass("TRN2", target_bir_lowering=False)

    def din(name, shape, dt=F32):
        return nc.dram_tensor(name, list(shape), dt, kind="ExternalInput").ap()

    def dint(name, shape, dt=F32):
        return nc.dram_tensor(name, list(shape), dt, kind="Internal").ap()

    T = dict(cc=din('cc', [128, 16]), xT=din('xT', [1024, 4096]), ctxT=din('ctxT', [1024, 256]),
             e_mod_w=din('e_mod_w', [1024, 3072]), e_mod_b=din('e_mod_b', [128, 24]), e_norm_g=din('e_norm_g', [128, 8]),
             e_w_in=din('e_w_in', [1024, 4112]), e_w_out=din('e_w_out', [1024, 1024]),
             na_bias=din('na_bias', [5, 128, 5120]), ident=din('ident', [128, 128]),
             e_dn_conv=din('e_dn_conv', [128, 60]), e_dn_a_log=din('e_dn_a_log', [1, 8]), e_dn_dt_bias=din('e_dn_dt_bias', [1, 8]),
             e_dn_norm_g=din('e_dn_norm_g', [1, 128]), dn_masks=din('dn_masks', [8, 128, 128]),
             o_mod_w=din('o_mod_w', [1024, 3072]), o_mod_b=din('o_mod_b', [128, 24]), o_norm_g=din('o_norm_g', [128, 8]),
             o_w_in=din('o_w_in', [1024, 4096]), o_hy_conv=din('o_hy_conv', [128, 72]), o_w_out=din('o_w_out', [1024, 1024]),
             kvec=din('kvec', [1, 2176]), rvals=din('rvals', [128, 17]), hy_tw=din('hy_tw', [128, 51]), featsT=din('featsT', [33, 4096]), ntpos=din('ntpos', [128, 32]),
             decay=din('decay', [1, 1024]), wk=din('wk', [128, 51]), Etab=din('Etab', [8, 128, 32, 128]),
             o_ffn_w1=din('o_ffn_w1', [33, 64]), o_ffn_w2=din('o_ffn_w2', [64, 64]), o_ffn_vec=din('o_ffn_vec', [64, 4]),
             o_ffn_w3=din('o_ffn_w3', [64, 4096]), o_hy_skip=din('o_hy_skip', [1, 2048]),
             final_norm_g=din('final_norm_g', [128, 8]))
    S = dict(qT=dint('qT', [512, 4096], BF16), kT=dint('kT', [512, 4352], BF16), vtok=dint('vtok', [4352, 512], BF16),
             dnT=dint('dnT', [1536, 4096]), dncT=dint('dncT', [1536, 256]), abtok=dint('abtok', [4352, 16]),
             ztok=dint('ztok', [4096, 1024], BF16), mixtok=dint('mixtok', [4096, 1024], BF16),
             dQT=dint('dQT', [4, 128, 4352], BF16), dKT=dint('dKT', [4, 128, 4352], BF16),
             dKtok=dint('dKtok', [4352, 4, 128], BF16), dVtok=dint('dVtok', [4352, 4, 128], BF16),
             dno=dint('dno', [2, 4096, 512]),
             x1T=dint('x1T', [1024, 4096]),
             Ctab=dint('Ctab', [2176, 2176], BF16), Stab=dint('Stab', [2176, 2176], BF16),
             Hsc=dint('Hsc', [2, 4, 2176, 1024]), h2sc=dint('h2sc', [64, 4096]), Ysc=dint('Ysc', [4, 2176, 1024], BF16),
             uT=dint('uT', [3072, 4096]), gT=dint('gT', [1024, 4096], BF16), zT=dint('zT', [1024, 4096]),
             mixT=dint('mixT', [1024, 4096], BF16), x2T=dint('x2T', [1024, 4096]))
    outT = nc.dram_tensor('outT', [1024, 4096], F32, kind="ExternalOutput").ap()
    P = Prog(nc)
    pers = dict(modv=P.sb('modv', [128, 24, 2], F32), gs=P.sb('gs', [128, 8, 2], F32))
    phase_mod(P, T, 'e', pers)
    phase_l0_proj(P, T, pers, S)
    phase_na(P, T, S)
    with ExitStack() as dn_stack:
        pers['g'] = P.sb('g', [128, 34, 8], F32, dn_stack)
        pers['beta'] = P.sb('beta', [128, 34, 8], F32, dn_stack)
        phase_dn_gb(P, T, S, pers)
        phase_dn_prep(P, T, S)
        phase_dn_main(P, T, S, pers)
    phase_dn_out(P, T, S)
    phase_out(P, T, S, pers, 'e', 'e_w_out', True, T['xT'], S['x1T'])
    phase_dft_tables(P, T, S)
    phase_mod(P, T, 'o', pers)
    phase_l1_proj(P, T, pers, S, S['x1T'])
    phase_hy_filters(P, T, S)
    phase_hy_conv(P, T, S, 0, S['uT'][0:1024, :], S['uT'][1024:2048, :], None, S['zT'], False)
    phase_hy_conv(P, T, S, 1, S['zT'], S['uT'][2048:3072, :], S['gT'], S['mixT'], True)
    phase_out(P, T, S, pers, 'o', 'o_w_out', False, S['x1T'], S['x2T'], final_out=outT)
    P.finalize()
    return nc


def _pl(v, k):
    return np.ascontiguousarray(np.asarray(v, np.float32).reshape(k, 128).T)


def kernel(**inp):
    inp = {k: np.asarray(v) for k, v in inp.items()}
    nc = build_program()
    f32 = np.float32
    tab = na_bias_table(inp['e_na_rpb'][0]).reshape(5, 128, 5120)
    ident = np.eye(128, dtype=f32)
    kvec, rvals = dft_consts()
    featsT, ntpos, decay, wk, Etab = hy_consts()
    dncw = np.ascontiguousarray(inp['e_dn_conv'][0].T.reshape(12, 128, 5).transpose(1, 0, 2).reshape(128, 60)).astype(f32)
    hycw = np.ascontiguousarray(inp['o_hy_conv'][0].T.reshape(24, 128, 3).transpose(1, 0, 2).reshape(128, 72)).astype(f32)
    fvec = np.stack([inp['o_ffn_b1'][0], inp['o_ffn_f1'][0], inp['o_ffn_b2'][0], inp['o_ffn_f2'][0]], axis=1).astype(f32)
    shared = dict(
        e_mod_w=inp['e_mod_w'][0], e_mod_b=_pl(inp['e_mod_b'][0], 24), e_norm_g=_pl(inp['e_norm_g'][0], 8),
        e_w_in=inp['e_w_in'][0], e_w_out=inp['e_w_out'][0], na_bias=tab, ident=ident,
        e_dn_conv=dncw, e_dn_a_log=inp['e_dn_a_log'][0].reshape(1, 8).astype(f32), e_dn_dt_bias=inp['e_dn_dt_bias'][0].reshape(1, 8).astype(f32),
        e_dn_norm_g=inp['e_dn_norm_g'][0].reshape(1, 128).astype(f32), dn_masks=dn_masks(),
        o_mod_w=inp['o_mod_w'][0], o_mod_b=_pl(inp['o_mod_b'][0], 24), o_norm_g=_pl(inp['o_norm_g'][0], 8),
        o_w_in=inp['o_w_in'][0], o_hy_conv=hycw, o_w_out=inp['o_w_out'][0],
        kvec=kvec, rvals=rvals, hy_tw=hy_twiddles(), featsT=featsT, ntpos=ntpos, decay=decay, wk=wk, Etab=Etab,
        o_ffn_w1=inp['o_ffn_w1'][0], o_ffn_w2=inp['o_ffn_w2'][0], o_ffn_vec=fvec, o_ffn_w3=inp['o_ffn_w3'][0],
        o_hy_skip=inp['o_hy_skip'][0].reshape(1, 2048).astype(f32), final_norm_g=_pl(inp['final_norm_g'], 8))
    shared = {k: np.ascontiguousarray(v, dtype=f32) for k, v in shared.items()}
    in_maps = []
    for b in range(8):
        cc = np.zeros((128, 16), f32)
        cc[:, 0::2] = _pl(inp['c'][b], 8)
        cc[:, 1::2] = _pl(inp['c_ctx'], 8)
        m = dict(shared)
        m.update(cc=cc, xT=np.ascontiguousarray(inp['x'][b].T, dtype=f32), ctxT=np.ascontiguousarray(inp['ctx'][b].T, dtype=f32))
        in_maps.append(m)
    res = run_bass_kernel_spmd(nc, in_maps, core_ids=list(range(8)))
    out = np.stack([np.ascontiguousarray(np.asarray(r['outT']).T) for r in res.results], axis=0)
    return out.astype(np.float32)
```
